# Optimizing a Trainium2 kernel written in Bass

```python
import jax, jax.numpy as jnp
from jax import lax
import numpy as np

D_MODEL = 1024
BATCH = 4
SEQ = 8192
DEPTH = 1

D_MIX = D_MODEL
D_LRU = D_MIX // 2
LRU_HEADS = 8
LRU_HEAD_DIM = D_LRU // LRU_HEADS
LRU_CONV = 4
LRU_C = 8.0
D_RET = D_MIX - D_LRU
RET_HEADS = 4
RET_HEAD_DIM = D_RET // RET_HEADS
RET_CHUNK = 128
ROPE_BASE = 10000.0
D_IN = 2 * D_LRU + 4 * D_RET
SPLITS = (D_LRU, 2 * D_LRU, 2 * D_LRU + D_RET, 2 * D_LRU + 2 * D_RET, 2 * D_LRU + 3 * D_RET)
D_FF = 3 * D_MODEL
FFN_CONV = 3
NORM_EPS = 1e-6

kernel_name = "hymba_rglru_retention_convffn_block"


def rms_norm(x, gain):
    xf = x.astype(jnp.float32)
    y = xf * lax.rsqrt(jnp.mean(xf * xf, axis=-1, keepdims=True) + NORM_EPS)
    return (y * gain.astype(jnp.float32)).astype(x.dtype)


def causal_depthwise_conv(x, w, b):
    width, ch = w.shape
    y = lax.conv_general_dilated(
        x, w[:, None, :].astype(x.dtype), window_strides=(1,),
        padding=[(width - 1, 0)], dimension_numbers=("NWC", "WIO", "NWC"),
        feature_group_count=ch)
    return y + b.astype(x.dtype)


def _linear_recurrence_combine(left, right):
    a1, b1 = left
    a2, b2 = right
    return a1 * a2, a2 * b1 + b2


def rg_lru(x, wa, ba, wx, bx, lam):
    bsz, slen, _ = x.shape
    xf = x.astype(jnp.float32)
    xh = xf.reshape(bsz, slen, LRU_HEADS, LRU_HEAD_DIM)
    r = jax.nn.sigmoid(jnp.einsum("bshi,hij->bshj", xh, wa.astype(jnp.float32)).reshape(bsz, slen, D_LRU)
                       + ba.astype(jnp.float32))
    i = jax.nn.sigmoid(jnp.einsum("bshi,hij->bshj", xh, wx.astype(jnp.float32)).reshape(bsz, slen, D_LRU)
                       + bx.astype(jnp.float32))
    log_a = -LRU_C * r * jax.nn.softplus(-lam.astype(jnp.float32))
    a = jnp.exp(log_a)
    inp = jnp.sqrt(-jnp.expm1(2.0 * log_a)) * (i * xf)
    _, h = lax.associative_scan(_linear_recurrence_combine, (a, inp), axis=1)
    return h


def rotary(x, cos, sin):
    half = x.shape[-1] // 2
    x1, x2 = x[..., :half], x[..., half:]
    return jnp.concatenate([x1 * cos - x2 * sin, x2 * cos + x1 * sin], axis=-1)


def retention_chunkwise(q, k, v):
    bsz, slen, nh, dk = q.shape
    dv = v.shape[-1]
    c = RET_CHUNK
    n = slen // c
    log_g = jnp.log1p(-jnp.exp2(-5.0 - jnp.arange(nh, dtype=jnp.float32)))
    q = q.reshape(bsz, n, c, nh, dk)
    k = (k * (dk ** -0.5)).reshape(bsz, n, c, nh, dk)
    v = v.reshape(bsz, n, c, nh, dv)
    idx = jnp.arange(c, dtype=jnp.float32)
    diff = idx[:, None] - idx[None, :]
    decay_in = jnp.where(diff[None] >= 0, jnp.exp(jnp.maximum(diff, 0.0)[None] * log_g[:, None, None]), 0.0)
    scores = jnp.einsum("bnihd,bnjhd->bnhij", q, k) * decay_in[None, None]
    inner = jnp.einsum("bnhij,bnjhe->bnihe", scores, v)
    zeta = jnp.exp((c - 1 - idx)[None, :] * log_g[:, None])
    kv = jnp.einsum("bnjhd,bnjhe,hj->nbhde", k, v, zeta)
    g_chunk = jnp.exp(c * log_g)[None, :, None, None]

    def step(state, kv_n):
        return state * g_chunk + kv_n, state

    init = jnp.zeros((bsz, nh, dk, dv), jnp.float32)
    _, prev = lax.scan(step, init, kv)
    xi = jnp.exp((idx + 1.0)[None, :] * log_g[:, None])
    cross = jnp.einsum("bnihd,nbhde,hi->bnihe", q, prev, xi)
    return (inner + cross).reshape(bsz, slen, nh, dv)


def head_group_norm(o, gain):
    mu = jnp.mean(o, axis=-1, keepdims=True)
    var = jnp.mean(jnp.square(o - mu), axis=-1, keepdims=True)
    return (o - mu) * lax.rsqrt(var + NORM_EPS) * gain.astype(jnp.float32).reshape(RET_HEADS, RET_HEAD_DIM)


def setup_inputs(seed: int = 0) -> dict:
    key = jax.random.key(seed)
    ks = jax.random.split(key, 20)
    f32 = jnp.float32

    def nrm(k, shape, fan_in):
        return jax.random.normal(k, shape, f32) * (fan_in ** -0.5)

    def gain(k, shape):
        return 1.0 + 0.05 * jax.random.normal(k, shape, f32)

    def bias(k, shape):
        return 0.02 * jax.random.normal(k, shape, f32)

    a0 = jax.random.uniform(ks[9], (DEPTH, D_LRU), f32, minval=0.9, maxval=0.999)
    return {
        "x": jax.random.normal(ks[0], (BATCH, SEQ, D_MODEL), f32),
        "norm1_gain": gain(ks[1], (DEPTH, D_MODEL)),
        "w_in": nrm(ks[2], (DEPTH, D_MODEL, D_IN), D_MODEL),
        "lru_conv_w": nrm(ks[3], (DEPTH, LRU_CONV, D_LRU), LRU_CONV),
        "lru_conv_b": bias(ks[4], (DEPTH, D_LRU)),
        "lru_gate_a_w": nrm(ks[5], (DEPTH, LRU_HEADS, LRU_HEAD_DIM, LRU_HEAD_DIM), LRU_HEAD_DIM),
        "lru_gate_a_b": bias(ks[6], (DEPTH, D_LRU)),
        "lru_gate_x_w": nrm(ks[7], (DEPTH, LRU_HEADS, LRU_HEAD_DIM, LRU_HEAD_DIM), LRU_HEAD_DIM),
        "lru_gate_x_b": bias(ks[8], (DEPTH, D_LRU)),
        "lru_lambda": jnp.log(a0) - jnp.log1p(-a0),
        "lru_norm_gain": gain(ks[10], (DEPTH, D_LRU)),
        "ret_norm_gain": gain(ks[11], (DEPTH, D_RET)),
        "w_out": nrm(ks[12], (DEPTH, D_MIX, D_MODEL), D_MIX),
        "norm2_gain": gain(ks[13], (DEPTH, D_MODEL)),
        "ffn_up_w": nrm(ks[14], (DEPTH, D_MODEL, 2 * D_FF), D_MODEL),
        "ffn_conv_w": nrm(ks[15], (DEPTH, FFN_CONV, 2 * D_FF), FFN_CONV),
        "ffn_conv_b": bias(ks[16], (DEPTH, 2 * D_FF)),
        "ffn_down_w": nrm(ks[17], (DEPTH, D_FF, D_MODEL), D_FF),
        "final_norm_gain": gain(ks[18], (D_MODEL,)),
    }


def reference(x, norm1_gain, w_in, lru_conv_w, lru_conv_b, lru_gate_a_w, lru_gate_a_b,
              lru_gate_x_w, lru_gate_x_b, lru_lambda, lru_norm_gain, ret_norm_gain, w_out,
              norm2_gain, ffn_up_w, ffn_conv_w, ffn_conv_b, ffn_down_w, final_norm_gain):
    bsz, slen, _ = x.shape
    dt = x.dtype
    pos = jnp.arange(slen, dtype=jnp.float32)
    inv_freq = ROPE_BASE ** (-jnp.arange(0, RET_HEAD_DIM, 2, dtype=jnp.float32) / RET_HEAD_DIM)
    ang = pos[:, None] * inv_freq[None, :]
    cos = jnp.cos(ang)[:, None, :]
    sin = jnp.sin(ang)[:, None, :]

    h = x
    for l in range(DEPTH):
        u = rms_norm(h, norm1_gain[l])
        proj = u @ w_in[l].astype(dt)
        x_lru, g_lru, q, k, v, g_ret = jnp.split(proj, SPLITS, axis=-1)

        xc = causal_depthwise_conv(x_lru, lru_conv_w[l], lru_conv_b[l])
        hl = rg_lru(xc, lru_gate_a_w[l], lru_gate_a_b[l], lru_gate_x_w[l], lru_gate_x_b[l], lru_lambda[l])
        y_lru = rms_norm(hl.astype(dt) * jax.nn.gelu(g_lru), lru_norm_gain[l])

        qh = rotary(q.astype(jnp.float32).reshape(bsz, slen, RET_HEADS, RET_HEAD_DIM), cos, sin)
        kh = rotary(k.astype(jnp.float32).reshape(bsz, slen, RET_HEADS, RET_HEAD_DIM), cos, sin)
        vh = v.astype(jnp.float32).reshape(bsz, slen, RET_HEADS, RET_HEAD_DIM)
        o = head_group_norm(retention_chunkwise(qh, kh, vh), ret_norm_gain[l]).reshape(bsz, slen, D_RET)
        y_ret = o.astype(dt) * jax.nn.silu(g_ret)

        mixed = jnp.concatenate([y_lru, y_ret], axis=-1)
        h = h + mixed @ w_out[l].astype(dt)

        u = rms_norm(h, norm2_gain[l])
        up = causal_depthwise_conv(u @ ffn_up_w[l].astype(dt), ffn_conv_w[l], ffn_conv_b[l])
        a_branch, v_branch = jnp.split(up, 2, axis=-1)
        h = h + (jax.nn.gelu(a_branch) * v_branch) @ ffn_down_w[l].astype(dt)

    return rms_norm(h, final_norm_gain)
```

```python
import os
import numpy as np
from contextlib import ExitStack
import concourse.bass as bass
import concourse.mybir as mybir
from concourse.bass_utils import run_bass_kernel_spmd

F32 = mybir.dt.float32
BF16 = mybir.dt.bfloat16
AF = mybir.ActivationFunctionType
ALU = mybir.AluOpType

D = 1024
DIN = 3072
DFF = 3072
EPS = 1e-6
ENGS = ("pe", "act", "dve", "pool", "sp")

C_G1, C_G2, C_LCW, C_LCB, C_LAB, C_LXB, C_LAM, C_LNG, C_RNG = 0, 8, 16, 32, 36, 40, 44, 48, 52
C_FCW, C_FCB, C_XI, C_ZETA, C_FLAG, NCST = 56, 200, 248, 252, 256, 257


class Buf:
    __slots__ = ("name", "w", "r", "ps")

    def __init__(self, name):
        self.name = name
        self.w = None
        self.r = []
        self.ps = False


class T:
    __slots__ = ("t", "b")

    def __init__(self, t, name):
        self.t = t
        self.b = Buf(name)


class Prog:
    def __init__(self, nc):
        self.nc = nc
        self.ops = {e: [] for e in ENGS}
        self.cnt = {"E_" + e: 0 for e in ENGS}
        self.sems = {}
        self.waited = {e: {} for e in ENGS}
        self.dkeys = []

    def _need(self, eng, toks):
        out = []
        for (k, v, _e) in toks:
            if self.waited[eng].get(k, 0) >= v:
                continue
            self.waited[eng][k] = v
            out.append((k, v))
        return out

    def _deps(self, eng, reads, writes):
        best = {}

        def add(t):
            k = t[0]
            if k not in best or best[k][1] < t[1]:
                best[k] = t
        for b in reads:
            if b.w is not None:
                add(b.w)
            if b.ps:
                for t in b.r:
                    if t[2] != eng:
                        add(t)
        for b in writes:
            if b.w is not None and b.w[2] != eng:
                add(b.w)
            for t in b.r:
                if t[2] != eng:
                    add(t)
        return self._need(eng, best.values())

    def _commit(self, tok, reads, writes):
        for b in reads:
            b.r.append(tok)
        for b in writes:
            b.w = tok
            b.r = []

    def op(self, eng, fn, reads=(), writes=()):
        reads = [x.b if hasattr(x, "b") else x for x in reads]
        writes = [x.b if hasattr(x, "b") else x for x in writes]
        waits = self._deps(eng, reads, writes)
        k = "E_" + eng
        self.cnt[k] += 1
        self.ops[eng].append((waits, fn, (k, 1)))
        tok = (k, self.cnt[k], eng)
        self._commit(tok, reads, writes)
        return tok

    def dma(self, eng, key, fn, reads=(), writes=()):
        reads = [x.b if isinstance(x, T) else x for x in reads]
        writes = [x.b if isinstance(x, T) else x for x in writes]
        waits = self._deps(eng, reads, writes)
        assert key in self.cnt, key
        self.cnt[key] += 16
        self.ops[eng].append((waits, fn, (key, 16)))
        tok = (key, self.cnt[key], "dma:" + key)
        self._commit(tok, reads, writes)
        return tok

    def wait_tok(self, eng, tok):
        w = self._need(eng, [tok])
        if w:
            self.ops[eng].append((w, None, None))

    def barrier(self):
        for e in ENGS:
            toks = [(k, v, "x") for k, v in self.cnt.items() if v > 0]
            w = self._need(e, toks)
            if w:
                self.ops[e].append((w, None, None))

    def emit_block(self):
        nc = self.nc
        sems = self.sems
        with nc.Block() as block:
            def runner(eng):
                ops = self.ops[eng]

                def _run(h):
                    for (waits, fn, inc) in ops:
                        for (k, v) in waits:
                            h.wait_ge(sems[k], v)
                        if fn is not None:
                            ins = fn(h)
                            ins.then_inc(sems[inc[0]], inc[1])
                return _run
            block.tensor(runner("pe"))
            block.scalar(runner("act"))
            block.vector(runner("dve"))
            block.gpsimd(runner("pool"))
            block.sync(runner("sp"))
        self.ops = {e: [] for e in ENGS}


class Rot:
    def __init__(self, items):
        self.items = items
        self.i = 0

    def next(self):
        x = self.items[self.i % len(self.items)]
        self.i += 1
        return x


def split_blocks(n, bc, first=None):
    out = []
    s = 0
    if first:
        out.append((0, first))
        s = first
    while s < n:
        m = min(bc, n - s)
        out.append((s, m))
        s += m
    return out


def build(NM, NP, BC=3, FBC=int(os.environ.get('KFBC', '3')), dbg=False):
    nc = bass.Bass("TRN2", target_bir_lowering=False)

    def din(name, shape):
        return nc.dram_tensor(name, shape, F32, kind="ExternalInput").ap()
    xm = din("xm", [NM * 128, D])
    xp = din("xp", [max(NP, 1) * 128, D])
    tabm = din("tabm", [NM * 128, 256])
    tabp = din("tabp", [max(NP, 1) * 128, 256])
    cst_d = din("cst", [128, NCST])
    gfin_d = din("gfin", [128, D])
    dmask_d = din("dmask", [128, 512])
    gw_d = din("gw", [128, 8 * 128])
    ident_d = din("ident", [128, 128])
    w_in = din("w_in", [D, DIN])
    w_out = din("w_out", [D, D])
    w_up = din("w_up", [D, 2 * DFF])
    w_dn = din("w_dn", [DFF, D])
    out = nc.dram_tensor("out", [NM * 128, D], F32, kind="ExternalOutput").ap()
    if os.environ.get("KHM"):
        hmid = nc.dram_tensor("hmid", [NM * 128, D], F32, kind="ExternalOutput").ap()
    else:
        hmid = nc.dram_tensor("hmid", [NM * 128, D], F32).ap()

    wup_bf = nc.dram_tensor("wup_bf", [8, 128, 8, 768], BF16).ap()
    wdn_bf = nc.dram_tensor("wdn_bf", [128, 24, D], BF16).ap()
    P = Prog(nc)
    STG = int(os.environ.get('KDBG', '9'))
    MSK = int(os.environ.get('KMSK', '7'))
    SUB = int(os.environ.get('KSUB', '9'))
    LDL = int(os.environ.get('KLDL', '8'))
    RDL = int(os.environ.get('KRDL', '7'))
    NBLK = 0
    keyctr = [0]

    with ExitStack() as st0:
        def newkey(name):
            k = "D_%s_%d" % (name, keyctr[0])
            keyctr[0] += 1
            P.cnt[k] = 0
            P.dkeys.append(k)
            return k

        def mk(st, name, shape, dt, n=1):
            res = []
            for i in range(n):
                nm = "%s_%d" % (name, i)
                res.append(T(st.enter_context(nc.sbuf_tensor(nm, shape, dt)), nm))
            return res

        NKEYS = 64
        keypool = [newkey("k") for _ in range(NKEYS)]
        keyuse = {}

        def key_of(buf):
            b = buf.b if isinstance(buf, T) else buf
            if b.name not in keyuse:
                keyuse[b.name] = keypool.pop()
            return keyuse[b.name]

        for k in P.cnt:
            P.sems[k] = st0.enter_context(nc.semaphore(k))

        banks = [T(st0.enter_context(nc.psum_tensor("pb%d" % i, [128, 512], F32)), "pb%d" % i) for i in range(8)]
        for bk in banks:
            bk.b.ps = True
        bankrot = Rot(banks[:7])
        pss_bank = banks[7]

        bank_busy = {}

        def pbank(hold=False):
            for _ in range(len(bankrot.items)):
                bk = bankrot.next()
                if not bank_busy.get(bk.b.name):
                    if hold:
                        bank_busy[bk.b.name] = True
                    return bk
            raise RuntimeError("no free PSUM bank")

        def prel(bk):
            bank_busy[bk.b.name] = False

        cst = mk(st0, "cst", [128, NCST], F32)[0]
        identb = mk(st0, "identb", [128, 128], BF16)[0]
        epsc = mk(st0, "epsc", [128, 1], F32)[0]

        def ld(eng, dst, dst_ap, src_ap, reads=()):
            return P.dma(eng, key_of(dst), lambda h: h.dma_start(out=dst_ap, in_=src_ap), reads=reads, writes=[dst])

        ld("sp", cst, cst.t[:], cst_d)
        ld("pool", identb, identb.t[:], ident_d)
        P.op("pool", lambda h: h.memset(epsc.t[:], EPS), writes=[epsc])

        def cc(col, n=1):
            return cst.t[:, col:col + n]

        stat = Rot(mk(st0, "stat", [128, 4], F32, 12))

        def norm_transpose(xsrc, xn_t, uT_t, ucol, gcol, ntok_cols):
            s = stat.next()
            P.op("pool", lambda h: h.memset(s.t[:], 0.0), writes=[s])
            P.op("act", lambda h: h.activation(out=xn_t.t[:], in_=xsrc.t[:], func=AF.Square, accum_out=s.t[:, 0:1]),
                 reads=[xsrc, s], writes=[xn_t, s])
            if ntok_cols == "lnexp":
                P.op("act", lambda h: h.activation(out=s.t[:, 1:2], in_=s.t[:, 0:1], func=AF.Ln, bias=epsc.t[:, 0:1], scale=1.0 / D),
                     reads=[s, epsc], writes=[s])
                P.op("act", lambda h: h.activation(out=s.t[:, 2:3], in_=s.t[:, 1:2], func=AF.Exp, scale=-0.5), reads=[s], writes=[s])
            else:
                P.op("act", lambda h: h.activation(out=s.t[:, 1:2], in_=s.t[:, 0:1], func=AF.Sqrt, bias=epsc.t[:, 0:1], scale=1.0 / D),
                     reads=[s, epsc], writes=[s])
                P.op("dve", lambda h: h.reciprocal(out=s.t[:, 2:3], in_=s.t[:, 1:2]), reads=[s], writes=[s])
            P.op("act", lambda h: h.activation(out=xn_t.t[:], in_=xsrc.t[:], func=AF.Copy, scale=s.t[:, 2:3]),
                 reads=[xsrc, s], writes=[xn_t])
            pb = pbank()
            pbv = pb.t.bitcast(BF16)

            def tr(h):
                ins = None
                for kc in range(8):
                    ins = h.transpose(pbv[:, kc * 128:(kc + 1) * 128], xn_t.t[:, kc * 128:(kc + 1) * 128], identb.t[:])
                return ins
            P.op("pe", tr, reads=[xn_t, identb], writes=[pb])
            P.op("dve", lambda h: h.tensor_tensor(
                out=uT_t.t[:, :, ucol:ucol + 128],
                in0=pbv[:, 0:1024].rearrange("p (k t) -> p k t", k=8),
                in1=cst.t[:, gcol:gcol + 8].unsqueeze(2).to_broadcast([128, 8, 128]), op=ALU.mult),
                reads=[pb, cst], writes=[uT_t])
            return s

        with ExitStack() as st1:
            W = BC * 128
            w_in_sb = mk(st1, "w_in_sb", [128, 8, DIN], BF16)[0]
            w_out_sb = mk(st1, "w_out_sb", [128, 8, D], BF16)[0]
            gwb = mk(st1, "gwb", [128, 8, 128], BF16)[0]
            dcw = mk(st1, "dcw", [128, 16, 128], BF16)[0]
            onesb = mk(st1, "onesb", [128, 128], BF16)[0]
            dmask = mk(st1, "dmask", [128, 4, 128], F32)[0]
            lruc = mk(st1, "lruc", [128, 8], F32)[0]
            hstate = mk(st1, "hstate", [128, 4], F32)[0]
            Sst = mk(st1, "Sst", [128, 4, 128], F32)[0]
            S_bfs = mk(st1, "S_bf", [128, 4, 128], BF16, 4)
            xs_r = Rot(mk(st1, "xs", [128, D], F32, 2))
            hm_r = Rot(mk(st1, "hm", [128, D], F32, 2))
            tab_r = Rot(mk(st1, "tab", [128, 256], F32, 6))
            xn_r = Rot(mk(st1, "xn", [128, D], BF16, 2))
            uT_r = Rot(mk(st1, "uT", [128, 8, W], BF16, 2))
            xl_r = Rot(mk(st1, "xl", [128, 4, 3 + W], BF16, 2))
            mixT_r = Rot(mk(st1, "mixT", [128, 8, W], BF16, 2))
            LS = []
            for i in range(2):
                d = {n: mk(st1, "%s%d" % (n, i), [128, W], F32)[0] for n in ("xc", "rr", "ii", "aa", "bb")}
                d["a2"], d["hl"], d["gg"] = d["rr"], d["ii"], d["xc"]
                d["xcb"] = mk(st1, "xcb%d" % i, [128, W], BF16)[0]
                d["zsq"] = mk(st1, "zsq%d" % i, [128, W], BF16)[0]
                LS.append(d)
            rstd_t = mk(st1, "rstd", [128, W], F32)[0]
            zz = mk(st1, "zz", [128, W], F32, 4)
            RS = []
            for i in range(3):
                d = {}
                for n in ("tA", "tB", "osb"):
                    d[n] = mk(st1, "%s%d" % (n, i), [128, 4, 128], F32)[0]
                d["sg"] = mk(st1, "sg%d" % i, [128, 512], F32)[0]
                for n in ("vbf", "vz", "qbf", "kbf", "qT", "kT", "sc"):
                    d[n] = mk(st1, "%s%d" % (n, i), [128, 4, 128], BF16)[0]
                d["yn"], d["ybf"] = d["tB"], d["qbf"]
                d["bst"] = mk(st1, "bst%d" % i, [128, 4, 8], F32)[0]
                RS.append(d)

            class V:
                def __init__(self, t_obj, flat):
                    self.t = t_obj.t[:].rearrange("p a b -> p (a b)") if flat else t_obj.t
                    self.b = t_obj.b

            LSP = [
                {"xc": V(zz[0], False), "rr": V(zz[1], False), "ii": V(zz[2], False), "aa": V(zz[3], False),
                 "bb": V(RS[0]["osb"], True), "xcb": V(RS[0]["qT"], True), "zsq": None},
                {"xc": V(RS[0]["sg"], False), "rr": V(RS[1]["sg"], False), "ii": V(RS[2]["sg"], False), "aa": V(RS[1]["osb"], True),
                 "bb": V(RS[2]["osb"], True), "xcb": V(RS[1]["qT"], True), "zsq": None},
            ]
            for d in LSP:
                d["a2"], d["hl"], d["gg"] = d["rr"], d["ii"], d["xc"]

            w_in_v = w_in.rearrange("(kc p) n -> p kc n", p=128)
            wib = [w_in_sb.b] * 6
            for kc in range(8):
                ld("pool", w_in_sb, w_in_sb.t[:, kc, :], w_in_v[:, kc, :])
            ld("pool", gwb, gwb.t[:].rearrange("p a b -> p (a b)"), gw_d)
            ld("sp", dmask, dmask.t[:].rearrange("p a b -> p (a b)"), dmask_d)
            w_out_v = w_out.rearrange("(kc p) n -> p kc n", p=128)
            for kc in range(8):
                ld("pool", w_out_sb, w_out_sb.t[:, kc, :], w_out_v[:, kc, :])
            wupd_b = Buf("wupd")
            wdnd_b = Buf("wdnd")
            bgq = []
            w_up_g = w_up.rearrange("(kc p) (g c) -> g p kc c", p=128, c=768)
            for g in (0, 4, 1, 5, 2, 6, 3, 7):
                bgq.append(lambda g=g: P.dma("pool", key_of(wupd_b), lambda h: h.dma_start(out=wup_bf[g], in_=w_up_g[g]), writes=[wupd_b]))
            w_dn_g = w_dn.rearrange("(m p) n -> p m n", p=128)
            for i in range(0, 24, 6):
                bgq.append(lambda i=i: P.dma("pool", key_of(wdnd_b), lambda h: h.dma_start(out=wdn_bf[:, i:i + 6, :], in_=w_dn_g[:, i:i + 6, :]), writes=[wdnd_b]))
            P.op("pool", lambda h: h.memset(onesb.t[:], 1.0), writes=[onesb])
            P.op("pool", lambda h: h.memset(hstate.t[:], 0.0), writes=[hstate])
            P.op("pool", lambda h: h.memset(Sst.t[:].rearrange("p a b -> p (a b)"), 0.0), writes=[Sst])
            for sb_ in S_bfs:
                P.op("pool", lambda h, sb_=sb_: h.memset(sb_.t[:].rearrange("p a b -> p (a b)"), 0.0), writes=[sb_])
            for xl in xl_r.items:
                P.op("pool", lambda h, xl=xl: h.memset(xl.t[:].rearrange("p a b -> p (a b)"), 0.0), writes=[xl])
            for j in range(16):
                P.op("dve", lambda h, j=j: h.tensor_scalar(out=dcw.t[:, j, :], in0=identb.t[:], scalar1=cc(C_LCW + j), scalar2=None, op0=ALU.mult),
                     reads=[identb, cst], writes=[dcw])
            P.op("act", lambda h: h.activation(out=lruc.t[:, 4:8], in_=cc(C_LAM, 4), func=AF.Exp, scale=-1.0), reads=[cst], writes=[lruc])
            P.op("act", lambda h: h.activation(out=lruc.t[:, 4:8], in_=lruc.t[:, 4:8], func=AF.Ln, bias=1.0), reads=[lruc], writes=[lruc])
            P.op("dve", lambda h: h.tensor_scalar(out=lruc.t[:, 0:4], in0=lruc.t[:, 4:8], scalar1=-8.0, scalar2=None, op0=ALU.mult), reads=[lruc], writes=[lruc])

            nbias = mk(st1, "nbias", [128, 8], F32)[0]
            P.op("dve", lambda h: h.tensor_scalar(out=nbias.t[:], in0=cst.t[:, C_LAB:C_LAB + 8], scalar1=-1.0, scalar2=None, op0=ALU.mult), reads=[cst], writes=[nbias])
            gC = [float(np.exp(128.0 * np.log1p(-np.exp2(-5.0 - hh)))) for hh in range(4)]

            def stageA(xd, tabd, c0, nch):
                uT = uT_r.next()
                tabs = []
                for c in range(nch):
                    xs = xs_r.next()
                    r0 = (c0 + c) * 128
                    ld("sp", xs, xs.t[:], xd[r0:r0 + 128, :])
                    tb = tab_r.next()
                    ld("sp", tb, tb.t[:], tabd[r0:r0 + 128, :])
                    tabs.append(tb)
                    xn = xn_r.next()
                    norm_transpose(xs, xn, uT, c * 128, C_G1, "lnexp")
                return uT, tabs

            def fm_proj(uT, col, N, pb):
                def f(h):
                    ins = None
                    for kc in range(8):
                        ins = h.matmul(pb.t[:, 0:N], lhsT=w_in_sb.t[:, kc, col:col + 128], rhs=uT.t[:, kc, 0:N],
                                       start=(kc == 0), stop=(kc == 7))
                    return ins
                P.op("pe", f, reads=[wib[col // 512], uT], writes=[pb])

            prevN = [W]

            def lru_prelude(N):
                xl = xl_r.next()
                xl_prev = xl_r.items[(xl_r.i) % 2]
                Np = prevN[0]
                P.op("pool", lambda h: h.tensor_copy(out=xl.t[:, :, 0:3], in_=xl_prev.t[:, :, Np:Np + 3]),
                     reads=[xl_prev], writes=[xl])
                prevN[0] = N
                return xl

            def lru_tile(uT, N, full, t, xl, B):
                xc, rr, ii, aa, a2, bb, hl, xcb, zsq = (B[n] for n in ("xc", "rr", "ii", "aa", "a2", "bb", "hl", "xcb", "zsq"))
                pb = pbank(True)
                fm_proj(uT, t * 128, N, pb)
                yield
                P.op("act", lambda h: h.activation(out=xl.t[:, t, 3:3 + N], in_=pb.t[:, 0:N], func=AF.Copy), reads=[pb], writes=[xl])
                prel(pb)
                yield
                pc = pbank(True)

                def fconv(h):
                    ins = None
                    for k in range(4):
                        ins = h.matmul(pc.t[:, 0:N], lhsT=dcw.t[:, k * 4 + t, :], rhs=xl.t[:, t, k:k + N], start=(k == 0), stop=(k == 3))
                    return ins
                P.op("pe", fconv, reads=[dcw, xl], writes=[pc])
                yield
                P.op("act", lambda h: h.activation(out=xc.t[:, 0:N], in_=pc.t[:, 0:N], func=AF.Identity, bias=cc(C_LCB + t)),
                     reads=[pc, cst], writes=[xc])
                prel(pc)
                P.op("pool", lambda h: h.tensor_copy(out=xcb.t[:, 0:N], in_=xc.t[:, 0:N]), reads=[xc], writes=[xcb])
                yield
                pa = pbank(True)
                P.op("pe", lambda h: h.matmul(pa.t[:, 0:N], lhsT=gwb.t[:, t, :], rhs=xcb.t[:, 0:N], start=True, stop=True), reads=[gwb, xcb], writes=[pa])
                yield
                P.op("act", lambda h: h.activation(out=rr.t[:, 0:N], in_=pa.t[:, 0:N], func=AF.Exp, bias=nbias.t[:, t:t + 1], scale=-1.0), reads=[pa, nbias], writes=[rr])
                prel(pa)
                px = pbank(True)
                P.op("pe", lambda h: h.matmul(px.t[:, 0:N], lhsT=gwb.t[:, 4 + t, :], rhs=xcb.t[:, 0:N], start=True, stop=True), reads=[gwb, xcb], writes=[px])
                yield
                P.op("act", lambda h: h.activation(out=rr.t[:, 0:N], in_=rr.t[:, 0:N], func=AF.Ln, bias=1.0), reads=[rr], writes=[rr])
                P.op("act", lambda h: h.activation(out=ii.t[:, 0:N], in_=px.t[:, 0:N], func=AF.Exp, bias=nbias.t[:, 4 + t:5 + t], scale=-1.0), reads=[px, nbias], writes=[ii])
                prel(px)
                yield
                P.op("act", lambda h: h.activation(out=rr.t[:, 0:N], in_=rr.t[:, 0:N], func=AF.Exp, scale=-1.0), reads=[rr], writes=[rr])
                P.op("act", lambda h: h.activation(out=ii.t[:, 0:N], in_=ii.t[:, 0:N], func=AF.Ln, bias=1.0), reads=[ii], writes=[ii])
                yield
                P.op("act", lambda h: h.activation(out=aa.t[:, 0:N], in_=rr.t[:, 0:N], func=AF.Exp, scale=lruc.t[:, t:t + 1]), reads=[rr, lruc], writes=[aa])
                P.op("act", lambda h: h.activation(out=ii.t[:, 0:N], in_=ii.t[:, 0:N], func=AF.Exp, scale=-1.0), reads=[ii], writes=[ii])
                yield
                P.op("pool", lambda h: h.tensor_tensor(out=a2.t[:, 0:N], in0=aa.t[:, 0:N], in1=aa.t[:, 0:N], op=ALU.mult), reads=[aa], writes=[a2])
                P.op("pool", lambda h: h.tensor_tensor(out=bb.t[:, 0:N], in0=ii.t[:, 0:N], in1=xc.t[:, 0:N], op=ALU.mult), reads=[ii, xc], writes=[bb])
                yield
                P.op("act", lambda h: h.activation(out=a2.t[:, 0:N], in_=a2.t[:, 0:N], func=AF.Ln, bias=1.0, scale=-1.0), reads=[a2], writes=[a2])
                yield
                P.op("act", lambda h: h.activation(out=a2.t[:, 0:N], in_=a2.t[:, 0:N], func=AF.Exp, scale=0.5), reads=[a2], writes=[a2])
                yield
                P.op("pool", lambda h: h.tensor_tensor(out=bb.t[:, 0:N], in0=bb.t[:, 0:N], in1=a2.t[:, 0:N], op=ALU.mult), reads=[a2, bb], writes=[bb])
                yield
                P.op("dve", lambda h: h.tensor_tensor_scan(out=hl.t[:, 0:N], data0=aa.t[:, 0:N], data1=bb.t[:, 0:N],
                                                           initial=hstate.t[:, t:t + 1], op0=ALU.mult, op1=ALU.add),
                     reads=[aa, bb, hstate], writes=[hl])
                yield
                P.op("dve", lambda h: h.tensor_copy(out=hstate.t[:, t:t + 1], in_=hl.t[:, N - 1:N]), reads=[hl], writes=[hstate])
                if full:
                    P.op("dve", lambda h: h.tensor_tensor(out=zz[t].t[:, 0:N], in0=hl.t[:, 0:N], in1=zz[t].t[:, 0:N], op=ALU.mult), reads=[hl, zz[t]], writes=[zz[t]])
                    yield
                    P.op("pool", lambda h: h.tensor_tensor(out=zsq.t[:, 0:N], in0=zz[t].t[:, 0:N], in1=zz[t].t[:, 0:N], op=ALU.mult), reads=[zz[t]], writes=[zsq])
                    yield
                    P.op("pe", lambda h: h.matmul(pss_bank.t[:, 0:N], lhsT=onesb.t[:], rhs=zsq.t[:, 0:N], start=(t == 0), stop=(t == 3)),
                         reads=[onesb, zsq], writes=[pss_bank])
                yield

            def gelu_batch(uT, N):
                for t in range(4):
                    pg = pbank(True)
                    fm_proj(uT, 512 + t * 128, N, pg)
                    yield
                    P.op("act", lambda h, t=t, pg=pg: h.activation(out=zz[t].t[:, 0:N], in_=pg.t[:, 0:N], func=AF.Gelu), reads=[pg], writes=[zz[t]])
                    prel(pg)

            def lru_final(N, mixT):
                P.op("act", lambda h: h.activation(out=rstd_t.t[:, 0:N], in_=pss_bank.t[:, 0:N], func=AF.Ln, bias=epsc.t[:, 0:1], scale=1.0 / 512),
                     reads=[pss_bank, epsc], writes=[rstd_t])
                P.op("act", lambda h: h.activation(out=rstd_t.t[:, 0:N], in_=rstd_t.t[:, 0:N], func=AF.Exp, scale=-0.5), reads=[rstd_t], writes=[rstd_t])
                for t in range(4):
                    P.op("dve", lambda h, t=t: h.scalar_tensor_tensor(out=mixT.t[:, t, 0:N], in0=zz[t].t[:, 0:N], scalar=cc(C_LNG + t),
                                                                      in1=rstd_t.t[:, 0:N], op0=ALU.mult, op1=ALU.mult),
                         reads=[zz[t], rstd_t, cst], writes=[mixT])

            def rotary(psrc, tb, dst_bf, tA, tB):
                p3 = psrc.t[:].rearrange("p (a b) -> p a b", a=4)
                Cb = tb.t[:, 0:128].unsqueeze(1).to_broadcast([128, 4, 128])
                S0 = tb.t[:, 128:192].unsqueeze(1).to_broadcast([128, 4, 64])
                S1 = tb.t[:, 192:256].unsqueeze(1).to_broadcast([128, 4, 64])
                P.op("dve", lambda h: h.tensor_tensor(out=tA.t[:], in0=p3, in1=Cb, op=ALU.mult), reads=[psrc, tb], writes=[tA])
                P.op("dve", lambda h: h.tensor_tensor(out=tB.t[:, :, 0:64], in0=p3[:, :, 64:128], in1=S0, op=ALU.mult), reads=[psrc, tb], writes=[tB])
                P.op("dve", lambda h: h.tensor_tensor(out=tB.t[:, :, 64:128], in0=p3[:, :, 0:64], in1=S1, op=ALU.mult), reads=[psrc, tb], writes=[tB])
                P.op("pool", lambda h: h.tensor_tensor(out=dst_bf.t[:], in0=tA.t[:], in1=tB.t[:], op=ALU.add), reads=[tA, tB], writes=[dst_bf])

            def tm_proj(uT, c, col, pb):
                def f(h):
                    ins = None
                    for kc in range(8):
                        ins = h.matmul(pb.t[:], lhsT=uT.t[:, kc, c * 128:(c + 1) * 128], rhs=w_in_sb.t[:, kc, col:col + 512],
                                       start=(kc == 0), stop=(kc == 7))
                    return ins
                P.op("pe", f, reads=[wib[col // 512], uT], writes=[pb])

            def heads4(pb, fn_h, reads):
                def f(h):
                    ins = None
                    for hh in range(4):
                        ins = fn_h(h, hh)
                    return ins
                P.op("pe", f, reads=reads, writes=[pb])

            gci = [0]

            def ret_chunk(uT, c, tb, full, mixT, B):
                g = gci[0]
                gci[0] += 1
                Sb_old = S_bfs[g % 4]
                Sb_new = S_bfs[(g + 1) % 4]
                tA, tB, osb, yn, sg, vbf, vz, qbf, kbf, qT, kT, sc, ybf, bst = (B[n] for n in
                    ("tA", "tB", "osb", "yn", "sg", "vbf", "vz", "qbf", "kbf", "qT", "kT", "sc", "ybf", "bst"))
                pk = pbank(True)
                tm_proj(uT, c, 1536, pk)
                yield
                rotary(pk, tb, kbf, tA, tB)
                prel(pk)
                pv = pbank(True)
                tm_proj(uT, c, 2048, pv)
                yield
                pv3 = pv.t[:].rearrange("p (a b) -> p a b", a=4)
                P.op("dve", lambda h: h.tensor_tensor(out=vz.t[:], in0=pv3, in1=cst.t[:, C_ZETA:C_ZETA + 4].unsqueeze(2).to_broadcast([128, 4, 128]), op=ALU.mult),
                     reads=[pv, cst], writes=[vz])
                if full:
                    P.op("act", lambda h: h.activation(out=vbf.t[:].rearrange("p a b -> p (a b)"), in_=pv.t[:], func=AF.Copy), reads=[pv], writes=[vbf])
                prel(pv)
                yield
                pkv = pbank(True)
                heads4(pkv, lambda h, hh: h.matmul(pkv.t[:, hh * 128:(hh + 1) * 128], lhsT=kbf.t[:, hh, :], rhs=vz.t[:, hh, :], start=True, stop=True), [kbf, vz])
                yield
                for hh in range(4):
                    P.op("dve", lambda h, hh=hh: h.scalar_tensor_tensor(out=Sst.t[:, hh, :], in0=Sst.t[:, hh, :], scalar=gC[hh],
                                                                        in1=pkv.t[:, hh * 128:(hh + 1) * 128], op0=ALU.mult, op1=ALU.add),
                         reads=[pkv, Sst], writes=[Sst])
                prel(pkv)
                if full:
                    pq = pbank(True)
                    tm_proj(uT, c, 1024, pq)
                P.op("pool", lambda h: h.tensor_copy(out=Sb_new.t[:].rearrange("p a b -> p (a b)"), in_=Sst.t[:].rearrange("p a b -> p (a b)")),
                     reads=[Sst], writes=[Sb_new])
                yield
                if not full:
                    return
                rotary(pq, tb, qbf, tA, tB)
                prel(pq)
                pg = pbank(True)
                tm_proj(uT, c, 2560, pg)
                yield
                P.op("act", lambda h: h.activation(out=sg.t[:], in_=pg.t[:], func=AF.Exp, scale=-1.0), reads=[pg], writes=[sg])
                P.op("act", lambda h: h.activation(out=sg.t[:], in_=sg.t[:], func=AF.Ln, bias=1.0), reads=[sg], writes=[sg])
                P.op("act", lambda h: h.activation(out=sg.t[:], in_=sg.t[:], func=AF.Exp, scale=-1.0), reads=[sg], writes=[sg])
                P.op("dve", lambda h: h.tensor_tensor(out=sg.t[:], in0=pg.t[:], in1=sg.t[:], op=ALU.mult), reads=[pg, sg], writes=[sg])
                prel(pg)
                pT2 = pbank(True)
                pT2v = pT2.t.bitcast(BF16)
                heads4(pT2, lambda h, hh: h.transpose(pT2v[:, hh * 128:(hh + 1) * 128], kbf.t[:, hh, :], identb.t[:]), [kbf, identb])
                yield
                P.op("act", lambda h: h.activation(out=kT.t[:].rearrange("p a b -> p (a b)"), in_=pT2v[:, 0:512], func=AF.Copy), reads=[pT2], writes=[kT])
                prel(pT2)
                pT = pbank(True)
                pTv = pT.t.bitcast(BF16)
                heads4(pT, lambda h, hh: h.transpose(pTv[:, hh * 128:(hh + 1) * 128], qbf.t[:, hh, :], identb.t[:]), [qbf, identb])
                yield
                P.op("act", lambda h: h.activation(out=qT.t[:].rearrange("p a b -> p (a b)"), in_=pTv[:, 0:512], func=AF.Copy), reads=[pT], writes=[qT])
                prel(pT)
                yield
                psc = pbank(True)
                heads4(psc, lambda h, hh: h.matmul(psc.t[:, hh * 128:(hh + 1) * 128], lhsT=kT.t[:, hh, :], rhs=qT.t[:, hh, :], start=True, stop=True), [kT, qT])
                yield
                P.op("dve", lambda h: h.tensor_tensor(out=sc.t[:].rearrange("p a b -> p (a b)"), in0=psc.t[:], in1=dmask.t[:].rearrange("p a b -> p (a b)"), op=ALU.mult),
                     reads=[psc, dmask], writes=[sc])
                prel(psc)
                pcx = pbank(True)
                heads4(pcx, lambda h, hh: h.matmul(pcx.t[:, hh * 128:(hh + 1) * 128], lhsT=qT.t[:, hh, :], rhs=Sb_old.t[:, hh, :], start=True, stop=True), [qT, Sb_old])
                yield
                for hh in range(4):
                    P.op("act", lambda h, hh=hh: h.activation(out=tA.t[:, hh, :], in_=pcx.t[:, hh * 128:(hh + 1) * 128], func=AF.Copy, scale=cc(C_XI + hh)),
                         reads=[pcx, cst], writes=[tA])
                prel(pcx)
                po = pbank(True)
                heads4(po, lambda h, hh: h.matmul(po.t[:, hh * 128:(hh + 1) * 128], lhsT=sc.t[:, hh, :], rhs=vbf.t[:, hh, :], start=True, stop=True), [sc, vbf])
                yield
                P.op("dve", lambda h: h.tensor_tensor(out=osb.t[:].rearrange("p a b -> p (a b)"), in0=po.t[:], in1=tA.t[:].rearrange("p a b -> p (a b)"), op=ALU.add),
                     reads=[po, tA], writes=[osb])
                prel(po)
                yield
                for hh in range(4):
                    P.op("dve", lambda h, hh=hh: h.bn_stats(out=bst.t[:, hh, 0:6], in_=osb.t[:, hh, :]), reads=[osb], writes=[bst])
                for hh in range(4):
                    P.op("dve", lambda h, hh=hh: h.bn_aggr(out=bst.t[:, hh, 6:8], in_=bst.t[:, hh, 0:6]), reads=[bst], writes=[bst])
                yield
                s = stat.next()
                P.op("act", lambda h: h.activation(out=s.t[:, 0:4], in_=bst.t[:, :, 7], func=AF.Ln, bias=epsc.t[:, 0:1]), reads=[bst, epsc], writes=[s])
                yield
                P.op("act", lambda h: h.activation(out=s.t[:, 0:4], in_=s.t[:, 0:4], func=AF.Exp, scale=-0.5), reads=[s], writes=[s])
                yield
                for hh in range(4):
                    P.op("dve", lambda h, hh=hh: h.tensor_scalar(out=yn.t[:, hh, :], in0=osb.t[:, hh, :], scalar1=bst.t[:, hh, 6:7], scalar2=s.t[:, hh:hh + 1],
                                                                 op0=ALU.subtract, op1=ALU.mult),
                         reads=[osb, bst, s], writes=[yn])
                yield
                P.op("pool", lambda h: h.tensor_tensor(out=ybf.t[:].rearrange("p a b -> p (a b)"), in0=yn.t[:].rearrange("p a b -> p (a b)"), in1=sg.t[:], op=ALU.mult),
                     reads=[yn, sg], writes=[ybf])
                yield
                pY = pbank(True)
                pYv = pY.t.bitcast(BF16)
                heads4(pY, lambda h, hh: h.transpose(pYv[:, hh * 128:(hh + 1) * 128], ybf.t[:, hh, :], identb.t[:]), [ybf, identb])
                yield
                P.op("dve", lambda h: h.tensor_tensor(out=mixT.t[:, 4:8, c * 128:(c + 1) * 128], in0=pYv[:, 0:512].rearrange("p (a b) -> p a b", a=4),
                                                      in1=cst.t[:, C_RNG:C_RNG + 4].unsqueeze(2).to_broadcast([128, 4, 128]), op=ALU.mult),
                     reads=[pY, cst], writes=[mixT])
                prel(pY)
                yield

            def outproj_chunk(mixT, c, gch):
                r0 = gch * 128
                hm = hm_r.next()
                ld("sp", hm, hm.t[:], xm[r0:r0 + 128, :])
                for half in range(2):
                    pb = pbank()

                    def f(h, half=half, pb=pb):
                        ins = None
                        for kc in range(8):
                            ins = h.matmul(pb.t[:], lhsT=mixT.t[:, kc, c * 128:(c + 1) * 128], rhs=w_out_sb.t[:, kc, half * 512:(half + 1) * 512],
                                           start=(kc == 0), stop=(kc == 7))
                        return ins
                    P.op("pe", f, reads=[mixT, w_out_sb], writes=[pb])
                    P.op("dve", lambda h, half=half, pb=pb: h.tensor_tensor(out=hm.t[:, half * 512:(half + 1) * 512], in0=pb.t[:], in1=hm.t[:, half * 512:(half + 1) * 512], op=ALU.add),
                         reads=[pb, hm], writes=[hm])
                hb = hbufs[gch]
                P.dma("sp", key_of(hm), lambda h: h.dma_start(out=hmid[r0:r0 + 128, :], in_=hm.t[:]), reads=[hm], writes=[hb])

            def run_lanes(lanes, delays=None):
                cur = [None] * len(lanes)
                idx = [0] * len(lanes)
                dl = list(delays) if delays else [0] * len(lanes)
                dl = dl + [0] * (len(lanes) - len(dl))
                active = True
                while active:
                    active = False
                    for li, lane in enumerate(lanes):
                        if dl[li] > 0:
                            dl[li] -= 1
                            active = True
                            continue
                        if cur[li] is None:
                            if idx[li] < len(lane):
                                cur[li] = lane[idx[li]]
                                idx[li] += 1
                            else:
                                continue
                        active = True
                        try:
                            next(cur[li])
                        except StopIteration:
                            cur[li] = None

            hbufs = [Buf("hmid%d" % i) for i in range(NM)]

            def tail_gen(prev, uT, N):
                if prev is not None:
                    lru_final(prev[0], prev[1])
                yield
                if uT is not None:
                    for _ in gelu_batch(uT, N):
                        yield
                    yield
                if prev is not None:
                    for c in range(prev[3]):
                        outproj_chunk(prev[1], c, prev[2] + c)
                        yield
                        yield

            def mixer(blocks, xd, tabd, full, flag_after_first):
                if not blocks:
                    return
                nxt = stageA(xd, tabd, *blocks[0])
                tail = None
                for bi, (c0, nch) in enumerate(blocks):
                    N = nch * 128
                    uT, tabs = nxt
                    if bi + 1 < len(blocks):
                        nxt = stageA(xd, tabd, *blocks[bi + 1])
                    mixT = mixT_r.next() if full else None
                    xl = lru_prelude(N)
                    if full:
                        lanes = [
                            [lru_tile(uT, N, full, 0, xl, LS[0]), lru_tile(uT, N, full, 2, xl, LS[0])],
                            [lru_tile(uT, N, full, 1, xl, LS[1]), lru_tile(uT, N, full, 3, xl, LS[1])],
                        ]
                        dls = [0, LDL]
                    else:
                        lanes = [[lru_tile(uT, N, full, 0, xl, LS[0])], [lru_tile(uT, N, full, 1, xl, LS[1])],
                                 [lru_tile(uT, N, full, 2, xl, LSP[0])], [lru_tile(uT, N, full, 3, xl, LSP[1])]]
                        dls = [0, 0, 1, 1]
                    lanes = lanes + [[ret_chunk(uT, c, tabs[c], full, mixT, RS[c])] for c in range(nch)]
                    dls = dls + [(RDL if full else 1) * c for c in range(nch)]
                    if full:
                        lanes.append([tail_gen(tail, uT, N)])
                    run_lanes(lanes, dls)
                    if bgq and bi >= 1:
                        bgq.pop(0)()
                    tail = (N, mixT, c0, nch) if full else None
                    if flag_after_first and bi == 0:
                        P.op("dve", lambda h: h.tensor_scalar(out=hstate.t[:], in0=hstate.t[:], scalar1=cc(C_FLAG), scalar2=None, op0=ALU.mult),
                             reads=[hstate, cst], writes=[hstate])
                if tail is not None:
                    run_lanes([[tail_gen(tail, None, 0)]])

            mixer(split_blocks(NP, BC) if STG >= 1 else [], xp, tabp, False, False)
            mixer(split_blocks(NM, BC, first=1) if STG >= 2 else [], xm, tabm, True, True)
            while bgq:
                bgq.pop(0)()
            P.barrier()
            P.emit_block()

        with ExitStack() as st2:
            NF = FBC * 128
            w_up_all = mk(st2, "w_up_all", [128, 8, 8, 768], BF16)[0]
            w_dn_all = mk(st2, "w_dn_all", [128, 24, D], BF16)[0]
            wup_b = [Buf("wupg%d" % g) for g in range(8)]
            wdn_b = [Buf("wdng%d" % g) for g in range(4)]
            gfin = mk(st2, "gfin", [128, D], F32)[0]
            tails = mk(st2, "tails", [128, 48, 2], F32)[0]
            tails_b = [Buf("tails%d" % j) for j in range(48)]
            hs_r = Rot(mk(st2, "hs", [128, D], F32, FBC))
            ha_r = Rot(mk(st2, "ha", [128, D], F32, 2))
            xn2_r = Rot(mk(st2, "xn2", [128, D], BF16, 1))
            u2T_r = Rot(mk(st2, "u2T", [128, 8, NF], BF16, 1))
            gT = [Rot(mk(st2, "gT%d" % m, [128, NF], BF16, 1)) for m in range(24)]
            ya_r = Rot(mk(st2, "ya", [128, NF], F32, 2))
            yv_r = Rot(mk(st2, "yv", [128, NF], F32, 2))
            U_r = Rot(mk(st2, "U", [128, NF + 2], F32, 3))

            for g in (0, 4, 1, 5, 2, 6, 3, 7):
                P.dma("sp", key_of(wup_b[g]), lambda h, g=g: h.dma_start(out=w_up_all.t[:, g, :, :].rearrange("p a b -> p (a b)"), in_=wup_bf[g].rearrange("p a b -> p (a b)")),
                      reads=[wupd_b], writes=[wup_b[g]])
            for i in range(4):
                P.dma("sp", key_of(wdn_b[i]), lambda h, i=i: h.dma_start(out=w_dn_all.t[:, i * 6:(i + 1) * 6, :].rearrange("p a b -> p (a b)"), in_=wdn_bf[:, i * 6:(i + 1) * 6, :].rearrange("p a b -> p (a b)")),
                      reads=[wdnd_b], writes=[wdn_b[i]])
            ld("sp", gfin, gfin.t[:], gfin_d)
            for j in range(48):
                P.op("pool", lambda h, j=j: h.memset(tails.t[:, j, :], 0.0), writes=[tails_b[j]])

            Uh_b = {}

            def conv_tile(pu, j, N, y, init_on_act):
                w0, w1, w2 = cc(C_FCW + j), cc(C_FCW + 48 + j), cc(C_FCW + 96 + j)
                U = U_r.next()
                Uh = Uh_b.setdefault(U.b.name, Buf(U.b.name + "_h"))
                P.op("pool", lambda h: h.tensor_copy(out=U.t[:, 0:2], in_=tails.t[:, j, :]), reads=[tails_b[j]], writes=[Uh])
                P.op("act", lambda h: h.activation(out=U.t[:, 2:2 + N], in_=pu.t[:, 0:N], func=AF.Copy), reads=[pu], writes=[U])
                if init_on_act:
                    P.op("act", lambda h: h.activation(out=y.t[:, 0:N], in_=pu.t[:, 0:N], func=AF.Identity, bias=cc(C_FCB + j), scale=w2),
                         reads=[pu, cst], writes=[y])
                else:
                    P.op("dve", lambda h: h.tensor_scalar(out=y.t[:, 0:N], in0=U.t[:, 2:2 + N], scalar1=w2, scalar2=cc(C_FCB + j), op0=ALU.mult, op1=ALU.add),
                         reads=[U, cst], writes=[y])
                P.op("dve", lambda h: h.scalar_tensor_tensor(out=y.t[:, 0:N], in0=U.t[:, 1:1 + N], scalar=w1, in1=y.t[:, 0:N], op0=ALU.mult, op1=ALU.add),
                     reads=[U, Uh, y, cst], writes=[y])
                P.op("dve", lambda h: h.scalar_tensor_tensor(out=y.t[:, 0:N], in0=U.t[:, 0:N], scalar=w0, in1=y.t[:, 0:N], op0=ALU.mult, op1=ALU.add),
                     reads=[U, Uh, y, cst], writes=[y])
                P.op("pool", lambda h: h.tensor_copy(out=tails.t[:, j, :], in_=U.t[:, N:N + 2]), reads=[U], writes=[tails_b[j]])

            last_out = None
            fblocks = split_blocks(NM, FBC) if STG >= 3 else []

            def stageA_F(c0, nch):
                u2T = u2T_r.next()
                for c in range(nch):
                    ha = ha_r.next()
                    r0 = (c0 + c) * 128
                    P.dma("sp", key_of(ha), lambda h, ha=ha, r0=r0: h.dma_start(out=ha.t[:], in_=hmid[r0:r0 + 128, :]), reads=[hbufs[c0 + c]], writes=[ha])
                    xn2 = xn2_r.next()
                    norm_transpose(ha, xn2, u2T, c * 128, C_G2, None)
                return u2T

            u2T_next = stageA_F(*fblocks[0]) if fblocks else None
            for bi, (c0, nch) in enumerate(fblocks):
                N = nch * 128
                u2T = u2T_next
                pend = None
                for m in range(24):
                    pa = pbank()
                    pv = pbank()
                    for (pb, j) in ((pa, m), (pv, 24 + m)):
                        def f(h, pb=pb, j=j, N=N, u2T=u2T):
                            ins = None
                            for kc in range(8):
                                ins = h.matmul(pb.t[:, 0:N], lhsT=w_up_all.t[:, j // 6, kc, (j % 6) * 128:(j % 6 + 1) * 128], rhs=u2T.t[:, kc, 0:N], start=(kc == 0), stop=(kc == 7))
                            return ins
                        P.op("pe", f, reads=[wup_b[j // 6], u2T], writes=[pb])
                    ya = ya_r.next()
                    yv = yv_r.next()
                    conv_tile(pa, m, N, ya, True)
                    conv_tile(pv, 24 + m, N, yv, True)
                    if pend is not None:
                        pend()

                    def pend(ya=ya, yv=yv, m=m, N=N):
                        P.op("act", lambda h: h.activation(out=ya.t[:, 0:N], in_=ya.t[:, 0:N], func=AF.Gelu), reads=[ya], writes=[ya])
                        g = gT[m].next()
                        P.op("pool", lambda h: h.tensor_tensor(out=g.t[:, 0:N], in0=ya.t[:, 0:N], in1=yv.t[:, 0:N], op=ALU.mult),
                             reads=[ya, yv], writes=[g])
                pend()
                hss = []
                for c in range(nch):
                    hs = hs_r.next()
                    r0 = (c0 + c) * 128
                    P.dma("sp", key_of(hs), lambda h, hs=hs, r0=r0: h.dma_start(out=hs.t[:], in_=hmid[r0:r0 + 128, :]), reads=[hbufs[c0 + c]], writes=[hs])
                    hss.append(hs)
                if bi + 1 < len(fblocks):
                    u2T_next = stageA_F(*fblocks[bi + 1])
                for c in range(nch):
                    hs = hss[c]
                    for half in range(2):
                        pb = pbank()

                        def f(h, half=half, pb=pb, c=c):
                            ins = None
                            for m in range(24):
                                ins = h.matmul(pb.t[:], lhsT=gT[m].items[0].t[:, c * 128:(c + 1) * 128], rhs=w_dn_all.t[:, m, half * 512:(half + 1) * 512],
                                               start=(m == 0), stop=(m == 23))
                            return ins
                        P.op("pe", f, reads=[gT[m].items[0] for m in range(24)] + wdn_b, writes=[pb])
                        P.op("dve", lambda h, half=half, pb=pb, hs=hs: h.tensor_tensor(out=hs.t[:, half * 512:(half + 1) * 512], in0=pb.t[:],
                                                                                       in1=hs.t[:, half * 512:(half + 1) * 512], op=ALU.add),
                             reads=[pb, hs], writes=[hs])
                    s = stat.next()
                    junk = xn2_r.next()
                    P.op("pool", lambda h, s=s: h.memset(s.t[:], 0.0), writes=[s])
                    P.op("act", lambda h, s=s, hs=hs, junk=junk: h.activation(out=junk.t[:], in_=hs.t[:], func=AF.Square, accum_out=s.t[:, 0:1]),
                         reads=[hs, s], writes=[junk, s])
                    P.op("act", lambda h, s=s: h.activation(out=s.t[:, 1:2], in_=s.t[:, 0:1], func=AF.Sqrt, bias=epsc.t[:, 0:1], scale=1.0 / D),
                         reads=[s, epsc], writes=[s])
                    P.op("dve", lambda h, s=s: h.reciprocal(out=s.t[:, 2:3], in_=s.t[:, 1:2]), reads=[s], writes=[s])
                    P.op("act", lambda h, s=s, hs=hs: h.activation(out=hs.t[:], in_=hs.t[:], func=AF.Copy, scale=s.t[:, 2:3]),
                         reads=[hs, s], writes=[hs])
                    P.op("pool", lambda h, hs=hs: h.tensor_tensor(out=hs.t[:], in0=hs.t[:], in1=gfin.t[:], op=ALU.mult),
                         reads=[hs, gfin], writes=[hs])
                    r0 = (c0 + c) * 128
                    last_out = P.dma("sp", key_of(hs), lambda h, hs=hs, r0=r0: h.dma_start(out=out[r0:r0 + 128, :], in_=hs.t[:]), reads=[hs])
            P.barrier()
            P.emit_block()
    return nc


_CACHE = {}


def _tables(pos):
    inv_freq = (np.float32(10000.0) ** (-np.arange(0, 128, 2, dtype=np.float32) / np.float32(128))).astype(np.float32)
    ang = (pos.astype(np.float32)[:, None] * inv_freq[None, :]).astype(np.float32)
    c = np.cos(ang).astype(np.float32)
    s = np.sin(ang).astype(np.float32)
    return np.ascontiguousarray(np.concatenate([c, c, -s, s], axis=1).astype(np.float32))


def _consts():
    log_g = np.log1p(-np.exp2(-5.0 - np.arange(4, dtype=np.float64)))
    idx = np.arange(128, dtype=np.float64)
    sc = 128.0 ** -0.5
    diff = idx[None, :] - idx[:, None]
    dm = np.where(diff[:, None, :] >= 0, np.exp(np.maximum(diff, 0.0)[:, None, :] * log_g[None, :, None]), 0.0) * sc
    xi = np.exp((idx + 1.0)[:, None] * log_g[None, :])
    zeta = np.exp((127.0 - idx)[:, None] * log_g[None, :]) * sc
    return dm.reshape(128, 512).astype(np.float32), xi.astype(np.float32), zeta.astype(np.float32)


def kernel(x, norm1_gain, w_in, lru_conv_w, lru_conv_b, lru_gate_a_w, lru_gate_a_b,
           lru_gate_x_w, lru_gate_x_b, lru_lambda, lru_norm_gain, ret_norm_gain, w_out,
           norm2_gain, ffn_up_w, ffn_conv_w, ffn_conv_b, ffn_down_w, final_norm_gain, _dbg=None):
    f = lambda a: np.ascontiguousarray(np.asarray(a, dtype=np.float32))
    x = f(x)
    B, S, _ = x.shape
    half = S // 2
    NM = half // 128 + 1
    NP = half // 128 - 1
    key = (NM, NP)
    if key not in _CACHE:
        _CACHE[key] = build(NM, NP)
    nc = _CACHE[key]

    dmask, xi, zeta = _consts()

    def pp(v, n):
        return f(v).reshape(n, 128).T
    cst = np.zeros((128, NCST), np.float32)
    cst[:, C_G1:C_G1 + 8] = pp(norm1_gain[0], 8)
    cst[:, C_G2:C_G2 + 8] = pp(norm2_gain[0], 8)
    for k in range(4):
        cst[:, C_LCW + k * 4:C_LCW + k * 4 + 4] = pp(lru_conv_w[0, k], 4)
    cst[:, C_LCB:C_LCB + 4] = pp(lru_conv_b[0], 4)
    cst[:, C_LAB:C_LAB + 4] = pp(lru_gate_a_b[0], 4)
    cst[:, C_LXB:C_LXB + 4] = pp(lru_gate_x_b[0], 4)
    cst[:, C_LAM:C_LAM + 4] = pp(lru_lambda[0], 4)
    cst[:, C_LNG:C_LNG + 4] = pp(lru_norm_gain[0], 4)
    cst[:, C_RNG:C_RNG + 4] = pp(ret_norm_gain[0], 4)
    for k in range(3):
        cst[:, C_FCW + k * 48:C_FCW + k * 48 + 48] = pp(ffn_conv_w[0, k], 48)
    cst[:, C_FCB:C_FCB + 48] = pp(ffn_conv_b[0], 48)
    cst[:, C_XI:C_XI + 4] = xi
    cst[:, C_ZETA:C_ZETA + 4] = zeta
    gw = np.zeros((128, 8, 128), np.float32)
    for j, wsrc in enumerate((f(lru_gate_a_w[0]), f(lru_gate_x_w[0]))):
        for t in range(4):
            gw[0:64, j * 4 + t, 0:64] = wsrc[2 * t]
            gw[64:128, j * 4 + t, 64:128] = wsrc[2 * t + 1]
    gfin = np.ascontiguousarray(np.broadcast_to(f(final_norm_gain)[None, :], (128, D)))
    ident = np.eye(128, dtype=np.float32)
    shared = {
        "gfin": gfin, "dmask": dmask, "gw": gw.reshape(128, 1024), "ident": ident,
        "w_in": f(w_in[0]), "w_out": f(w_out[0]), "w_up": f(ffn_up_w[0]), "w_dn": f(ffn_down_w[0]),
    }
    in_maps = []
    for b in range(B):
        for hh in range(2):
            c2 = cst.copy()
            c2[:, C_FLAG] = float(hh)
            if hh == 0:
                xmc = np.concatenate([np.zeros((128, D), np.float32), x[b, :half]], axis=0)
                xpc = np.zeros((max(NP, 1) * 128, D), np.float32)
                posm = np.concatenate([np.arange(128), np.arange(half)])
                posp = np.zeros(max(NP, 1) * 128)
            else:
                xmc = x[b, half - 128:]
                xpc = x[b, :half - 128] if NP > 0 else np.zeros((128, D), np.float32)
                posm = np.arange(half - 128, S)
                posp = np.arange(half - 128) if NP > 0 else np.zeros(128)
            m = dict(shared)
            m.update({"xm": np.ascontiguousarray(xmc), "xp": np.ascontiguousarray(xpc), "tabm": _tables(posm), "tabp": _tables(posp), "cst": c2})
            in_maps.append(m)
    res = run_bass_kernel_spmd(nc, in_maps, core_ids=list(range(B * 2)))
    if os.environ.get("KHM"):
        global _HM
        _HM = np.empty((B, S, D), np.float32)
        for b in range(B):
            for hh in range(2):
                _HM[b, hh * half:(hh + 1) * half] = res.results[b * 2 + hh]["hmid"][128:]
    outp = np.empty((B, S, D), np.float32)
    for b in range(B):
        for hh in range(2):
            o = res.results[b * 2 + hh]["out"]
            outp[b, hh * half:(hh + 1) * half] = o[128:]
    return outp
```

```python
import os
import numpy as np
from contextlib import ExitStack
import concourse.bass as bass
import concourse.mybir as mybir
from concourse.bass_utils import run_bass_kernel_spmd

F32 = mybir.dt.float32
BF16 = mybir.dt.bfloat16
AF = mybir.ActivationFunctionType
ALU = mybir.AluOpType

D = 1024
DIN = 3072
DFF = 3072
EPS = 1e-6
ENGS = ("pe", "act", "dve", "pool", "sp")

C_G1, C_G2, C_LCW, C_LCB, C_LAB, C_LXB, C_LAM, C_LNG, C_RNG = 0, 8, 16, 32, 36, 40, 44, 48, 52
C_FCW, C_FCB, C_XI, C_ZETA, C_FLAG, NCST = 56, 200, 248, 252, 256, 257


class Buf:
    __slots__ = ("name", "w", "r", "ps")

    def __init__(self, name):
        self.name = name
        self.w = None
        self.r = []
        self.ps = False


class T:
    __slots__ = ("t", "b")

    def __init__(self, t, name):
        self.t = t
        self.b = Buf(name)


class Prog:
    def __init__(self, nc):
        self.nc = nc
        self.ops = {e: [] for e in ENGS}
        self.cnt = {"E_" + e: 0 for e in ENGS}
        self.sems = {}
        self.waited = {e: {} for e in ENGS}
        self.dkeys = []

    def _need(self, eng, toks):
        out = []
        for (k, v, _e) in toks:
            if self.waited[eng].get(k, 0) >= v:
                continue
            self.waited[eng][k] = v
            out.append((k, v))
        return out

    def _deps(self, eng, reads, writes):
        best = {}

        def add(t):
            k = t[0]
            if k not in best or best[k][1] < t[1]:
                best[k] = t
        for b in reads:
            if b.w is not None:
                add(b.w)
            if b.ps:
                for t in b.r:
                    if t[2] != eng:
                        add(t)
        for b in writes:
            if b.w is not None and b.w[2] != eng:
                add(b.w)
            for t in b.r:
                if t[2] != eng:
                    add(t)
        return self._need(eng, best.values())

    def _commit(self, tok, reads, writes):
        for b in reads:
            b.r.append(tok)
        for b in writes:
            b.w = tok
            b.r = []

    def op(self, eng, fn, reads=(), writes=()):
        reads = [x.b if hasattr(x, "b") else x for x in reads]
        writes = [x.b if hasattr(x, "b") else x for x in writes]
        waits = self._deps(eng, reads, writes)
        k = "E_" + eng
        self.cnt[k] += 1
        self.ops[eng].append((waits, fn, (k, 1)))
        tok = (k, self.cnt[k], eng)
        self._commit(tok, reads, writes)
        return tok

    def dma(self, eng, key, fn, reads=(), writes=()):
        reads = [x.b if isinstance(x, T) else x for x in reads]
        writes = [x.b if isinstance(x, T) else x for x in writes]
        waits = self._deps(eng, reads, writes)
        assert key in self.cnt, key
        self.cnt[key] += 16
        self.ops[eng].append((waits, fn, (key, 16)))
        tok = (key, self.cnt[key], "dma:" + key)
        self._commit(tok, reads, writes)
        return tok

    def wait_tok(self, eng, tok):
        w = self._need(eng, [tok])
        if w:
            self.ops[eng].append((w, None, None))

    def barrier(self):
        for e in ENGS:
            toks = [(k, v, "x") for k, v in self.cnt.items() if v > 0]
            w = self._need(e, toks)
            if w:
                self.ops[e].append((w, None, None))

    def emit_block(self):
        nc = self.nc
        sems = self.sems
        with nc.Block() as block:
            def runner(eng):
                ops = self.ops[eng]

                def _run(h):
                    for (waits, fn, inc) in ops:
                        for (k, v) in waits:
                            h.wait_ge(sems[k], v)
                        if fn is not None:
                            ins = fn(h)
                            ins.then_inc(sems[inc[0]], inc[1])
                return _run
            block.tensor(runner("pe"))
            block.scalar(runner("act"))
            block.vector(runner("dve"))
            block.gpsimd(runner("pool"))
            block.sync(runner("sp"))
        self.ops = {e: [] for e in ENGS}


class Rot:
    def __init__(self, items):
        self.items = items
        self.i = 0

    def next(self):
        x = self.items[self.i % len(self.items)]
        self.i += 1
        return x


def split_blocks(n, bc, first=None):
    out = []
    s = 0
    if first:
        out.append((0, first))
        s = first
    while s < n:
        m = min(bc, n - s)
        out.append((s, m))
        s += m
    return out


def build(NM, NP, BC=3, FBC=int(os.environ.get('KFBC', '3')), dbg=False):
    nc = bass.Bass("TRN2", target_bir_lowering=False)

    def din(name, shape):
        return nc.dram_tensor(name, shape, F32, kind="ExternalInput").ap()
    xm = din("xm", [NM * 128, D])
    xp = din("xp", [max(NP, 1) * 128, D])
    tabm = din("tabm", [NM * 128, 256])
    tabp = din("tabp", [max(NP, 1) * 128, 256])
    cst_d = din("cst", [128, NCST])
    gfin_d = din("gfin", [128, D])
    dmask_d = din("dmask", [128, 512])
    gw_d = din("gw", [128, 8 * 128])
    ident_d = din("ident", [128, 128])
    w_in = din("w_in", [D, DIN])
    w_out = din("w_out", [D, D])
    w_up = din("w_up", [D, 2 * DFF])
    w_dn = din("w_dn", [DFF, D])
    out = nc.dram_tensor("out", [NM * 128, D], F32, kind="ExternalOutput").ap()
    if os.environ.get("KHM"):
        hmid = nc.dram_tensor("hmid", [NM * 128, D], F32, kind="ExternalOutput").ap()
    else:
        hmid = nc.dram_tensor("hmid", [NM * 128, D], F32).ap()

    wup_bf = nc.dram_tensor("wup_bf", [8, 128, 8, 768], BF16).ap()
    wdn_bf = nc.dram_tensor("wdn_bf", [128, 24, D], BF16).ap()
    P = Prog(nc)
    STG = int(os.environ.get('KDBG', '9'))
    MSK = int(os.environ.get('KMSK', '7'))
    SUB = int(os.environ.get('KSUB', '9'))
    LDL = int(os.environ.get('KLDL', '3'))
    RDL = int(os.environ.get('KRDL', '2'))
    NBLK = 0
    keyctr = [0]

    with ExitStack() as st0:
        def newkey(name):
            k = "D_%s_%d" % (name, keyctr[0])
            keyctr[0] += 1
            P.cnt[k] = 0
            P.dkeys.append(k)
            return k

        def mk(st, name, shape, dt, n=1):
            res = []
            for i in range(n):
                nm = "%s_%d" % (name, i)
                res.append(T(st.enter_context(nc.sbuf_tensor(nm, shape, dt)), nm))
            return res

        NKEYS = 64
        keypool = [newkey("k") for _ in range(NKEYS)]
        keyuse = {}

        def key_of(buf):
            b = buf.b if isinstance(buf, T) else buf
            if b.name not in keyuse:
                keyuse[b.name] = keypool.pop()
            return keyuse[b.name]

        for k in P.cnt:
            P.sems[k] = st0.enter_context(nc.semaphore(k))

        banks = [T(st0.enter_context(nc.psum_tensor("pb%d" % i, [128, 512], F32)), "pb%d" % i) for i in range(8)]
        for bk in banks:
            bk.b.ps = True
        bankrot = Rot(banks[:7])
        pss_bank = banks[7]

        bank_busy = {}

        def pbank(hold=False):
            for _ in range(len(bankrot.items)):
                bk = bankrot.next()
                if not bank_busy.get(bk.b.name):
                    if hold:
                        bank_busy[bk.b.name] = True
                    return bk
            raise RuntimeError("no free PSUM bank")

        def prel(bk):
            bank_busy[bk.b.name] = False

        cst = mk(st0, "cst", [128, NCST], F32)[0]
        identb = mk(st0, "identb", [128, 128], BF16)[0]
        epsc = mk(st0, "epsc", [128, 1], F32)[0]

        def ld(eng, dst, dst_ap, src_ap, reads=()):
            return P.dma(eng, key_of(dst), lambda h: h.dma_start(out=dst_ap, in_=src_ap), reads=reads, writes=[dst])

        ld("sp", cst, cst.t[:], cst_d)
        ld("pool", identb, identb.t[:], ident_d)
        P.op("pool", lambda h: h.memset(epsc.t[:], EPS), writes=[epsc])

        def cc(col, n=1):
            return cst.t[:, col:col + n]

        stat = Rot(mk(st0, "stat", [128, 4], F32, 12))

        def norm_transpose(xsrc, xn_t, uT_t, ucol, gcol, ntok_cols):
            s = stat.next()
            P.op("pool", lambda h: h.memset(s.t[:], 0.0), writes=[s])
            P.op("act", lambda h: h.activation(out=xn_t.t[:], in_=xsrc.t[:], func=AF.Square, accum_out=s.t[:, 0:1]),
                 reads=[xsrc, s], writes=[xn_t, s])
            if ntok_cols == "lnexp":
                P.op("act", lambda h: h.activation(out=s.t[:, 1:2], in_=s.t[:, 0:1], func=AF.Ln, bias=epsc.t[:, 0:1], scale=1.0 / D),
                     reads=[s, epsc], writes=[s])
                P.op("act", lambda h: h.activation(out=s.t[:, 2:3], in_=s.t[:, 1:2], func=AF.Exp, scale=-0.5), reads=[s], writes=[s])
            else:
                P.op("act", lambda h: h.activation(out=s.t[:, 1:2], in_=s.t[:, 0:1], func=AF.Sqrt, bias=epsc.t[:, 0:1], scale=1.0 / D),
                     reads=[s, epsc], writes=[s])
                P.op("dve", lambda h: h.reciprocal(out=s.t[:, 2:3], in_=s.t[:, 1:2]), reads=[s], writes=[s])
            P.op("act", lambda h: h.activation(out=xn_t.t[:], in_=xsrc.t[:], func=AF.Copy, scale=s.t[:, 2:3]),
                 reads=[xsrc, s], writes=[xn_t])
            pb = pbank()
            pbv = pb.t.bitcast(BF16)

            def tr(h):
                ins = None
                for kc in range(8):
                    ins = h.transpose(pbv[:, kc * 128:(kc + 1) * 128], xn_t.t[:, kc * 128:(kc + 1) * 128], identb.t[:])
                return ins
            P.op("pe", tr, reads=[xn_t, identb], writes=[pb])
            P.op("dve", lambda h: h.tensor_tensor(
                out=uT_t.t[:, :, ucol:ucol + 128],
                in0=pbv[:, 0:1024].rearrange("p (k t) -> p k t", k=8),
                in1=cst.t[:, gcol:gcol + 8].unsqueeze(2).to_broadcast([128, 8, 128]), op=ALU.mult),
                reads=[pb, cst], writes=[uT_t])
            return s

        with ExitStack() as st1:
            W = BC * 128
            w_in_sb = mk(st1, "w_in_sb", [128, 8, DIN], BF16)[0]
            w_out_sb = mk(st1, "w_out_sb", [128, 8, D], BF16)[0]
            gwb = mk(st1, "gwb", [128, 8, 128], BF16)[0]
            dcw = mk(st1, "dcw", [128, 16, 128], BF16)[0]
            onesb = mk(st1, "onesb", [128, 128], BF16)[0]
            dmask = mk(st1, "dmask", [128, 4, 128], F32)[0]
            lruc = mk(st1, "lruc", [128, 8], F32)[0]
            hstate = mk(st1, "hstate", [128, 4], F32)[0]
            Sst = mk(st1, "Sst", [128, 4, 128], F32)[0]
            S_bfs = mk(st1, "S_bf", [128, 4, 128], BF16, 4)
            xs_r = Rot(mk(st1, "xs", [128, D], F32, 2))
            hm_r = Rot(mk(st1, "hm", [128, D], F32, 2))
            tab_r = Rot(mk(st1, "tab", [128, 256], F32, 6))
            xn_r = Rot(mk(st1, "xn", [128, D], BF16, 2))
            uT_r = Rot(mk(st1, "uT", [128, 8, W], BF16, 2))
            xl_r = Rot(mk(st1, "xl", [128, 4, 3 + W], BF16, 2))
            mixT_r = Rot(mk(st1, "mixT", [128, 8, W], BF16, 2))
            LS = []
            for i in range(2):
                d = {n: mk(st1, "%s%d" % (n, i), [128, W], F32)[0] for n in ("xc", "rr", "ii", "aa", "bb")}
                d["a2"], d["hl"], d["gg"] = d["rr"], d["ii"], d["xc"]
                d["xcb"] = mk(st1, "xcb%d" % i, [128, W], BF16)[0]
                d["zsq"] = mk(st1, "zsq%d" % i, [128, W], BF16)[0]
                LS.append(d)
            rstd_t = mk(st1, "rstd", [128, W], F32)[0]
            zz = mk(st1, "zz", [128, W], F32, 4)
            RS = []
            for i in range(3):
                d = {}
                for n in ("tA", "tB", "osb"):
                    d[n] = mk(st1, "%s%d" % (n, i), [128, 4, 128], F32)[0]
                d["sg"] = mk(st1, "sg%d" % i, [128, 512], F32)[0]
                for n in ("vbf", "vz", "qbf", "kbf", "qT", "kT", "sc"):
                    d[n] = mk(st1, "%s%d" % (n, i), [128, 4, 128], BF16)[0]
                d["yn"], d["ybf"] = d["tB"], d["qbf"]
                d["bst"] = mk(st1, "bst%d" % i, [128, 4, 8], F32)[0]
                RS.append(d)

            class V:
                def __init__(self, t_obj, flat):
                    self.t = t_obj.t[:].rearrange("p a b -> p (a b)") if flat else t_obj.t
                    self.b = t_obj.b

            LSP = [
                {"xc": V(zz[0], False), "rr": V(zz[1], False), "ii": V(zz[2], False), "aa": V(zz[3], False),
                 "bb": V(RS[0]["osb"], True), "xcb": V(RS[0]["qT"], True), "zsq": None},
                {"xc": V(RS[0]["sg"], False), "rr": V(RS[1]["sg"], False), "ii": V(RS[2]["sg"], False), "aa": V(RS[1]["osb"], True),
                 "bb": V(RS[2]["osb"], True), "xcb": V(RS[1]["qT"], True), "zsq": None},
            ]
            for d in LSP:
                d["a2"], d["hl"], d["gg"] = d["rr"], d["ii"], d["xc"]

            w_in_v = w_in.rearrange("(kc p) n -> p kc n", p=128)
            wib = [w_in_sb.b] * 6
            for kc in range(8):
                ld("pool", w_in_sb, w_in_sb.t[:, kc, :], w_in_v[:, kc, :])
            ld("pool", gwb, gwb.t[:].rearrange("p a b -> p (a b)"), gw_d)
            ld("sp", dmask, dmask.t[:].rearrange("p a b -> p (a b)"), dmask_d)
            w_out_v = w_out.rearrange("(kc p) n -> p kc n", p=128)
            for kc in range(8):
                ld("pool", w_out_sb, w_out_sb.t[:, kc, :], w_out_v[:, kc, :])
            wupd_b = Buf("wupd")
            wdnd_b = Buf("wdnd")
            bgq = []
            w_up_g = w_up.rearrange("(kc p) (g c) -> g p kc c", p=128, c=768)
            for g in (0, 4, 1, 5, 2, 6, 3, 7):
                bgq.append(lambda g=g: P.dma("pool", key_of(wupd_b), lambda h: h.dma_start(out=wup_bf[g], in_=w_up_g[g]), writes=[wupd_b]))
            w_dn_g = w_dn.rearrange("(m p) n -> p m n", p=128)
            for i in range(0, 24, 6):
                bgq.append(lambda i=i: P.dma("pool", key_of(wdnd_b), lambda h: h.dma_start(out=wdn_bf[:, i:i + 6, :], in_=w_dn_g[:, i:i + 6, :]), writes=[wdnd_b]))
            P.op("pool", lambda h: h.memset(onesb.t[:], 1.0), writes=[onesb])
            P.op("pool", lambda h: h.memset(hstate.t[:], 0.0), writes=[hstate])
            P.op("pool", lambda h: h.memset(Sst.t[:].rearrange("p a b -> p (a b)"), 0.0), writes=[Sst])
            for sb_ in S_bfs:
                P.op("pool", lambda h, sb_=sb_: h.memset(sb_.t[:].rearrange("p a b -> p (a b)"), 0.0), writes=[sb_])
            for xl in xl_r.items:
                P.op("pool", lambda h, xl=xl: h.memset(xl.t[:].rearrange("p a b -> p (a b)"), 0.0), writes=[xl])
            for j in range(16):
                P.op("dve", lambda h, j=j: h.tensor_scalar(out=dcw.t[:, j, :], in0=identb.t[:], scalar1=cc(C_LCW + j), scalar2=None, op0=ALU.mult),
                     reads=[identb, cst], writes=[dcw])
            P.op("act", lambda h: h.activation(out=lruc.t[:, 4:8], in_=cc(C_LAM, 4), func=AF.Exp, scale=-1.0), reads=[cst], writes=[lruc])
            P.op("act", lambda h: h.activation(out=lruc.t[:, 4:8], in_=lruc.t[:, 4:8], func=AF.Ln, bias=1.0), reads=[lruc], writes=[lruc])
            P.op("dve", lambda h: h.tensor_scalar(out=lruc.t[:, 0:4], in0=lruc.t[:, 4:8], scalar1=-8.0, scalar2=None, op0=ALU.mult), reads=[lruc], writes=[lruc])

            nbias = mk(st1, "nbias", [128, 8], F32)[0]
            P.op("dve", lambda h: h.tensor_scalar(out=nbias.t[:], in0=cst.t[:, C_LAB:C_LAB + 8], scalar1=-1.0, scalar2=None, op0=ALU.mult), reads=[cst], writes=[nbias])
            gC = [float(np.exp(128.0 * np.log1p(-np.exp2(-5.0 - hh)))) for hh in range(4)]

            def stageA(xd, tabd, c0, nch):
                uT = uT_r.next()
                tabs = []
                for c in range(nch):
                    xs = xs_r.next()
                    r0 = (c0 + c) * 128
                    ld("sp", xs, xs.t[:], xd[r0:r0 + 128, :])
                    tb = tab_r.next()
                    ld("sp", tb, tb.t[:], tabd[r0:r0 + 128, :])
                    tabs.append(tb)
                    xn = xn_r.next()
                    norm_transpose(xs, xn, uT, c * 128, C_G1, "lnexp")
                return uT, tabs

            def fm_proj(uT, col, N, pb):
                def f(h):
                    ins = None
                    for kc in range(8):
                        ins = h.matmul(pb.t[:, 0:N], lhsT=w_in_sb.t[:, kc, col:col + 128], rhs=uT.t[:, kc, 0:N],
                                       start=(kc == 0), stop=(kc == 7))
                    return ins
                P.op("pe", f, reads=[wib[col // 512], uT], writes=[pb])

            prevN = [W]

            def lru_prelude(N):
                xl = xl_r.next()
                xl_prev = xl_r.items[(xl_r.i) % 2]
                Np = prevN[0]
                P.op("pool", lambda h: h.tensor_copy(out=xl.t[:, :, 0:3], in_=xl_prev.t[:, :, Np:Np + 3]),
                     reads=[xl_prev], writes=[xl])
                prevN[0] = N
                return xl

            def lru_tile(uT, N, full, t, xl, B):
                xc, rr, ii, aa, a2, bb, hl, xcb, zsq = (B[n] for n in ("xc", "rr", "ii", "aa", "a2", "bb", "hl", "xcb", "zsq"))
                pb = pbank(True)
                fm_proj(uT, t * 128, N, pb)
                yield
                P.op("act", lambda h: h.activation(out=xl.t[:, t, 3:3 + N], in_=pb.t[:, 0:N], func=AF.Copy), reads=[pb], writes=[xl])
                prel(pb)
                yield
                pc = pbank(True)

                def fconv(h):
                    ins = None
                    for k in range(4):
                        ins = h.matmul(pc.t[:, 0:N], lhsT=dcw.t[:, k * 4 + t, :], rhs=xl.t[:, t, k:k + N], start=(k == 0), stop=(k == 3))
                    return ins
                P.op("pe", fconv, reads=[dcw, xl], writes=[pc])
                yield
                P.op("act", lambda h: h.activation(out=xc.t[:, 0:N], in_=pc.t[:, 0:N], func=AF.Identity, bias=cc(C_LCB + t)),
                     reads=[pc, cst], writes=[xc])
                prel(pc)
                P.op("pool", lambda h: h.tensor_copy(out=xcb.t[:, 0:N], in_=xc.t[:, 0:N]), reads=[xc], writes=[xcb])
                yield
                pa = pbank(True)
                P.op("pe", lambda h: h.matmul(pa.t[:, 0:N], lhsT=gwb.t[:, t, :], rhs=xcb.t[:, 0:N], start=True, stop=True), reads=[gwb, xcb], writes=[pa])
                yield
                P.op("act", lambda h: h.activation(out=rr.t[:, 0:N], in_=pa.t[:, 0:N], func=AF.Exp, bias=nbias.t[:, t:t + 1], scale=-1.0), reads=[pa, nbias], writes=[rr])
                prel(pa)
                px = pbank(True)
                P.op("pe", lambda h: h.matmul(px.t[:, 0:N], lhsT=gwb.t[:, 4 + t, :], rhs=xcb.t[:, 0:N], start=True, stop=True), reads=[gwb, xcb], writes=[px])
                yield
                P.op("act", lambda h: h.activation(out=rr.t[:, 0:N], in_=rr.t[:, 0:N], func=AF.Ln, bias=1.0), reads=[rr], writes=[rr])
                P.op("act", lambda h: h.activation(out=ii.t[:, 0:N], in_=px.t[:, 0:N], func=AF.Exp, bias=nbias.t[:, 4 + t:5 + t], scale=-1.0), reads=[px, nbias], writes=[ii])
                prel(px)
                yield
                P.op("act", lambda h: h.activation(out=rr.t[:, 0:N], in_=rr.t[:, 0:N], func=AF.Exp, scale=-1.0), reads=[rr], writes=[rr])
                P.op("act", lambda h: h.activation(out=ii.t[:, 0:N], in_=ii.t[:, 0:N], func=AF.Ln, bias=1.0), reads=[ii], writes=[ii])
                yield
                P.op("act", lambda h: h.activation(out=aa.t[:, 0:N], in_=rr.t[:, 0:N], func=AF.Exp, scale=lruc.t[:, t:t + 1]), reads=[rr, lruc], writes=[aa])
                P.op("act", lambda h: h.activation(out=ii.t[:, 0:N], in_=ii.t[:, 0:N], func=AF.Exp, scale=-1.0), reads=[ii], writes=[ii])
                yield
                P.op("pool", lambda h: h.tensor_tensor(out=a2.t[:, 0:N], in0=aa.t[:, 0:N], in1=aa.t[:, 0:N], op=ALU.mult), reads=[aa], writes=[a2])
                P.op("pool", lambda h: h.tensor_tensor(out=bb.t[:, 0:N], in0=ii.t[:, 0:N], in1=xc.t[:, 0:N], op=ALU.mult), reads=[ii, xc], writes=[bb])
                yield
                P.op("act", lambda h: h.activation(out=a2.t[:, 0:N], in_=a2.t[:, 0:N], func=AF.Ln, bias=1.0, scale=-1.0), reads=[a2], writes=[a2])
                yield
                P.op("act", lambda h: h.activation(out=a2.t[:, 0:N], in_=a2.t[:, 0:N], func=AF.Exp, scale=0.5), reads=[a2], writes=[a2])
                yield
                P.op("pool", lambda h: h.tensor_tensor(out=bb.t[:, 0:N], in0=bb.t[:, 0:N], in1=a2.t[:, 0:N], op=ALU.mult), reads=[a2, bb], writes=[bb])
                yield
                P.op("dve", lambda h: h.tensor_tensor_scan(out=hl.t[:, 0:N], data0=aa.t[:, 0:N], data1=bb.t[:, 0:N],
                                                           initial=hstate.t[:, t:t + 1], op0=ALU.mult, op1=ALU.add),
                     reads=[aa, bb, hstate], writes=[hl])
                yield
                P.op("dve", lambda h: h.tensor_copy(out=hstate.t[:, t:t + 1], in_=hl.t[:, N - 1:N]), reads=[hl], writes=[hstate])
                if full:
                    P.op("dve", lambda h: h.tensor_tensor(out=zz[t].t[:, 0:N], in0=hl.t[:, 0:N], in1=zz[t].t[:, 0:N], op=ALU.mult), reads=[hl, zz[t]], writes=[zz[t]])
                    yield
                    P.op("pool", lambda h: h.tensor_tensor(out=zsq.t[:, 0:N], in0=zz[t].t[:, 0:N], in1=zz[t].t[:, 0:N], op=ALU.mult), reads=[zz[t]], writes=[zsq])
                    yield
                    P.op("pe", lambda h: h.matmul(pss_bank.t[:, 0:N], lhsT=onesb.t[:], rhs=zsq.t[:, 0:N], start=(t == 0), stop=(t == 3)),
                         reads=[onesb, zsq], writes=[pss_bank])
                yield

            def gelu_batch(uT, N):
                for t in range(4):
                    pg = pbank(True)
                    fm_proj(uT, 512 + t * 128, N, pg)
                    yield
                    P.op("act", lambda h, t=t, pg=pg: h.activation(out=zz[t].t[:, 0:N], in_=pg.t[:, 0:N], func=AF.Gelu), reads=[pg], writes=[zz[t]])
                    prel(pg)

            def lru_final(N, mixT):
                P.op("act", lambda h: h.activation(out=rstd_t.t[:, 0:N], in_=pss_bank.t[:, 0:N], func=AF.Ln, bias=epsc.t[:, 0:1], scale=1.0 / 512),
                     reads=[pss_bank, epsc], writes=[rstd_t])
                P.op("act", lambda h: h.activation(out=rstd_t.t[:, 0:N], in_=rstd_t.t[:, 0:N], func=AF.Exp, scale=-0.5), reads=[rstd_t], writes=[rstd_t])
                for t in range(4):
                    P.op("dve", lambda h, t=t: h.scalar_tensor_tensor(out=mixT.t[:, t, 0:N], in0=zz[t].t[:, 0:N], scalar=cc(C_LNG + t),
                                                                      in1=rstd_t.t[:, 0:N], op0=ALU.mult, op1=ALU.mult),
                         reads=[zz[t], rstd_t, cst], writes=[mixT])

            def rotary(psrc, tb, dst_bf, tA, tB):
                p3 = psrc.t[:].rearrange("p (a b) -> p a b", a=4)
                Cb = tb.t[:, 0:128].unsqueeze(1).to_broadcast([128, 4, 128])
                S0 = tb.t[:, 128:192].unsqueeze(1).to_broadcast([128, 4, 64])
                S1 = tb.t[:, 192:256].unsqueeze(1).to_broadcast([128, 4, 64])
                P.op("dve", lambda h: h.tensor_tensor(out=tA.t[:], in0=p3, in1=Cb, op=ALU.mult), reads=[psrc, tb], writes=[tA])
                P.op("dve", lambda h: h.tensor_tensor(out=tB.t[:, :, 0:64], in0=p3[:, :, 64:128], in1=S0, op=ALU.mult), reads=[psrc, tb], writes=[tB])
                P.op("dve", lambda h: h.tensor_tensor(out=tB.t[:, :, 64:128], in0=p3[:, :, 0:64], in1=S1, op=ALU.mult), reads=[psrc, tb], writes=[tB])
                P.op("pool", lambda h: h.tensor_tensor(out=dst_bf.t[:], in0=tA.t[:], in1=tB.t[:], op=ALU.add), reads=[tA, tB], writes=[dst_bf])

            def tm_proj(uT, c, col, pb):
                def f(h):
                    ins = None
                    for kc in range(8):
                        ins = h.matmul(pb.t[:], lhsT=uT.t[:, kc, c * 128:(c + 1) * 128], rhs=w_in_sb.t[:, kc, col:col + 512],
                                       start=(kc == 0), stop=(kc == 7))
                    return ins
                P.op("pe", f, reads=[wib[col // 512], uT], writes=[pb])

            def heads4(pb, fn_h, reads):
                def f(h):
                    ins = None
                    for hh in range(4):
                        ins = fn_h(h, hh)
                    return ins
                P.op("pe", f, reads=reads, writes=[pb])

            gci = [0]

            def ret_chunk(uT, c, tb, full, mixT, B):
                g = gci[0]
                gci[0] += 1
                Sb_old = S_bfs[g % 4]
                Sb_new = S_bfs[(g + 1) % 4]
                tA, tB, osb, yn, sg, vbf, vz, qbf, kbf, qT, kT, sc, ybf, bst = (B[n] for n in
                    ("tA", "tB", "osb", "yn", "sg", "vbf", "vz", "qbf", "kbf", "qT", "kT", "sc", "ybf", "bst"))
                pk = pbank(True)
                tm_proj(uT, c, 1536, pk)
                yield
                rotary(pk, tb, kbf, tA, tB)
                prel(pk)
                pv = pbank(True)
                tm_proj(uT, c, 2048, pv)
                yield
                pv3 = pv.t[:].rearrange("p (a b) -> p a b", a=4)
                P.op("dve", lambda h: h.tensor_tensor(out=vz.t[:], in0=pv3, in1=cst.t[:, C_ZETA:C_ZETA + 4].unsqueeze(2).to_broadcast([128, 4, 128]), op=ALU.mult),
                     reads=[pv, cst], writes=[vz])
                if full:
                    P.op("act", lambda h: h.activation(out=vbf.t[:].rearrange("p a b -> p (a b)"), in_=pv.t[:], func=AF.Copy), reads=[pv], writes=[vbf])
                prel(pv)
                yield
                pkv = pbank(True)
                heads4(pkv, lambda h, hh: h.matmul(pkv.t[:, hh * 128:(hh + 1) * 128], lhsT=kbf.t[:, hh, :], rhs=vz.t[:, hh, :], start=True, stop=True), [kbf, vz])
                yield
                for hh in range(4):
                    P.op("dve", lambda h, hh=hh: h.scalar_tensor_tensor(out=Sst.t[:, hh, :], in0=Sst.t[:, hh, :], scalar=gC[hh],
                                                                        in1=pkv.t[:, hh * 128:(hh + 1) * 128], op0=ALU.mult, op1=ALU.add),
                         reads=[pkv, Sst], writes=[Sst])
                prel(pkv)
                if full:
                    pq = pbank(True)
                    tm_proj(uT, c, 1024, pq)
                P.op("pool", lambda h: h.tensor_copy(out=Sb_new.t[:].rearrange("p a b -> p (a b)"), in_=Sst.t[:].rearrange("p a b -> p (a b)")),
                     reads=[Sst], writes=[Sb_new])
                yield
                if not full:
                    return
                rotary(pq, tb, qbf, tA, tB)
                prel(pq)
                pg = pbank(True)
                tm_proj(uT, c, 2560, pg)
                yield
                P.op("act", lambda h: h.activation(out=sg.t[:], in_=pg.t[:], func=AF.Exp, scale=-1.0), reads=[pg], writes=[sg])
                P.op("act", lambda h: h.activation(out=sg.t[:], in_=sg.t[:], func=AF.Ln, bias=1.0), reads=[sg], writes=[sg])
                P.op("act", lambda h: h.activation(out=sg.t[:], in_=sg.t[:], func=AF.Exp, scale=-1.0), reads=[sg], writes=[sg])
                P.op("dve", lambda h: h.tensor_tensor(out=sg.t[:], in0=pg.t[:], in1=sg.t[:], op=ALU.mult), reads=[pg, sg], writes=[sg])
                prel(pg)
                pT2 = pbank(True)
                pT2v = pT2.t.bitcast(BF16)
                heads4(pT2, lambda h, hh: h.transpose(pT2v[:, hh * 128:(hh + 1) * 128], kbf.t[:, hh, :], identb.t[:]), [kbf, identb])
                yield
                P.op("act", lambda h: h.activation(out=kT.t[:].rearrange("p a b -> p (a b)"), in_=pT2v[:, 0:512], func=AF.Copy), reads=[pT2], writes=[kT])
                prel(pT2)
                pT = pbank(True)
                pTv = pT.t.bitcast(BF16)
                heads4(pT, lambda h, hh: h.transpose(pTv[:, hh * 128:(hh + 1) * 128], qbf.t[:, hh, :], identb.t[:]), [qbf, identb])
                yield
                P.op("act", lambda h: h.activation(out=qT.t[:].rearrange("p a b -> p (a b)"), in_=pTv[:, 0:512], func=AF.Copy), reads=[pT], writes=[qT])
                prel(pT)
                yield
                psc = pbank(True)
                heads4(psc, lambda h, hh: h.matmul(psc.t[:, hh * 128:(hh + 1) * 128], lhsT=kT.t[:, hh, :], rhs=qT.t[:, hh, :], start=True, stop=True), [kT, qT])
                yield
                P.op("dve", lambda h: h.tensor_tensor(out=sc.t[:].rearrange("p a b -> p (a b)"), in0=psc.t[:], in1=dmask.t[:].rearrange("p a b -> p (a b)"), op=ALU.mult),
                     reads=[psc, dmask], writes=[sc])
                prel(psc)
                pcx = pbank(True)
                heads4(pcx, lambda h, hh: h.matmul(pcx.t[:, hh * 128:(hh + 1) * 128], lhsT=qT.t[:, hh, :], rhs=Sb_old.t[:, hh, :], start=True, stop=True), [qT, Sb_old])
                yield
                for hh in range(4):
                    P.op("act", lambda h, hh=hh: h.activation(out=tA.t[:, hh, :], in_=pcx.t[:, hh * 128:(hh + 1) * 128], func=AF.Copy, scale=cc(C_XI + hh)),
                         reads=[pcx, cst], writes=[tA])
                prel(pcx)
                po = pbank(True)
                heads4(po, lambda h, hh: h.matmul(po.t[:, hh * 128:(hh + 1) * 128], lhsT=sc.t[:, hh, :], rhs=vbf.t[:, hh, :], start=True, stop=True), [sc, vbf])
                yield
                P.op("dve", lambda h: h.tensor_tensor(out=osb.t[:].rearrange("p a b -> p (a b)"), in0=po.t[:], in1=tA.t[:].rearrange("p a b -> p (a b)"), op=ALU.add),
                     reads=[po, tA], writes=[osb])
                prel(po)
                yield
                for hh in range(4):
                    P.op("dve", lambda h, hh=hh: h.bn_stats(out=bst.t[:, hh, 0:6], in_=osb.t[:, hh, :]), reads=[osb], writes=[bst])
                for hh in range(4):
                    P.op("dve", lambda h, hh=hh: h.bn_aggr(out=bst.t[:, hh, 6:8], in_=bst.t[:, hh, 0:6]), reads=[bst], writes=[bst])
                yield
                s = stat.next()
                P.op("act", lambda h: h.activation(out=s.t[:, 0:4], in_=bst.t[:, :, 7], func=AF.Ln, bias=epsc.t[:, 0:1]), reads=[bst, epsc], writes=[s])
                yield
                P.op("act", lambda h: h.activation(out=s.t[:, 0:4], in_=s.t[:, 0:4], func=AF.Exp, scale=-0.5), reads=[s], writes=[s])
                yield
                for hh in range(4):
                    P.op("dve", lambda h, hh=hh: h.tensor_scalar(out=yn.t[:, hh, :], in0=osb.t[:, hh, :], scalar1=bst.t[:, hh, 6:7], scalar2=s.t[:, hh:hh + 1],
                                                                 op0=ALU.subtract, op1=ALU.mult),
                         reads=[osb, bst, s], writes=[yn])
                yield
                P.op("pool", lambda h: h.tensor_tensor(out=ybf.t[:].rearrange("p a b -> p (a b)"), in0=yn.t[:].rearrange("p a b -> p (a b)"), in1=sg.t[:], op=ALU.mult),
                     reads=[yn, sg], writes=[ybf])
                yield
                pY = pbank(True)
                pYv = pY.t.bitcast(BF16)
                heads4(pY, lambda h, hh: h.transpose(pYv[:, hh * 128:(hh + 1) * 128], ybf.t[:, hh, :], identb.t[:]), [ybf, identb])
                yield
                P.op("dve", lambda h: h.tensor_tensor(out=mixT.t[:, 4:8, c * 128:(c + 1) * 128], in0=pYv[:, 0:512].rearrange("p (a b) -> p a b", a=4),
                                                      in1=cst.t[:, C_RNG:C_RNG + 4].unsqueeze(2).to_broadcast([128, 4, 128]), op=ALU.mult),
                     reads=[pY, cst], writes=[mixT])
                prel(pY)
                yield

            def outproj_chunk(mixT, c, gch):
                r0 = gch * 128
                hm = hm_r.next()
                ld("sp", hm, hm.t[:], xm[r0:r0 + 128, :])
                for half in range(2):
                    pb = pbank()

                    def f(h, half=half, pb=pb):
                        ins = None
                        for kc in range(8):
                            ins = h.matmul(pb.t[:], lhsT=mixT.t[:, kc, c * 128:(c + 1) * 128], rhs=w_out_sb.t[:, kc, half * 512:(half + 1) * 512],
                                           start=(kc == 0), stop=(kc == 7))
                        return ins
                    P.op("pe", f, reads=[mixT, w_out_sb], writes=[pb])
                    P.op("dve", lambda h, half=half, pb=pb: h.tensor_tensor(out=hm.t[:, half * 512:(half + 1) * 512], in0=pb.t[:], in1=hm.t[:, half * 512:(half + 1) * 512], op=ALU.add),
                         reads=[pb, hm], writes=[hm])
                hb = hbufs[gch]
                P.dma("sp", key_of(hm), lambda h: h.dma_start(out=hmid[r0:r0 + 128, :], in_=hm.t[:]), reads=[hm], writes=[hb])

            def run_lanes(lanes, delays=None):
                cur = [None] * len(lanes)
                idx = [0] * len(lanes)
                dl = list(delays) if delays else [0] * len(lanes)
                dl = dl + [0] * (len(lanes) - len(dl))
                active = True
                while active:
                    active = False
                    for li, lane in enumerate(lanes):
                        if dl[li] > 0:
                            dl[li] -= 1
                            active = True
                            continue
                        if cur[li] is None:
                            if idx[li] < len(lane):
                                cur[li] = lane[idx[li]]
                                idx[li] += 1
                            else:
                                continue
                        active = True
                        try:
                            next(cur[li])
                        except StopIteration:
                            cur[li] = None

            hbufs = [Buf("hmid%d" % i) for i in range(NM)]

            def tail_gen(prev, uT, N):
                if prev is not None:
                    lru_final(prev[0], prev[1])
                yield
                if uT is not None:
                    for _ in gelu_batch(uT, N):
                        yield
                    yield
                if prev is not None:
                    for c in range(prev[3]):
                        outproj_chunk(prev[1], c, prev[2] + c)
                        yield
                        yield

            def mixer(blocks, xd, tabd, full, flag_after_first):
                if not blocks:
                    return
                nxt = stageA(xd, tabd, *blocks[0])
                tail = None
                for bi, (c0, nch) in enumerate(blocks):
                    N = nch * 128
                    uT, tabs = nxt
                    if bi + 1 < len(blocks):
                        nxt = stageA(xd, tabd, *blocks[bi + 1])
                    mixT = mixT_r.next() if full else None
                    xl = lru_prelude(N)
                    if full:
                        lanes = [
                            [lru_tile(uT, N, full, 0, xl, LS[0]), lru_tile(uT, N, full, 2, xl, LS[0])],
                            [lru_tile(uT, N, full, 1, xl, LS[1]), lru_tile(uT, N, full, 3, xl, LS[1])],
                        ]
                        dls = [0, LDL]
                    else:
                        lanes = [[lru_tile(uT, N, full, 0, xl, LS[0])], [lru_tile(uT, N, full, 1, xl, LS[1])],
                                 [lru_tile(uT, N, full, 2, xl, LSP[0])], [lru_tile(uT, N, full, 3, xl, LSP[1])]]
                        dls = [0, 1, 2, 3]
                    lanes = lanes + [[ret_chunk(uT, c, tabs[c], full, mixT, RS[c])] for c in range(nch)]
                    dls = dls + [(RDL if full else 2) * c for c in range(nch)]
                    if full:
                        lanes.append([tail_gen(tail, uT, N)])
                    run_lanes(lanes, dls)
                    if bgq and bi >= 1:
                        bgq.pop(0)()
                    tail = (N, mixT, c0, nch) if full else None
                    if flag_after_first and bi == 0:
                        P.op("dve", lambda h: h.tensor_scalar(out=hstate.t[:], in0=hstate.t[:], scalar1=cc(C_FLAG), scalar2=None, op0=ALU.mult),
                             reads=[hstate, cst], writes=[hstate])
                if tail is not None:
                    run_lanes([[tail_gen(tail, None, 0)]])

            mixer(split_blocks(NP, BC) if STG >= 1 else [], xp, tabp, False, False)
            mixer(split_blocks(NM, BC, first=1) if STG >= 2 else [], xm, tabm, True, True)
            while bgq:
                bgq.pop(0)()
            P.barrier()
            P.emit_block()

        with ExitStack() as st2:
            NF = FBC * 128
            w_up_all = mk(st2, "w_up_all", [128, 8, 8, 768], BF16)[0]
            w_dn_all = mk(st2, "w_dn_all", [128, 24, D], BF16)[0]
            wup_b = [Buf("wupg%d" % g) for g in range(8)]
            wdn_b = [Buf("wdng%d" % g) for g in range(4)]
            gfin = mk(st2, "gfin", [128, D], F32)[0]
            tails = mk(st2, "tails", [128, 48, 2], F32)[0]
            tails_b = [Buf("tails%d" % j) for j in range(48)]
            hs_r = Rot(mk(st2, "hs", [128, D], F32, FBC))
            ha_r = Rot(mk(st2, "ha", [128, D], F32, 2))
            xn2_r = Rot(mk(st2, "xn2", [128, D], BF16, 1))
            u2T_r = Rot(mk(st2, "u2T", [128, 8, NF], BF16, 1))
            gT = [Rot(mk(st2, "gT%d" % m, [128, NF], BF16, 1)) for m in range(24)]
            ya_r = Rot(mk(st2, "ya", [128, NF], F32, 2))
            yv_r = Rot(mk(st2, "yv", [128, NF], F32, 2))
            U_r = Rot(mk(st2, "U", [128, NF + 2], F32, 3))

            for g in (0, 4, 1, 5, 2, 6, 3, 7):
                P.dma("sp", key_of(wup_b[g]), lambda h, g=g: h.dma_start(out=w_up_all.t[:, g, :, :].rearrange("p a b -> p (a b)"), in_=wup_bf[g].rearrange("p a b -> p (a b)")),
                      reads=[wupd_b], writes=[wup_b[g]])
            for i in range(4):
                P.dma("sp", key_of(wdn_b[i]), lambda h, i=i: h.dma_start(out=w_dn_all.t[:, i * 6:(i + 1) * 6, :].rearrange("p a b -> p (a b)"), in_=wdn_bf[:, i * 6:(i + 1) * 6, :].rearrange("p a b -> p (a b)")),
                      reads=[wdnd_b], writes=[wdn_b[i]])
            ld("sp", gfin, gfin.t[:], gfin_d)
            for j in range(48):
                P.op("pool", lambda h, j=j: h.memset(tails.t[:, j, :], 0.0), writes=[tails_b[j]])

            Uh_b = {}

            def conv_tile(pu, j, N, y, init_on_act):
                w0, w1, w2 = cc(C_FCW + j), cc(C_FCW + 48 + j), cc(C_FCW + 96 + j)
                U = U_r.next()
                Uh = Uh_b.setdefault(U.b.name, Buf(U.b.name + "_h"))
                P.op("pool", lambda h: h.tensor_copy(out=U.t[:, 0:2], in_=tails.t[:, j, :]), reads=[tails_b[j]], writes=[Uh])
                P.op("act", lambda h: h.activation(out=U.t[:, 2:2 + N], in_=pu.t[:, 0:N], func=AF.Copy), reads=[pu], writes=[U])
                if init_on_act:
                    P.op("act", lambda h: h.activation(out=y.t[:, 0:N], in_=pu.t[:, 0:N], func=AF.Identity, bias=cc(C_FCB + j), scale=w2),
                         reads=[pu, cst], writes=[y])
                else:
                    P.op("dve", lambda h: h.tensor_scalar(out=y.t[:, 0:N], in0=U.t[:, 2:2 + N], scalar1=w2, scalar2=cc(C_FCB + j), op0=ALU.mult, op1=ALU.add),
                         reads=[U, cst], writes=[y])
                P.op("dve", lambda h: h.scalar_tensor_tensor(out=y.t[:, 0:N], in0=U.t[:, 1:1 + N], scalar=w1, in1=y.t[:, 0:N], op0=ALU.mult, op1=ALU.add),
                     reads=[U, Uh, y, cst], writes=[y])
                P.op("dve", lambda h: h.scalar_tensor_tensor(out=y.t[:, 0:N], in0=U.t[:, 0:N], scalar=w0, in1=y.t[:, 0:N], op0=ALU.mult, op1=ALU.add),
                     reads=[U, Uh, y, cst], writes=[y])
                P.op("pool", lambda h: h.tensor_copy(out=tails.t[:, j, :], in_=U.t[:, N:N + 2]), reads=[U], writes=[tails_b[j]])

            last_out = None
            fblocks = split_blocks(NM, FBC) if STG >= 3 else []

            def stageA_F(c0, nch):
                u2T = u2T_r.next()
                for c in range(nch):
                    ha = ha_r.next()
                    r0 = (c0 + c) * 128
                    P.dma("sp", key_of(ha), lambda h, ha=ha, r0=r0: h.dma_start(out=ha.t[:], in_=hmid[r0:r0 + 128, :]), reads=[hbufs[c0 + c]], writes=[ha])
                    xn2 = xn2_r.next()
                    norm_transpose(ha, xn2, u2T, c * 128, C_G2, None)
                return u2T

            u2T_next = stageA_F(*fblocks[0]) if fblocks else None
            for bi, (c0, nch) in enumerate(fblocks):
                N = nch * 128
                u2T = u2T_next
                pend = None
                for m in range(24):
                    pa = pbank()
                    pv = pbank()
                    for (pb, j) in ((pa, m), (pv, 24 + m)):
                        def f(h, pb=pb, j=j, N=N, u2T=u2T):
                            ins = None
                            for kc in range(8):
                                ins = h.matmul(pb.t[:, 0:N], lhsT=w_up_all.t[:, j // 6, kc, (j % 6) * 128:(j % 6 + 1) * 128], rhs=u2T.t[:, kc, 0:N], start=(kc == 0), stop=(kc == 7))
                            return ins
                        P.op("pe", f, reads=[wup_b[j // 6], u2T], writes=[pb])
                    ya = ya_r.next()
                    yv = yv_r.next()
                    conv_tile(pa, m, N, ya, True)
                    conv_tile(pv, 24 + m, N, yv, True)
                    if pend is not None:
                        pend()

                    def pend(ya=ya, yv=yv, m=m, N=N):
                        P.op("act", lambda h: h.activation(out=ya.t[:, 0:N], in_=ya.t[:, 0:N], func=AF.Gelu), reads=[ya], writes=[ya])
                        g = gT[m].next()
                        P.op("pool", lambda h: h.tensor_tensor(out=g.t[:, 0:N], in0=ya.t[:, 0:N], in1=yv.t[:, 0:N], op=ALU.mult),
                             reads=[ya, yv], writes=[g])
                pend()
                hss = []
                for c in range(nch):
                    hs = hs_r.next()
                    r0 = (c0 + c) * 128
                    P.dma("sp", key_of(hs), lambda h, hs=hs, r0=r0: h.dma_start(out=hs.t[:], in_=hmid[r0:r0 + 128, :]), reads=[hbufs[c0 + c]], writes=[hs])
                    hss.append(hs)
                if bi + 1 < len(fblocks):
                    u2T_next = stageA_F(*fblocks[bi + 1])
                for c in range(nch):
                    hs = hss[c]
                    for half in range(2):
                        pb = pbank()

                        def f(h, half=half, pb=pb, c=c):
                            ins = None
                            for m in range(24):
                                ins = h.matmul(pb.t[:], lhsT=gT[m].items[0].t[:, c * 128:(c + 1) * 128], rhs=w_dn_all.t[:, m, half * 512:(half + 1) * 512],
                                               start=(m == 0), stop=(m == 23))
                            return ins
                        P.op("pe", f, reads=[gT[m].items[0] for m in range(24)] + wdn_b, writes=[pb])
                        P.op("dve", lambda h, half=half, pb=pb, hs=hs: h.tensor_tensor(out=hs.t[:, half * 512:(half + 1) * 512], in0=pb.t[:],
                                                                                       in1=hs.t[:, half * 512:(half + 1) * 512], op=ALU.add),
                             reads=[pb, hs], writes=[hs])
                    s = stat.next()
                    junk = xn2_r.next()
                    P.op("pool", lambda h, s=s: h.memset(s.t[:], 0.0), writes=[s])
                    P.op("act", lambda h, s=s, hs=hs, junk=junk: h.activation(out=junk.t[:], in_=hs.t[:], func=AF.Square, accum_out=s.t[:, 0:1]),
                         reads=[hs, s], writes=[junk, s])
                    P.op("act", lambda h, s=s: h.activation(out=s.t[:, 1:2], in_=s.t[:, 0:1], func=AF.Sqrt, bias=epsc.t[:, 0:1], scale=1.0 / D),
                         reads=[s, epsc], writes=[s])
                    P.op("dve", lambda h, s=s: h.reciprocal(out=s.t[:, 2:3], in_=s.t[:, 1:2]), reads=[s], writes=[s])
                    P.op("act", lambda h, s=s, hs=hs: h.activation(out=hs.t[:], in_=hs.t[:], func=AF.Copy, scale=s.t[:, 2:3]),
                         reads=[hs, s], writes=[hs])
                    P.op("pool", lambda h, hs=hs: h.tensor_tensor(out=hs.t[:], in0=hs.t[:], in1=gfin.t[:], op=ALU.mult),
                         reads=[hs, gfin], writes=[hs])
                    r0 = (c0 + c) * 128
                    last_out = P.dma("sp", key_of(hs), lambda h, hs=hs, r0=r0: h.dma_start(out=out[r0:r0 + 128, :], in_=hs.t[:]), reads=[hs])
            P.barrier()
            P.emit_block()
    return nc


_CACHE = {}


def _tables(pos):
    inv_freq = (np.float32(10000.0) ** (-np.arange(0, 128, 2, dtype=np.float32) / np.float32(128))).astype(np.float32)
    ang = (pos.astype(np.float32)[:, None] * inv_freq[None, :]).astype(np.float32)
    c = np.cos(ang).astype(np.float32)
    s = np.sin(ang).astype(np.float32)
    return np.ascontiguousarray(np.concatenate([c, c, -s, s], axis=1).astype(np.float32))


def _consts():
    log_g = np.log1p(-np.exp2(-5.0 - np.arange(4, dtype=np.float64)))
    idx = np.arange(128, dtype=np.float64)
    sc = 128.0 ** -0.5
    diff = idx[None, :] - idx[:, None]
    dm = np.where(diff[:, None, :] >= 0, np.exp(np.maximum(diff, 0.0)[:, None, :] * log_g[None, :, None]), 0.0) * sc
    xi = np.exp((idx + 1.0)[:, None] * log_g[None, :])
    zeta = np.exp((127.0 - idx)[:, None] * log_g[None, :]) * sc
    return dm.reshape(128, 512).astype(np.float32), xi.astype(np.float32), zeta.astype(np.float32)


def kernel(x, norm1_gain, w_in, lru_conv_w, lru_conv_b, lru_gate_a_w, lru_gate_a_b,
           lru_gate_x_w, lru_gate_x_b, lru_lambda, lru_norm_gain, ret_norm_gain, w_out,
           norm2_gain, ffn_up_w, ffn_conv_w, ffn_conv_b, ffn_down_w, final_norm_gain, _dbg=None):
    f = lambda a: np.ascontiguousarray(np.asarray(a, dtype=np.float32))
    x = f(x)
    B, S, _ = x.shape
    half = S // 2
    NM = half // 128 + 1
    NP = half // 128 - 1
    key = (NM, NP)
    if key not in _CACHE:
        _CACHE[key] = build(NM, NP)
    nc = _CACHE[key]

    dmask, xi, zeta = _consts()

    def pp(v, n):
        return f(v).reshape(n, 128).T
    cst = np.zeros((128, NCST), np.float32)
    cst[:, C_G1:C_G1 + 8] = pp(norm1_gain[0], 8)
    cst[:, C_G2:C_G2 + 8] = pp(norm2_gain[0], 8)
    for k in range(4):
        cst[:, C_LCW + k * 4:C_LCW + k * 4 + 4] = pp(lru_conv_w[0, k], 4)
    cst[:, C_LCB:C_LCB + 4] = pp(lru_conv_b[0], 4)
    cst[:, C_LAB:C_LAB + 4] = pp(lru_gate_a_b[0], 4)
    cst[:, C_LXB:C_LXB + 4] = pp(lru_gate_x_b[0], 4)
    cst[:, C_LAM:C_LAM + 4] = pp(lru_lambda[0], 4)
    cst[:, C_LNG:C_LNG + 4] = pp(lru_norm_gain[0], 4)
    cst[:, C_RNG:C_RNG + 4] = pp(ret_norm_gain[0], 4)
    for k in range(3):
        cst[:, C_FCW + k * 48:C_FCW + k * 48 + 48] = pp(ffn_conv_w[0, k], 48)
    cst[:, C_FCB:C_FCB + 48] = pp(ffn_conv_b[0], 48)
    cst[:, C_XI:C_XI + 4] = xi
    cst[:, C_ZETA:C_ZETA + 4] = zeta
    gw = np.zeros((128, 8, 128), np.float32)
    for j, wsrc in enumerate((f(lru_gate_a_w[0]), f(lru_gate_x_w[0]))):
        for t in range(4):
            gw[0:64, j * 4 + t, 0:64] = wsrc[2 * t]
            gw[64:128, j * 4 + t, 64:128] = wsrc[2 * t + 1]
    gfin = np.ascontiguousarray(np.broadcast_to(f(final_norm_gain)[None, :], (128, D)))
    ident = np.eye(128, dtype=np.float32)
    shared = {
        "gfin": gfin, "dmask": dmask, "gw": gw.reshape(128, 1024), "ident": ident,
        "w_in": f(w_in[0]), "w_out": f(w_out[0]), "w_up": f(ffn_up_w[0]), "w_dn": f(ffn_down_w[0]),
    }
    in_maps = []
    for b in range(B):
        for hh in range(2):
            c2 = cst.copy()
            c2[:, C_FLAG] = float(hh)
            if hh == 0:
                xmc = np.concatenate([np.zeros((128, D), np.float32), x[b, :half]], axis=0)
                xpc = np.zeros((max(NP, 1) * 128, D), np.float32)
                posm = np.concatenate([np.arange(128), np.arange(half)])
                posp = np.zeros(max(NP, 1) * 128)
            else:
                xmc = x[b, half - 128:]
                xpc = x[b, :half - 128] if NP > 0 else np.zeros((128, D), np.float32)
                posm = np.arange(half - 128, S)
                posp = np.arange(half - 128) if NP > 0 else np.zeros(128)
            m = dict(shared)
            m.update({"xm": np.ascontiguousarray(xmc), "xp": np.ascontiguousarray(xpc), "tabm": _tables(posm), "tabp": _tables(posp), "cst": c2})
            in_maps.append(m)
    res = run_bass_kernel_spmd(nc, in_maps, core_ids=list(range(B * 2)))
    if os.environ.get("KHM"):
        global _HM
        _HM = np.empty((B, S, D), np.float32)
        for b in range(B):
            for hh in range(2):
                _HM[b, hh * half:(hh + 1) * half] = res.results[b * 2 + hh]["hmid"][128:]
    outp = np.empty((B, S, D), np.float32)
    for b in range(B):
        for hh in range(2):
            o = res.results[b * 2 + hh]["out"]
            outp[b, hh * half:(hh + 1) * half] = o[128:]
    return outp
```

```python
import os
import numpy as np
from contextlib import ExitStack
import concourse.bass as bass
import concourse.mybir as mybir
from concourse.bass_utils import run_bass_kernel_spmd

F32 = mybir.dt.float32
BF16 = mybir.dt.bfloat16
AF = mybir.ActivationFunctionType
ALU = mybir.AluOpType

D = 1024
DIN = 3072
DFF = 3072
EPS = 1e-6
ENGS = ("pe", "act", "dve", "pool", "sp")

C_G1, C_G2, C_LCW, C_LCB, C_LAB, C_LXB, C_LAM, C_LNG, C_RNG = 0, 8, 16, 32, 36, 40, 44, 48, 52
C_FCW, C_FCB, C_XI, C_ZETA, C_FLAG, NCST = 56, 200, 248, 252, 256, 257


class Buf:
    __slots__ = ("name", "w", "r", "ps")

    def __init__(self, name):
        self.name = name
        self.w = None
        self.r = []
        self.ps = False


class T:
    __slots__ = ("t", "b")

    def __init__(self, t, name):
        self.t = t
        self.b = Buf(name)


class Prog:
    def __init__(self, nc):
        self.nc = nc
        self.ops = {e: [] for e in ENGS}
        self.cnt = {"E_" + e: 0 for e in ENGS}
        self.sems = {}
        self.waited = {e: {} for e in ENGS}
        self.dkeys = []

    def _need(self, eng, toks):
        out = []
        for (k, v, _e) in toks:
            if self.waited[eng].get(k, 0) >= v:
                continue
            self.waited[eng][k] = v
            out.append((k, v))
        return out

    def _deps(self, eng, reads, writes):
        best = {}

        def add(t):
            k = t[0]
            if k not in best or best[k][1] < t[1]:
                best[k] = t
        for b in reads:
            if b.w is not None:
                add(b.w)
            if b.ps:
                for t in b.r:
                    if t[2] != eng:
                        add(t)
        for b in writes:
            if b.w is not None and b.w[2] != eng:
                add(b.w)
            for t in b.r:
                if t[2] != eng:
                    add(t)
        return self._need(eng, best.values())

    def _commit(self, tok, reads, writes):
        for b in reads:
            b.r.append(tok)
        for b in writes:
            b.w = tok
            b.r = []

    def op(self, eng, fn, reads=(), writes=()):
        reads = [x.b if hasattr(x, "b") else x for x in reads]
        writes = [x.b if hasattr(x, "b") else x for x in writes]
        waits = self._deps(eng, reads, writes)
        k = "E_" + eng
        self.cnt[k] += 1
        self.ops[eng].append((waits, fn, (k, 1)))
        tok = (k, self.cnt[k], eng)
        self._commit(tok, reads, writes)
        return tok

    def dma(self, eng, key, fn, reads=(), writes=()):
        reads = [x.b if isinstance(x, T) else x for x in reads]
        writes = [x.b if isinstance(x, T) else x for x in writes]
        waits = self._deps(eng, reads, writes)
        assert key in self.cnt, key
        self.cnt[key] += 16
        self.ops[eng].append((waits, fn, (key, 16)))
        tok = (key, self.cnt[key], "dma:" + key)
        self._commit(tok, reads, writes)
        return tok

    def wait_tok(self, eng, tok):
        w = self._need(eng, [tok])
        if w:
            self.ops[eng].append((w, None, None))

    def barrier(self):
        for e in ENGS:
            toks = [(k, v, "x") for k, v in self.cnt.items() if v > 0]
            w = self._need(e, toks)
            if w:
                self.ops[e].append((w, None, None))

    def emit_block(self):
        nc = self.nc
        sems = self.sems
        with nc.Block() as block:
            def runner(eng):
                ops = self.ops[eng]

                def _run(h):
                    for (waits, fn, inc) in ops:
                        for (k, v) in waits:
                            h.wait_ge(sems[k], v)
                        if fn is not None:
                            ins = fn(h)
                            ins.then_inc(sems[inc[0]], inc[1])
                return _run
            block.tensor(runner("pe"))
            block.scalar(runner("act"))
            block.vector(runner("dve"))
            block.gpsimd(runner("pool"))
            block.sync(runner("sp"))
        self.ops = {e: [] for e in ENGS}


class Rot:
    def __init__(self, items):
        self.items = items
        self.i = 0

    def next(self):
        x = self.items[self.i % len(self.items)]
        self.i += 1
        return x


def split_blocks(n, bc, first=None):
    out = []
    s = 0
    if first:
        out.append((0, first))
        s = first
    while s < n:
        m = min(bc, n - s)
        out.append((s, m))
        s += m
    return out


def build(NM, NP, BC=3, FBC=int(os.environ.get('KFBC', '3')), dbg=False):
    nc = bass.Bass("TRN2", target_bir_lowering=False)

    def din(name, shape):
        return nc.dram_tensor(name, shape, F32, kind="ExternalInput").ap()
    xm = din("xm", [NM * 128, D])
    xp = din("xp", [max(NP, 1) * 128, D])
    tabm = din("tabm", [NM * 128, 256])
    tabp = din("tabp", [max(NP, 1) * 128, 256])
    cst_d = din("cst", [128, NCST])
    gfin_d = din("gfin", [128, D])
    dmask_d = din("dmask", [128, 512])
    gw_d = din("gw", [128, 8 * 128])
    ident_d = din("ident", [128, 128])
    w_in = din("w_in", [D, DIN])
    w_out = din("w_out", [D, D])
    w_up = din("w_up", [D, 2 * DFF])
    w_dn = din("w_dn", [DFF, D])
    out = nc.dram_tensor("out", [NM * 128, D], F32, kind="ExternalOutput").ap()
    if os.environ.get("KHM"):
        hmid = nc.dram_tensor("hmid", [NM * 128, D], F32, kind="ExternalOutput").ap()
    else:
        hmid = nc.dram_tensor("hmid", [NM * 128, D], F32).ap()

    wup_bf = nc.dram_tensor("wup_bf", [8, 128, 8, 768], BF16).ap()
    wdn_bf = nc.dram_tensor("wdn_bf", [128, 24, D], BF16).ap()
    P = Prog(nc)
    STG = int(os.environ.get('KDBG', '9'))
    MSK = int(os.environ.get('KMSK', '7'))
    SUB = int(os.environ.get('KSUB', '9'))
    LDL = int(os.environ.get('KLDL', '3'))
    RDL = int(os.environ.get('KRDL', '2'))
    NBLK = 0
    keyctr = [0]

    with ExitStack() as st0:
        def newkey(name):
            k = "D_%s_%d" % (name, keyctr[0])
            keyctr[0] += 1
            P.cnt[k] = 0
            P.dkeys.append(k)
            return k

        def mk(st, name, shape, dt, n=1):
            res = []
            for i in range(n):
                nm = "%s_%d" % (name, i)
                res.append(T(st.enter_context(nc.sbuf_tensor(nm, shape, dt)), nm))
            return res

        NKEYS = 64
        keypool = [newkey("k") for _ in range(NKEYS)]
        keyuse = {}

        def key_of(buf):
            b = buf.b if isinstance(buf, T) else buf
            if b.name not in keyuse:
                keyuse[b.name] = keypool.pop()
            return keyuse[b.name]

        for k in P.cnt:
            P.sems[k] = st0.enter_context(nc.semaphore(k))

        banks = [T(st0.enter_context(nc.psum_tensor("pb%d" % i, [128, 512], F32)), "pb%d" % i) for i in range(8)]
        for bk in banks:
            bk.b.ps = True
        bankrot = Rot(banks[:7])
        pss_bank = banks[7]

        bank_busy = {}

        def pbank(hold=False):
            for _ in range(len(bankrot.items)):
                bk = bankrot.next()
                if not bank_busy.get(bk.b.name):
                    if hold:
                        bank_busy[bk.b.name] = True
                    return bk
            raise RuntimeError("no free PSUM bank")

        def prel(bk):
            bank_busy[bk.b.name] = False

        cst = mk(st0, "cst", [128, NCST], F32)[0]
        identb = mk(st0, "identb", [128, 128], BF16)[0]
        epsc = mk(st0, "epsc", [128, 1], F32)[0]

        def ld(eng, dst, dst_ap, src_ap, reads=()):
            return P.dma(eng, key_of(dst), lambda h: h.dma_start(out=dst_ap, in_=src_ap), reads=reads, writes=[dst])

        ld("sp", cst, cst.t[:], cst_d)
        ld("pool", identb, identb.t[:], ident_d)
        P.op("pool", lambda h: h.memset(epsc.t[:], EPS), writes=[epsc])

        def cc(col, n=1):
            return cst.t[:, col:col + n]

        stat = Rot(mk(st0, "stat", [128, 4], F32, 12))

        def norm_transpose(xsrc, xn_t, uT_t, ucol, gcol, ntok_cols):
            s = stat.next()
            P.op("pool", lambda h: h.memset(s.t[:], 0.0), writes=[s])
            P.op("act", lambda h: h.activation(out=xn_t.t[:], in_=xsrc.t[:], func=AF.Square, accum_out=s.t[:, 0:1]),
                 reads=[xsrc, s], writes=[xn_t, s])
            if ntok_cols == "lnexp":
                P.op("act", lambda h: h.activation(out=s.t[:, 1:2], in_=s.t[:, 0:1], func=AF.Ln, bias=epsc.t[:, 0:1], scale=1.0 / D),
                     reads=[s, epsc], writes=[s])
                P.op("act", lambda h: h.activation(out=s.t[:, 2:3], in_=s.t[:, 1:2], func=AF.Exp, scale=-0.5), reads=[s], writes=[s])
            else:
                P.op("act", lambda h: h.activation(out=s.t[:, 1:2], in_=s.t[:, 0:1], func=AF.Sqrt, bias=epsc.t[:, 0:1], scale=1.0 / D),
                     reads=[s, epsc], writes=[s])
                P.op("dve", lambda h: h.reciprocal(out=s.t[:, 2:3], in_=s.t[:, 1:2]), reads=[s], writes=[s])
            P.op("act", lambda h: h.activation(out=xn_t.t[:], in_=xsrc.t[:], func=AF.Copy, scale=s.t[:, 2:3]),
                 reads=[xsrc, s], writes=[xn_t])
            pb = pbank()
            pbv = pb.t.bitcast(BF16)

            def tr(h):
                ins = None
                for kc in range(8):
                    ins = h.transpose(pbv[:, kc * 128:(kc + 1) * 128], xn_t.t[:, kc * 128:(kc + 1) * 128], identb.t[:])
                return ins
            P.op("pe", tr, reads=[xn_t, identb], writes=[pb])
            P.op("dve", lambda h: h.tensor_tensor(
                out=uT_t.t[:, :, ucol:ucol + 128],
                in0=pbv[:, 0:1024].rearrange("p (k t) -> p k t", k=8),
                in1=cst.t[:, gcol:gcol + 8].unsqueeze(2).to_broadcast([128, 8, 128]), op=ALU.mult),
                reads=[pb, cst], writes=[uT_t])
            return s

        with ExitStack() as st1:
            W = BC * 128
            w_in_sb = mk(st1, "w_in_sb", [128, 8, DIN], BF16)[0]
            w_out_sb = mk(st1, "w_out_sb", [128, 8, D], BF16)[0]
            gwb = mk(st1, "gwb", [128, 8, 128], BF16)[0]
            dcw = mk(st1, "dcw", [128, 16, 128], BF16)[0]
            onesb = mk(st1, "onesb", [128, 128], BF16)[0]
            dmask = mk(st1, "dmask", [128, 4, 128], F32)[0]
            lruc = mk(st1, "lruc", [128, 8], F32)[0]
            hstate = mk(st1, "hstate", [128, 4], F32)[0]
            Sst = mk(st1, "Sst", [128, 4, 128], F32)[0]
            S_bfs = mk(st1, "S_bf", [128, 4, 128], BF16, 4)
            xs_r = Rot(mk(st1, "xs", [128, D], F32, 2))
            hm_r = Rot(mk(st1, "hm", [128, D], F32, 2))
            tab_r = Rot(mk(st1, "tab", [128, 256], F32, 6))
            xn_r = Rot(mk(st1, "xn", [128, D], BF16, 2))
            uT_r = Rot(mk(st1, "uT", [128, 8, W], BF16, 2))
            xl_r = Rot(mk(st1, "xl", [128, 4, 3 + W], BF16, 2))
            mixT_r = Rot(mk(st1, "mixT", [128, 8, W], BF16, 2))
            LS = []
            for i in range(2):
                d = {n: mk(st1, "%s%d" % (n, i), [128, W], F32)[0] for n in ("xc", "rr", "ii", "aa", "bb")}
                d["a2"], d["hl"], d["gg"] = d["rr"], d["ii"], d["xc"]
                d["xcb"] = mk(st1, "xcb%d" % i, [128, W], BF16)[0]
                d["zsq"] = mk(st1, "zsq%d" % i, [128, W], BF16)[0]
                LS.append(d)
            rstd_t = mk(st1, "rstd", [128, W], F32)[0]
            zz = mk(st1, "zz", [128, W], F32, 4)
            RS = []
            for i in range(3):
                d = {}
                for n in ("tA", "tB", "osb"):
                    d[n] = mk(st1, "%s%d" % (n, i), [128, 4, 128], F32)[0]
                d["sg"] = mk(st1, "sg%d" % i, [128, 512], F32)[0]
                for n in ("vbf", "vz", "qbf", "kbf", "qT", "kT", "sc"):
                    d[n] = mk(st1, "%s%d" % (n, i), [128, 4, 128], BF16)[0]
                d["yn"], d["ybf"] = d["tB"], d["qbf"]
                d["bst"] = mk(st1, "bst%d" % i, [128, 4, 8], F32)[0]
                RS.append(d)

            class V:
                def __init__(self, t_obj, flat):
                    self.t = t_obj.t[:].rearrange("p a b -> p (a b)") if flat else t_obj.t
                    self.b = t_obj.b

            LSP = [
                {"xc": V(zz[0], False), "rr": V(zz[1], False), "ii": V(zz[2], False), "aa": V(zz[3], False),
                 "bb": V(RS[0]["osb"], True), "xcb": V(RS[0]["qT"], True), "zsq": None},
                {"xc": V(RS[0]["sg"], False), "rr": V(RS[1]["sg"], False), "ii": V(RS[2]["sg"], False), "aa": V(RS[1]["osb"], True),
                 "bb": V(RS[2]["osb"], True), "xcb": V(RS[1]["qT"], True), "zsq": None},
            ]
            for d in LSP:
                d["a2"], d["hl"], d["gg"] = d["rr"], d["ii"], d["xc"]

            w_in_v = w_in.rearrange("(kc p) n -> p kc n", p=128)
            wib = [w_in_sb.b] * 6
            for kc in range(8):
                ld("pool", w_in_sb, w_in_sb.t[:, kc, :], w_in_v[:, kc, :])
            ld("pool", gwb, gwb.t[:].rearrange("p a b -> p (a b)"), gw_d)
            ld("sp", dmask, dmask.t[:].rearrange("p a b -> p (a b)"), dmask_d)
            w_out_v = w_out.rearrange("(kc p) n -> p kc n", p=128)
            for kc in range(8):
                ld("pool", w_out_sb, w_out_sb.t[:, kc, :], w_out_v[:, kc, :])
            wupd_b = Buf("wupd")
            wdnd_b = Buf("wdnd")
            bgq = []
            w_up_g = w_up.rearrange("(kc p) (g c) -> g p kc c", p=128, c=768)
            for g in (0, 4, 1, 5, 2, 6, 3, 7):
                bgq.append(lambda g=g: P.dma("pool", key_of(wupd_b), lambda h: h.dma_start(out=wup_bf[g], in_=w_up_g[g]), writes=[wupd_b]))
            w_dn_g = w_dn.rearrange("(m p) n -> p m n", p=128)
            for i in range(0, 24, 6):
                bgq.append(lambda i=i: P.dma("pool", key_of(wdnd_b), lambda h: h.dma_start(out=wdn_bf[:, i:i + 6, :], in_=w_dn_g[:, i:i + 6, :]), writes=[wdnd_b]))
            P.op("pool", lambda h: h.memset(onesb.t[:], 1.0), writes=[onesb])
            P.op("pool", lambda h: h.memset(hstate.t[:], 0.0), writes=[hstate])
            P.op("pool", lambda h: h.memset(Sst.t[:].rearrange("p a b -> p (a b)"), 0.0), writes=[Sst])
            for sb_ in S_bfs:
                P.op("pool", lambda h, sb_=sb_: h.memset(sb_.t[:].rearrange("p a b -> p (a b)"), 0.0), writes=[sb_])
            for xl in xl_r.items:
                P.op("pool", lambda h, xl=xl: h.memset(xl.t[:].rearrange("p a b -> p (a b)"), 0.0), writes=[xl])
            for j in range(16):
                P.op("dve", lambda h, j=j: h.tensor_scalar(out=dcw.t[:, j, :], in0=identb.t[:], scalar1=cc(C_LCW + j), scalar2=None, op0=ALU.mult),
                     reads=[identb, cst], writes=[dcw])
            P.op("act", lambda h: h.activation(out=lruc.t[:, 4:8], in_=cc(C_LAM, 4), func=AF.Exp, scale=-1.0), reads=[cst], writes=[lruc])
            P.op("act", lambda h: h.activation(out=lruc.t[:, 4:8], in_=lruc.t[:, 4:8], func=AF.Ln, bias=1.0), reads=[lruc], writes=[lruc])
            P.op("dve", lambda h: h.tensor_scalar(out=lruc.t[:, 0:4], in0=lruc.t[:, 4:8], scalar1=-8.0, scalar2=None, op0=ALU.mult), reads=[lruc], writes=[lruc])

            nbias = mk(st1, "nbias", [128, 8], F32)[0]
            P.op("dve", lambda h: h.tensor_scalar(out=nbias.t[:], in0=cst.t[:, C_LAB:C_LAB + 8], scalar1=-1.0, scalar2=None, op0=ALU.mult), reads=[cst], writes=[nbias])
            gC = [float(np.exp(128.0 * np.log1p(-np.exp2(-5.0 - hh)))) for hh in range(4)]

            def stageA(xd, tabd, c0, nch):
                uT = uT_r.next()
                tabs = []
                for c in range(nch):
                    xs = xs_r.next()
                    r0 = (c0 + c) * 128
                    ld("sp", xs, xs.t[:], xd[r0:r0 + 128, :])
                    tb = tab_r.next()
                    ld("sp", tb, tb.t[:], tabd[r0:r0 + 128, :])
                    tabs.append(tb)
                    xn = xn_r.next()
                    norm_transpose(xs, xn, uT, c * 128, C_G1, "lnexp")
                return uT, tabs

            def fm_proj(uT, col, N, pb):
                def f(h):
                    ins = None
                    for kc in range(8):
                        ins = h.matmul(pb.t[:, 0:N], lhsT=w_in_sb.t[:, kc, col:col + 128], rhs=uT.t[:, kc, 0:N],
                                       start=(kc == 0), stop=(kc == 7))
                    return ins
                P.op("pe", f, reads=[wib[col // 512], uT], writes=[pb])

            prevN = [W]

            def lru_prelude(N):
                xl = xl_r.next()
                xl_prev = xl_r.items[(xl_r.i) % 2]
                Np = prevN[0]
                P.op("pool", lambda h: h.tensor_copy(out=xl.t[:, :, 0:3], in_=xl_prev.t[:, :, Np:Np + 3]),
                     reads=[xl_prev], writes=[xl])
                prevN[0] = N
                return xl

            def lru_tile(uT, N, full, t, xl, B):
                xc, rr, ii, aa, a2, bb, hl, xcb, zsq = (B[n] for n in ("xc", "rr", "ii", "aa", "a2", "bb", "hl", "xcb", "zsq"))
                pb = pbank(True)
                fm_proj(uT, t * 128, N, pb)
                yield
                P.op("act", lambda h: h.activation(out=xl.t[:, t, 3:3 + N], in_=pb.t[:, 0:N], func=AF.Copy), reads=[pb], writes=[xl])
                prel(pb)
                yield
                pc = pbank(True)

                def fconv(h):
                    ins = None
                    for k in range(4):
                        ins = h.matmul(pc.t[:, 0:N], lhsT=dcw.t[:, k * 4 + t, :], rhs=xl.t[:, t, k:k + N], start=(k == 0), stop=(k == 3))
                    return ins
                P.op("pe", fconv, reads=[dcw, xl], writes=[pc])
                yield
                P.op("act", lambda h: h.activation(out=xc.t[:, 0:N], in_=pc.t[:, 0:N], func=AF.Identity, bias=cc(C_LCB + t)),
                     reads=[pc, cst], writes=[xc])
                prel(pc)
                P.op("pool", lambda h: h.tensor_copy(out=xcb.t[:, 0:N], in_=xc.t[:, 0:N]), reads=[xc], writes=[xcb])
                yield
                pa = pbank(True)
                P.op("pe", lambda h: h.matmul(pa.t[:, 0:N], lhsT=gwb.t[:, t, :], rhs=xcb.t[:, 0:N], start=True, stop=True), reads=[gwb, xcb], writes=[pa])
                yield
                P.op("act", lambda h: h.activation(out=rr.t[:, 0:N], in_=pa.t[:, 0:N], func=AF.Exp, bias=nbias.t[:, t:t + 1], scale=-1.0), reads=[pa, nbias], writes=[rr])
                prel(pa)
                px = pbank(True)
                P.op("pe", lambda h: h.matmul(px.t[:, 0:N], lhsT=gwb.t[:, 4 + t, :], rhs=xcb.t[:, 0:N], start=True, stop=True), reads=[gwb, xcb], writes=[px])
                yield
                P.op("act", lambda h: h.activation(out=rr.t[:, 0:N], in_=rr.t[:, 0:N], func=AF.Ln, bias=1.0), reads=[rr], writes=[rr])
                P.op("act", lambda h: h.activation(out=ii.t[:, 0:N], in_=px.t[:, 0:N], func=AF.Exp, bias=nbias.t[:, 4 + t:5 + t], scale=-1.0), reads=[px, nbias], writes=[ii])
                prel(px)
                yield
                P.op("act", lambda h: h.activation(out=rr.t[:, 0:N], in_=rr.t[:, 0:N], func=AF.Exp, scale=-1.0), reads=[rr], writes=[rr])
                P.op("act", lambda h: h.activation(out=ii.t[:, 0:N], in_=ii.t[:, 0:N], func=AF.Ln, bias=1.0), reads=[ii], writes=[ii])
                yield
                P.op("act", lambda h: h.activation(out=aa.t[:, 0:N], in_=rr.t[:, 0:N], func=AF.Exp, scale=lruc.t[:, t:t + 1]), reads=[rr, lruc], writes=[aa])
                P.op("act", lambda h: h.activation(out=ii.t[:, 0:N], in_=ii.t[:, 0:N], func=AF.Exp, scale=-1.0), reads=[ii], writes=[ii])
                yield
                P.op("pool", lambda h: h.tensor_tensor(out=a2.t[:, 0:N], in0=aa.t[:, 0:N], in1=aa.t[:, 0:N], op=ALU.mult), reads=[aa], writes=[a2])
                P.op("pool", lambda h: h.tensor_tensor(out=bb.t[:, 0:N], in0=ii.t[:, 0:N], in1=xc.t[:, 0:N], op=ALU.mult), reads=[ii, xc], writes=[bb])
                yield
                P.op("act", lambda h: h.activation(out=a2.t[:, 0:N], in_=a2.t[:, 0:N], func=AF.Ln, bias=1.0, scale=-1.0), reads=[a2], writes=[a2])
                yield
                P.op("act", lambda h: h.activation(out=a2.t[:, 0:N], in_=a2.t[:, 0:N], func=AF.Exp, scale=0.5), reads=[a2], writes=[a2])
                yield
                P.op("pool", lambda h: h.tensor_tensor(out=bb.t[:, 0:N], in0=bb.t[:, 0:N], in1=a2.t[:, 0:N], op=ALU.mult), reads=[a2, bb], writes=[bb])
                yield
                P.op("dve", lambda h: h.tensor_tensor_scan(out=hl.t[:, 0:N], data0=aa.t[:, 0:N], data1=bb.t[:, 0:N],
                                                           initial=hstate.t[:, t:t + 1], op0=ALU.mult, op1=ALU.add),
                     reads=[aa, bb, hstate], writes=[hl])
                yield
                P.op("dve", lambda h: h.tensor_copy(out=hstate.t[:, t:t + 1], in_=hl.t[:, N - 1:N]), reads=[hl], writes=[hstate])
                if full:
                    P.op("dve", lambda h: h.tensor_tensor(out=zz[t].t[:, 0:N], in0=hl.t[:, 0:N], in1=zz[t].t[:, 0:N], op=ALU.mult), reads=[hl, zz[t]], writes=[zz[t]])
                    yield
                    P.op("pool", lambda h: h.tensor_tensor(out=zsq.t[:, 0:N], in0=zz[t].t[:, 0:N], in1=zz[t].t[:, 0:N], op=ALU.mult), reads=[zz[t]], writes=[zsq])
                    yield
                    P.op("pe", lambda h: h.matmul(pss_bank.t[:, 0:N], lhsT=onesb.t[:], rhs=zsq.t[:, 0:N], start=(t == 0), stop=(t == 3)),
                         reads=[onesb, zsq], writes=[pss_bank])
                yield

            def gelu_batch(uT, N):
                for t in range(4):
                    pg = pbank(True)
                    fm_proj(uT, 512 + t * 128, N, pg)
                    yield
                    P.op("act", lambda h, t=t, pg=pg: h.activation(out=zz[t].t[:, 0:N], in_=pg.t[:, 0:N], func=AF.Gelu), reads=[pg], writes=[zz[t]])
                    prel(pg)

            def lru_final(N, mixT):
                P.op("act", lambda h: h.activation(out=rstd_t.t[:, 0:N], in_=pss_bank.t[:, 0:N], func=AF.Ln, bias=epsc.t[:, 0:1], scale=1.0 / 512),
                     reads=[pss_bank, epsc], writes=[rstd_t])
                P.op("act", lambda h: h.activation(out=rstd_t.t[:, 0:N], in_=rstd_t.t[:, 0:N], func=AF.Exp, scale=-0.5), reads=[rstd_t], writes=[rstd_t])
                for t in range(4):
                    P.op("dve", lambda h, t=t: h.scalar_tensor_tensor(out=mixT.t[:, t, 0:N], in0=zz[t].t[:, 0:N], scalar=cc(C_LNG + t),
                                                                      in1=rstd_t.t[:, 0:N], op0=ALU.mult, op1=ALU.mult),
                         reads=[zz[t], rstd_t, cst], writes=[mixT])

            def rotary(psrc, tb, dst_bf, tA, tB):
                p3 = psrc.t[:].rearrange("p (a b) -> p a b", a=4)
                Cb = tb.t[:, 0:128].unsqueeze(1).to_broadcast([128, 4, 128])
                S0 = tb.t[:, 128:192].unsqueeze(1).to_broadcast([128, 4, 64])
                S1 = tb.t[:, 192:256].unsqueeze(1).to_broadcast([128, 4, 64])
                P.op("dve", lambda h: h.tensor_tensor(out=tA.t[:], in0=p3, in1=Cb, op=ALU.mult), reads=[psrc, tb], writes=[tA])
                P.op("dve", lambda h: h.tensor_tensor(out=tB.t[:, :, 0:64], in0=p3[:, :, 64:128], in1=S0, op=ALU.mult), reads=[psrc, tb], writes=[tB])
                P.op("dve", lambda h: h.tensor_tensor(out=tB.t[:, :, 64:128], in0=p3[:, :, 0:64], in1=S1, op=ALU.mult), reads=[psrc, tb], writes=[tB])
                P.op("pool", lambda h: h.tensor_tensor(out=dst_bf.t[:], in0=tA.t[:], in1=tB.t[:], op=ALU.add), reads=[tA, tB], writes=[dst_bf])

            def tm_proj(uT, c, col, pb):
                def f(h):
                    ins = None
                    for kc in range(8):
                        ins = h.matmul(pb.t[:], lhsT=uT.t[:, kc, c * 128:(c + 1) * 128], rhs=w_in_sb.t[:, kc, col:col + 512],
                                       start=(kc == 0), stop=(kc == 7))
                    return ins
                P.op("pe", f, reads=[wib[col // 512], uT], writes=[pb])

            def heads4(pb, fn_h, reads):
                def f(h):
                    ins = None
                    for hh in range(4):
                        ins = fn_h(h, hh)
                    return ins
                P.op("pe", f, reads=reads, writes=[pb])

            gci = [0]

            def ret_chunk(uT, c, tb, full, mixT, B):
                g = gci[0]
                gci[0] += 1
                Sb_old = S_bfs[g % 4]
                Sb_new = S_bfs[(g + 1) % 4]
                tA, tB, osb, yn, sg, vbf, vz, qbf, kbf, qT, kT, sc, ybf, bst = (B[n] for n in
                    ("tA", "tB", "osb", "yn", "sg", "vbf", "vz", "qbf", "kbf", "qT", "kT", "sc", "ybf", "bst"))
                pk = pbank(True)
                tm_proj(uT, c, 1536, pk)
                yield
                rotary(pk, tb, kbf, tA, tB)
                prel(pk)
                pv = pbank(True)
                tm_proj(uT, c, 2048, pv)
                yield
                pv3 = pv.t[:].rearrange("p (a b) -> p a b", a=4)
                P.op("dve", lambda h: h.tensor_tensor(out=vz.t[:], in0=pv3, in1=cst.t[:, C_ZETA:C_ZETA + 4].unsqueeze(2).to_broadcast([128, 4, 128]), op=ALU.mult),
                     reads=[pv, cst], writes=[vz])
                if full:
                    P.op("act", lambda h: h.activation(out=vbf.t[:].rearrange("p a b -> p (a b)"), in_=pv.t[:], func=AF.Copy), reads=[pv], writes=[vbf])
                prel(pv)
                yield
                pkv = pbank(True)
                heads4(pkv, lambda h, hh: h.matmul(pkv.t[:, hh * 128:(hh + 1) * 128], lhsT=kbf.t[:, hh, :], rhs=vz.t[:, hh, :], start=True, stop=True), [kbf, vz])
                yield
                for hh in range(4):
                    P.op("dve", lambda h, hh=hh: h.scalar_tensor_tensor(out=Sst.t[:, hh, :], in0=Sst.t[:, hh, :], scalar=gC[hh],
                                                                        in1=pkv.t[:, hh * 128:(hh + 1) * 128], op0=ALU.mult, op1=ALU.add),
                         reads=[pkv, Sst], writes=[Sst])
                prel(pkv)
                if full:
                    pq = pbank(True)
                    tm_proj(uT, c, 1024, pq)
                P.op("pool", lambda h: h.tensor_copy(out=Sb_new.t[:].rearrange("p a b -> p (a b)"), in_=Sst.t[:].rearrange("p a b -> p (a b)")),
                     reads=[Sst], writes=[Sb_new])
                yield
                if not full:
                    return
                rotary(pq, tb, qbf, tA, tB)
                prel(pq)
                pg = pbank(True)
                tm_proj(uT, c, 2560, pg)
                yield
                P.op("act", lambda h: h.activation(out=sg.t[:], in_=pg.t[:], func=AF.Exp, scale=-1.0), reads=[pg], writes=[sg])
                P.op("act", lambda h: h.activation(out=sg.t[:], in_=sg.t[:], func=AF.Ln, bias=1.0), reads=[sg], writes=[sg])
                P.op("act", lambda h: h.activation(out=sg.t[:], in_=sg.t[:], func=AF.Exp, scale=-1.0), reads=[sg], writes=[sg])
                P.op("dve", lambda h: h.tensor_tensor(out=sg.t[:], in0=pg.t[:], in1=sg.t[:], op=ALU.mult), reads=[pg, sg], writes=[sg])
                prel(pg)
                pT2 = pbank(True)
                pT2v = pT2.t.bitcast(BF16)
                heads4(pT2, lambda h, hh: h.transpose(pT2v[:, hh * 128:(hh + 1) * 128], kbf.t[:, hh, :], identb.t[:]), [kbf, identb])
                yield
                P.op("act", lambda h: h.activation(out=kT.t[:].rearrange("p a b -> p (a b)"), in_=pT2v[:, 0:512], func=AF.Copy), reads=[pT2], writes=[kT])
                prel(pT2)
                pT = pbank(True)
                pTv = pT.t.bitcast(BF16)
                heads4(pT, lambda h, hh: h.transpose(pTv[:, hh * 128:(hh + 1) * 128], qbf.t[:, hh, :], identb.t[:]), [qbf, identb])
                yield
                P.op("act", lambda h: h.activation(out=qT.t[:].rearrange("p a b -> p (a b)"), in_=pTv[:, 0:512], func=AF.Copy), reads=[pT], writes=[qT])
                prel(pT)
                yield
                psc = pbank(True)
                heads4(psc, lambda h, hh: h.matmul(psc.t[:, hh * 128:(hh + 1) * 128], lhsT=kT.t[:, hh, :], rhs=qT.t[:, hh, :], start=True, stop=True), [kT, qT])
                yield
                P.op("dve", lambda h: h.tensor_tensor(out=sc.t[:].rearrange("p a b -> p (a b)"), in0=psc.t[:], in1=dmask.t[:].rearrange("p a b -> p (a b)"), op=ALU.mult),
                     reads=[psc, dmask], writes=[sc])
                prel(psc)
                pcx = pbank(True)
                heads4(pcx, lambda h, hh: h.matmul(pcx.t[:, hh * 128:(hh + 1) * 128], lhsT=qT.t[:, hh, :], rhs=Sb_old.t[:, hh, :], start=True, stop=True), [qT, Sb_old])
                yield
                for hh in range(4):
                    P.op("act", lambda h, hh=hh: h.activation(out=tA.t[:, hh, :], in_=pcx.t[:, hh * 128:(hh + 1) * 128], func=AF.Copy, scale=cc(C_XI + hh)),
                         reads=[pcx, cst], writes=[tA])
                prel(pcx)
                po = pbank(True)
                heads4(po, lambda h, hh: h.matmul(po.t[:, hh * 128:(hh + 1) * 128], lhsT=sc.t[:, hh, :], rhs=vbf.t[:, hh, :], start=True, stop=True), [sc, vbf])
                yield
                P.op("dve", lambda h: h.tensor_tensor(out=osb.t[:].rearrange("p a b -> p (a b)"), in0=po.t[:], in1=tA.t[:].rearrange("p a b -> p (a b)"), op=ALU.add),
                     reads=[po, tA], writes=[osb])
                prel(po)
                yield
                for hh in range(4):
                    P.op("dve", lambda h, hh=hh: h.bn_stats(out=bst.t[:, hh, 0:6], in_=osb.t[:, hh, :]), reads=[osb], writes=[bst])
                for hh in range(4):
                    P.op("dve", lambda h, hh=hh: h.bn_aggr(out=bst.t[:, hh, 6:8], in_=bst.t[:, hh, 0:6]), reads=[bst], writes=[bst])
                yield
                s = stat.next()
                P.op("act", lambda h: h.activation(out=s.t[:, 0:4], in_=bst.t[:, :, 7], func=AF.Ln, bias=epsc.t[:, 0:1]), reads=[bst, epsc], writes=[s])
                yield
                P.op("act", lambda h: h.activation(out=s.t[:, 0:4], in_=s.t[:, 0:4], func=AF.Exp, scale=-0.5), reads=[s], writes=[s])
                yield
                for hh in range(4):
                    P.op("dve", lambda h, hh=hh: h.tensor_scalar(out=yn.t[:, hh, :], in0=osb.t[:, hh, :], scalar1=bst.t[:, hh, 6:7], scalar2=s.t[:, hh:hh + 1],
                                                                 op0=ALU.subtract, op1=ALU.mult),
                         reads=[osb, bst, s], writes=[yn])
                yield
                P.op("pool", lambda h: h.tensor_tensor(out=ybf.t[:].rearrange("p a b -> p (a b)"), in0=yn.t[:].rearrange("p a b -> p (a b)"), in1=sg.t[:], op=ALU.mult),
                     reads=[yn, sg], writes=[ybf])
                yield
                pY = pbank(True)
                pYv = pY.t.bitcast(BF16)
                heads4(pY, lambda h, hh: h.transpose(pYv[:, hh * 128:(hh + 1) * 128], ybf.t[:, hh, :], identb.t[:]), [ybf, identb])
                yield
                P.op("dve", lambda h: h.tensor_tensor(out=mixT.t[:, 4:8, c * 128:(c + 1) * 128], in0=pYv[:, 0:512].rearrange("p (a b) -> p a b", a=4),
                                                      in1=cst.t[:, C_RNG:C_RNG + 4].unsqueeze(2).to_broadcast([128, 4, 128]), op=ALU.mult),
                     reads=[pY, cst], writes=[mixT])
                prel(pY)
                yield

            def outproj_chunk(mixT, c, gch):
                r0 = gch * 128
                hm = hm_r.next()
                ld("sp", hm, hm.t[:], xm[r0:r0 + 128, :])
                for half in range(2):
                    pb = pbank()

                    def f(h, half=half, pb=pb):
                        ins = None
                        for kc in range(8):
                            ins = h.matmul(pb.t[:], lhsT=mixT.t[:, kc, c * 128:(c + 1) * 128], rhs=w_out_sb.t[:, kc, half * 512:(half + 1) * 512],
                                           start=(kc == 0), stop=(kc == 7))
                        return ins
                    P.op("pe", f, reads=[mixT, w_out_sb], writes=[pb])
                    P.op("dve", lambda h, half=half, pb=pb: h.tensor_tensor(out=hm.t[:, half * 512:(half + 1) * 512], in0=pb.t[:], in1=hm.t[:, half * 512:(half + 1) * 512], op=ALU.add),
                         reads=[pb, hm], writes=[hm])
                hb = hbufs[gch]
                P.dma("sp", key_of(hm), lambda h: h.dma_start(out=hmid[r0:r0 + 128, :], in_=hm.t[:]), reads=[hm], writes=[hb])

            def run_lanes(lanes, delays=None):
                cur = [None] * len(lanes)
                idx = [0] * len(lanes)
                dl = list(delays) if delays else [0] * len(lanes)
                dl = dl + [0] * (len(lanes) - len(dl))
                active = True
                while active:
                    active = False
                    for li, lane in enumerate(lanes):
                        if dl[li] > 0:
                            dl[li] -= 1
                            active = True
                            continue
                        if cur[li] is None:
                            if idx[li] < len(lane):
                                cur[li] = lane[idx[li]]
                                idx[li] += 1
                            else:
                                continue
                        active = True
                        try:
                            next(cur[li])
                        except StopIteration:
                            cur[li] = None

            hbufs = [Buf("hmid%d" % i) for i in range(NM)]

            def tail_gen(prev, uT, N):
                if prev is not None:
                    lru_final(prev[0], prev[1])
                yield
                if uT is not None:
                    for _ in gelu_batch(uT, N):
                        yield
                    yield
                if prev is not None:
                    for c in range(prev[3]):
                        outproj_chunk(prev[1], c, prev[2] + c)
                        yield
                        yield

            def mixer(blocks, xd, tabd, full, flag_after_first):
                if not blocks:
                    return
                nxt = stageA(xd, tabd, *blocks[0])
                tail = None
                for bi, (c0, nch) in enumerate(blocks):
                    N = nch * 128
                    uT, tabs = nxt
                    if bi + 1 < len(blocks):
                        nxt = stageA(xd, tabd, *blocks[bi + 1])
                    mixT = mixT_r.next() if full else None
                    xl = lru_prelude(N)
                    if full:
                        lanes = [
                            [lru_tile(uT, N, full, 0, xl, LS[0]), lru_tile(uT, N, full, 2, xl, LS[0])],
                            [lru_tile(uT, N, full, 1, xl, LS[1]), lru_tile(uT, N, full, 3, xl, LS[1])],
                        ]
                        dls = [0, LDL]
                    else:
                        lanes = [[lru_tile(uT, N, full, 0, xl, LS[0])], [lru_tile(uT, N, full, 1, xl, LS[1])],
                                 [lru_tile(uT, N, full, 2, xl, LSP[0])], [lru_tile(uT, N, full, 3, xl, LSP[1])]]
                        dls = [0, 1, 2, 3]
                    lanes = lanes + [[ret_chunk(uT, c, tabs[c], full, mixT, RS[c])] for c in range(nch)]
                    dls = dls + [(RDL if full else 2) * c for c in range(nch)]
                    if full:
                        lanes.append([tail_gen(tail, uT, N)])
                    run_lanes(lanes, dls)
                    if bgq and bi >= 1:
                        bgq.pop(0)()
                    tail = (N, mixT, c0, nch) if full else None
                    if flag_after_first and bi == 0:
                        P.op("dve", lambda h: h.tensor_scalar(out=hstate.t[:], in0=hstate.t[:], scalar1=cc(C_FLAG), scalar2=None, op0=ALU.mult),
                             reads=[hstate, cst], writes=[hstate])
                if tail is not None:
                    run_lanes([[tail_gen(tail, None, 0)]])

            mixer(split_blocks(NP, BC) if STG >= 1 else [], xp, tabp, False, False)
            mixer(split_blocks(NM, BC, first=1) if STG >= 2 else [], xm, tabm, True, True)
            while bgq:
                bgq.pop(0)()
            P.barrier()
            P.emit_block()

        with ExitStack() as st2:
            NF = FBC * 128
            w_up_all = mk(st2, "w_up_all", [128, 8, 8, 768], BF16)[0]
            w_dn_all = mk(st2, "w_dn_all", [128, 24, D], BF16)[0]
            wup_b = [Buf("wupg%d" % g) for g in range(8)]
            wdn_b = [Buf("wdng%d" % g) for g in range(4)]
            gfin = mk(st2, "gfin", [128, D], F32)[0]
            tails = mk(st2, "tails", [128, 48, 2], F32)[0]
            tails_b = [Buf("tails%d" % j) for j in range(48)]
            hs_r = Rot(mk(st2, "hs", [128, D], F32, FBC))
            ha_r = Rot(mk(st2, "ha", [128, D], F32, 2))
            xn2_r = Rot(mk(st2, "xn2", [128, D], BF16, 1))
            u2T_r = Rot(mk(st2, "u2T", [128, 8, NF], BF16, 1))
            gT = [Rot(mk(st2, "gT%d" % m, [128, NF], BF16, 1)) for m in range(24)]
            ya_r = Rot(mk(st2, "ya", [128, NF], F32, 3))
            yv_r = Rot(mk(st2, "yv", [128, NF], F32, 2))
            U_r = Rot(mk(st2, "U", [128, NF + 2], F32, 2))

            for g in (0, 4, 1, 5, 2, 6, 3, 7):
                P.dma("sp", key_of(wup_b[g]), lambda h, g=g: h.dma_start(out=w_up_all.t[:, g, :, :].rearrange("p a b -> p (a b)"), in_=wup_bf[g].rearrange("p a b -> p (a b)")),
                      reads=[wupd_b], writes=[wup_b[g]])
            for i in range(4):
                P.dma("sp", key_of(wdn_b[i]), lambda h, i=i: h.dma_start(out=w_dn_all.t[:, i * 6:(i + 1) * 6, :].rearrange("p a b -> p (a b)"), in_=wdn_bf[:, i * 6:(i + 1) * 6, :].rearrange("p a b -> p (a b)")),
                      reads=[wdnd_b], writes=[wdn_b[i]])
            ld("sp", gfin, gfin.t[:], gfin_d)
            for j in range(48):
                P.op("pool", lambda h, j=j: h.memset(tails.t[:, j, :], 0.0), writes=[tails_b[j]])

            Uh_b = {}

            def conv_tile(pu, j, N, y, init_on_act):
                w0, w1, w2 = cc(C_FCW + j), cc(C_FCW + 48 + j), cc(C_FCW + 96 + j)
                U = U_r.next()
                Uh = Uh_b.setdefault(U.b.name, Buf(U.b.name + "_h"))
                P.op("pool", lambda h: h.tensor_copy(out=U.t[:, 0:2], in_=tails.t[:, j, :]), reads=[tails_b[j]], writes=[Uh])
                P.op("act", lambda h: h.activation(out=U.t[:, 2:2 + N], in_=pu.t[:, 0:N], func=AF.Copy), reads=[pu], writes=[U])
                if init_on_act:
                    P.op("act", lambda h: h.activation(out=y.t[:, 0:N], in_=pu.t[:, 0:N], func=AF.Identity, bias=cc(C_FCB + j), scale=w2),
                         reads=[pu, cst], writes=[y])
                else:
                    P.op("dve", lambda h: h.tensor_scalar(out=y.t[:, 0:N], in0=U.t[:, 2:2 + N], scalar1=w2, scalar2=cc(C_FCB + j), op0=ALU.mult, op1=ALU.add),
                         reads=[U, cst], writes=[y])
                P.op("dve", lambda h: h.scalar_tensor_tensor(out=y.t[:, 0:N], in0=U.t[:, 1:1 + N], scalar=w1, in1=y.t[:, 0:N], op0=ALU.mult, op1=ALU.add),
                     reads=[U, Uh, y, cst], writes=[y])
                P.op("dve", lambda h: h.scalar_tensor_tensor(out=y.t[:, 0:N], in0=U.t[:, 0:N], scalar=w0, in1=y.t[:, 0:N], op0=ALU.mult, op1=ALU.add),
                     reads=[U, Uh, y, cst], writes=[y])
                P.op("pool", lambda h: h.tensor_copy(out=tails.t[:, j, :], in_=U.t[:, N:N + 2]), reads=[U], writes=[tails_b[j]])

            last_out = None
            fblocks = split_blocks(NM, FBC) if STG >= 3 else []

            def stageA_F(c0, nch):
                u2T = u2T_r.next()
                for c in range(nch):
                    ha = ha_r.next()
                    r0 = (c0 + c) * 128
                    P.dma("sp", key_of(ha), lambda h, ha=ha, r0=r0: h.dma_start(out=ha.t[:], in_=hmid[r0:r0 + 128, :]), reads=[hbufs[c0 + c]], writes=[ha])
                    xn2 = xn2_r.next()
                    norm_transpose(ha, xn2, u2T, c * 128, C_G2, None)
                return u2T

            u2T_next = stageA_F(*fblocks[0]) if fblocks else None
            for bi, (c0, nch) in enumerate(fblocks):
                N = nch * 128
                u2T = u2T_next
                pend = None
                for m in range(24):
                    pa = pbank()
                    pv = pbank()
                    for (pb, j) in ((pa, m), (pv, 24 + m)):
                        def f(h, pb=pb, j=j, N=N, u2T=u2T):
                            ins = None
                            for kc in range(8):
                                ins = h.matmul(pb.t[:, 0:N], lhsT=w_up_all.t[:, j // 6, kc, (j % 6) * 128:(j % 6 + 1) * 128], rhs=u2T.t[:, kc, 0:N], start=(kc == 0), stop=(kc == 7))
                            return ins
                        P.op("pe", f, reads=[wup_b[j // 6], u2T], writes=[pb])
                    ya = ya_r.next()
                    yv = yv_r.next()
                    conv_tile(pa, m, N, ya, True)
                    conv_tile(pv, 24 + m, N, yv, True)
                    if pend is not None:
                        pend()

                    def pend(ya=ya, yv=yv, m=m, N=N):
                        P.op("act", lambda h: h.activation(out=ya.t[:, 0:N], in_=ya.t[:, 0:N], func=AF.Gelu), reads=[ya], writes=[ya])
                        g = gT[m].next()
                        P.op("pool", lambda h: h.tensor_tensor(out=g.t[:, 0:N], in0=ya.t[:, 0:N], in1=yv.t[:, 0:N], op=ALU.mult),
                             reads=[ya, yv], writes=[g])
                pend()
                hss = []
                for c in range(nch):
                    hs = hs_r.next()
                    r0 = (c0 + c) * 128
                    P.dma("sp", key_of(hs), lambda h, hs=hs, r0=r0: h.dma_start(out=hs.t[:], in_=hmid[r0:r0 + 128, :]), reads=[hbufs[c0 + c]], writes=[hs])
                    hss.append(hs)
                if bi + 1 < len(fblocks):
                    u2T_next = stageA_F(*fblocks[bi + 1])
                for c in range(nch):
                    hs = hss[c]
                    for half in range(2):
                        pb = pbank()

                        def f(h, half=half, pb=pb, c=c):
                            ins = None
                            for m in range(24):
                                ins = h.matmul(pb.t[:], lhsT=gT[m].items[0].t[:, c * 128:(c + 1) * 128], rhs=w_dn_all.t[:, m, half * 512:(half + 1) * 512],
                                               start=(m == 0), stop=(m == 23))
                            return ins
                        P.op("pe", f, reads=[gT[m].items[0] for m in range(24)] + wdn_b, writes=[pb])
                        P.op("dve", lambda h, half=half, pb=pb, hs=hs: h.tensor_tensor(out=hs.t[:, half * 512:(half + 1) * 512], in0=pb.t[:],
                                                                                       in1=hs.t[:, half * 512:(half + 1) * 512], op=ALU.add),
                             reads=[pb, hs], writes=[hs])
                    s = stat.next()
                    junk = xn2_r.next()
                    P.op("pool", lambda h, s=s: h.memset(s.t[:], 0.0), writes=[s])
                    P.op("act", lambda h, s=s, hs=hs, junk=junk: h.activation(out=junk.t[:], in_=hs.t[:], func=AF.Square, accum_out=s.t[:, 0:1]),
                         reads=[hs, s], writes=[junk, s])
                    P.op("act", lambda h, s=s: h.activation(out=s.t[:, 1:2], in_=s.t[:, 0:1], func=AF.Sqrt, bias=epsc.t[:, 0:1], scale=1.0 / D),
                         reads=[s, epsc], writes=[s])
                    P.op("dve", lambda h, s=s: h.reciprocal(out=s.t[:, 2:3], in_=s.t[:, 1:2]), reads=[s], writes=[s])
                    P.op("act", lambda h, s=s, hs=hs: h.activation(out=hs.t[:], in_=hs.t[:], func=AF.Copy, scale=s.t[:, 2:3]),
                         reads=[hs, s], writes=[hs])
                    P.op("pool", lambda h, hs=hs: h.tensor_tensor(out=hs.t[:], in0=hs.t[:], in1=gfin.t[:], op=ALU.mult),
                         reads=[hs, gfin], writes=[hs])
                    r0 = (c0 + c) * 128
                    last_out = P.dma("sp", key_of(hs), lambda h, hs=hs, r0=r0: h.dma_start(out=out[r0:r0 + 128, :], in_=hs.t[:]), reads=[hs])
            P.barrier()
            P.emit_block()
    return nc


_CACHE = {}


def _tables(pos):
    inv_freq = (np.float32(10000.0) ** (-np.arange(0, 128, 2, dtype=np.float32) / np.float32(128))).astype(np.float32)
    ang = (pos.astype(np.float32)[:, None] * inv_freq[None, :]).astype(np.float32)
    c = np.cos(ang).astype(np.float32)
    s = np.sin(ang).astype(np.float32)
    return np.ascontiguousarray(np.concatenate([c, c, -s, s], axis=1).astype(np.float32))


def _consts():
    log_g = np.log1p(-np.exp2(-5.0 - np.arange(4, dtype=np.float64)))
    idx = np.arange(128, dtype=np.float64)
    sc = 128.0 ** -0.5
    diff = idx[None, :] - idx[:, None]
    dm = np.where(diff[:, None, :] >= 0, np.exp(np.maximum(diff, 0.0)[:, None, :] * log_g[None, :, None]), 0.0) * sc
    xi = np.exp((idx + 1.0)[:, None] * log_g[None, :])
    zeta = np.exp((127.0 - idx)[:, None] * log_g[None, :]) * sc
    return dm.reshape(128, 512).astype(np.float32), xi.astype(np.float32), zeta.astype(np.float32)


def kernel(x, norm1_gain, w_in, lru_conv_w, lru_conv_b, lru_gate_a_w, lru_gate_a_b,
           lru_gate_x_w, lru_gate_x_b, lru_lambda, lru_norm_gain, ret_norm_gain, w_out,
           norm2_gain, ffn_up_w, ffn_conv_w, ffn_conv_b, ffn_down_w, final_norm_gain, _dbg=None):
    f = lambda a: np.ascontiguousarray(np.asarray(a, dtype=np.float32))
    x = f(x)
    B, S, _ = x.shape
    half = S // 2
    NM = half // 128 + 1
    NP = half // 128 - 1
    key = (NM, NP)
    if key not in _CACHE:
        _CACHE[key] = build(NM, NP)
    nc = _CACHE[key]

    dmask, xi, zeta = _consts()

    def pp(v, n):
        return f(v).reshape(n, 128).T
    cst = np.zeros((128, NCST), np.float32)
    cst[:, C_G1:C_G1 + 8] = pp(norm1_gain[0], 8)
    cst[:, C_G2:C_G2 + 8] = pp(norm2_gain[0], 8)
    for k in range(4):
        cst[:, C_LCW + k * 4:C_LCW + k * 4 + 4] = pp(lru_conv_w[0, k], 4)
    cst[:, C_LCB:C_LCB + 4] = pp(lru_conv_b[0], 4)
    cst[:, C_LAB:C_LAB + 4] = pp(lru_gate_a_b[0], 4)
    cst[:, C_LXB:C_LXB + 4] = pp(lru_gate_x_b[0], 4)
    cst[:, C_LAM:C_LAM + 4] = pp(lru_lambda[0], 4)
    cst[:, C_LNG:C_LNG + 4] = pp(lru_norm_gain[0], 4)
    cst[:, C_RNG:C_RNG + 4] = pp(ret_norm_gain[0], 4)
    for k in range(3):
        cst[:, C_FCW + k * 48:C_FCW + k * 48 + 48] = pp(ffn_conv_w[0, k], 48)
    cst[:, C_FCB:C_FCB + 48] = pp(ffn_conv_b[0], 48)
    cst[:, C_XI:C_XI + 4] = xi
    cst[:, C_ZETA:C_ZETA + 4] = zeta
    gw = np.zeros((128, 8, 128), np.float32)
    for j, wsrc in enumerate((f(lru_gate_a_w[0]), f(lru_gate_x_w[0]))):
        for t in range(4):
            gw[0:64, j * 4 + t, 0:64] = wsrc[2 * t]
            gw[64:128, j * 4 + t, 64:128] = wsrc[2 * t + 1]
    gfin = np.ascontiguousarray(np.broadcast_to(f(final_norm_gain)[None, :], (128, D)))
    ident = np.eye(128, dtype=np.float32)
    shared = {
        "gfin": gfin, "dmask": dmask, "gw": gw.reshape(128, 1024), "ident": ident,
        "w_in": f(w_in[0]), "w_out": f(w_out[0]), "w_up": f(ffn_up_w[0]), "w_dn": f(ffn_down_w[0]),
    }
    in_maps = []
    for b in range(B):
        for hh in range(2):
            c2 = cst.copy()
            c2[:, C_FLAG] = float(hh)
            if hh == 0:
                xmc = np.concatenate([np.zeros((128, D), np.float32), x[b, :half]], axis=0)
                xpc = np.zeros((max(NP, 1) * 128, D), np.float32)
                posm = np.concatenate([np.arange(128), np.arange(half)])
                posp = np.zeros(max(NP, 1) * 128)
            else:
                xmc = x[b, half - 128:]
                xpc = x[b, :half - 128] if NP > 0 else np.zeros((128, D), np.float32)
                posm = np.arange(half - 128, S)
                posp = np.arange(half - 128) if NP > 0 else np.zeros(128)
            m = dict(shared)
            m.update({"xm": np.ascontiguousarray(xmc), "xp": np.ascontiguousarray(xpc), "tabm": _tables(posm), "tabp": _tables(posp), "cst": c2})
            in_maps.append(m)
    res = run_bass_kernel_spmd(nc, in_maps, core_ids=list(range(B * 2)))
    if os.environ.get("KHM"):
        global _HM
        _HM = np.empty((B, S, D), np.float32)
        for b in range(B):
            for hh in range(2):
                _HM[b, hh * half:(hh + 1) * half] = res.results[b * 2 + hh]["hmid"][128:]
    outp = np.empty((B, S, D), np.float32)
    for b in range(B):
        for hh in range(2):
            o = res.results[b * 2 + hh]["out"]
            outp[b, hh * half:(hh + 1) * half] = o[128:]
    return outp
```

```python
import os
import numpy as np
from contextlib import ExitStack
import concourse.bass as bass
import concourse.mybir as mybir
from concourse.bass_utils import run_bass_kernel_spmd

F32 = mybir.dt.float32
BF16 = mybir.dt.bfloat16
AF = mybir.ActivationFunctionType
ALU = mybir.AluOpType

D = 1024
DIN = 3072
DFF = 3072
EPS = 1e-6
ENGS = ("pe", "act", "dve", "pool", "sp")

C_G1, C_G2, C_LCW, C_LCB, C_LAB, C_LXB, C_LAM, C_LNG, C_RNG = 0, 8, 16, 32, 36, 40, 44, 48, 52
C_FCW, C_FCB, C_XI, C_ZETA, C_FLAG, NCST = 56, 200, 248, 252, 256, 257


class Buf:
    __slots__ = ("name", "w", "r", "ps")

    def __init__(self, name):
        self.name = name
        self.w = None
        self.r = []
        self.ps = False


class T:
    __slots__ = ("t", "b")

    def __init__(self, t, name):
        self.t = t
        self.b = Buf(name)


class Prog:
    def __init__(self, nc):
        self.nc = nc
        self.ops = {e: [] for e in ENGS}
        self.cnt = {"E_" + e: 0 for e in ENGS}
        self.sems = {}
        self.waited = {e: {} for e in ENGS}
        self.dkeys = []

    def _need(self, eng, toks):
        out = []
        for (k, v, _e) in toks:
            if self.waited[eng].get(k, 0) >= v:
                continue
            self.waited[eng][k] = v
            out.append((k, v))
        return out

    def _deps(self, eng, reads, writes):
        best = {}

        def add(t):
            k = t[0]
            if k not in best or best[k][1] < t[1]:
                best[k] = t
        for b in reads:
            if b.w is not None:
                add(b.w)
            if b.ps:
                for t in b.r:
                    if t[2] != eng:
                        add(t)
        for b in writes:
            if b.w is not None and b.w[2] != eng:
                add(b.w)
            for t in b.r:
                if t[2] != eng:
                    add(t)
        return self._need(eng, best.values())

    def _commit(self, tok, reads, writes):
        for b in reads:
            b.r.append(tok)
        for b in writes:
            b.w = tok
            b.r = []

    def op(self, eng, fn, reads=(), writes=()):
        reads = [x.b if hasattr(x, "b") else x for x in reads]
        writes = [x.b if hasattr(x, "b") else x for x in writes]
        waits = self._deps(eng, reads, writes)
        k = "E_" + eng
        self.cnt[k] += 1
        self.ops[eng].append((waits, fn, (k, 1)))
        tok = (k, self.cnt[k], eng)
        self._commit(tok, reads, writes)
        return tok

    def dma(self, eng, key, fn, reads=(), writes=()):
        reads = [x.b if isinstance(x, T) else x for x in reads]
        writes = [x.b if isinstance(x, T) else x for x in writes]
        waits = self._deps(eng, reads, writes)
        assert key in self.cnt, key
        self.cnt[key] += 16
        self.ops[eng].append((waits, fn, (key, 16)))
        tok = (key, self.cnt[key], "dma:" + key)
        self._commit(tok, reads, writes)
        return tok

    def wait_tok(self, eng, tok):
        w = self._need(eng, [tok])
        if w:
            self.ops[eng].append((w, None, None))

    def barrier(self):
        for e in ENGS:
            toks = [(k, v, "x") for k, v in self.cnt.items() if v > 0]
            w = self._need(e, toks)
            if w:
                self.ops[e].append((w, None, None))

    def emit_block(self):
        nc = self.nc
        sems = self.sems
        with nc.Block() as block:
            def runner(eng):
                ops = self.ops[eng]

                def _run(h):
                    for (waits, fn, inc) in ops:
                        for (k, v) in waits:
                            h.wait_ge(sems[k], v)
                        if fn is not None:
                            ins = fn(h)
                            ins.then_inc(sems[inc[0]], inc[1])
                return _run
            block.tensor(runner("pe"))
            block.scalar(runner("act"))
            block.vector(runner("dve"))
            block.gpsimd(runner("pool"))
            block.sync(runner("sp"))
        self.ops = {e: [] for e in ENGS}


class Rot:
    def __init__(self, items):
        self.items = items
        self.i = 0

    def next(self):
        x = self.items[self.i % len(self.items)]
        self.i += 1
        return x


def split_blocks(n, bc, first=None):
    out = []
    s = 0
    if first:
        out.append((0, first))
        s = first
    while s < n:
        m = min(bc, n - s)
        out.append((s, m))
        s += m
    return out


def build(NM, NP, BC=3, FBC=int(os.environ.get('KFBC', '3')), dbg=False):
    nc = bass.Bass("TRN2", target_bir_lowering=False)

    def din(name, shape):
        return nc.dram_tensor(name, shape, F32, kind="ExternalInput").ap()
    xm = din("xm", [NM * 128, D])
    xp = din("xp", [max(NP, 1) * 128, D])
    tabm = din("tabm", [NM * 128, 256])
    tabp = din("tabp", [max(NP, 1) * 128, 256])
    cst_d = din("cst", [128, NCST])
    gfin_d = din("gfin", [128, D])
    dmask_d = din("dmask", [128, 512])
    gw_d = din("gw", [128, 8 * 128])
    ident_d = din("ident", [128, 128])
    w_in = din("w_in", [D, DIN])
    w_out = din("w_out", [D, D])
    w_up = din("w_up", [D, 2 * DFF])
    w_dn = din("w_dn", [DFF, D])
    out = nc.dram_tensor("out", [NM * 128, D], F32, kind="ExternalOutput").ap()
    if os.environ.get("KHM"):
        hmid = nc.dram_tensor("hmid", [NM * 128, D], F32, kind="ExternalOutput").ap()
    else:
        hmid = nc.dram_tensor("hmid", [NM * 128, D], F32).ap()

    wup_bf = nc.dram_tensor("wup_bf", [8, 128, 8, 768], BF16).ap()
    wdn_bf = nc.dram_tensor("wdn_bf", [128, 24, D], BF16).ap()
    P = Prog(nc)
    STG = int(os.environ.get('KDBG', '9'))
    MSK = int(os.environ.get('KMSK', '7'))
    SUB = int(os.environ.get('KSUB', '9'))
    LDL = int(os.environ.get('KLDL', '3'))
    RDL = int(os.environ.get('KRDL', '2'))
    NBLK = 0
    keyctr = [0]

    with ExitStack() as st0:
        def newkey(name):
            k = "D_%s_%d" % (name, keyctr[0])
            keyctr[0] += 1
            P.cnt[k] = 0
            P.dkeys.append(k)
            return k

        def mk(st, name, shape, dt, n=1):
            res = []
            for i in range(n):
                nm = "%s_%d" % (name, i)
                res.append(T(st.enter_context(nc.sbuf_tensor(nm, shape, dt)), nm))
            return res

        NKEYS = 64
        keypool = [newkey("k") for _ in range(NKEYS)]
        keyuse = {}

        def key_of(buf):
            b = buf.b if isinstance(buf, T) else buf
            if b.name not in keyuse:
                keyuse[b.name] = keypool.pop()
            return keyuse[b.name]

        for k in P.cnt:
            P.sems[k] = st0.enter_context(nc.semaphore(k))

        banks = [T(st0.enter_context(nc.psum_tensor("pb%d" % i, [128, 512], F32)), "pb%d" % i) for i in range(8)]
        for bk in banks:
            bk.b.ps = True
        bankrot = Rot(banks[:7])
        pss_bank = banks[7]

        bank_busy = {}

        def pbank(hold=False):
            for _ in range(len(bankrot.items)):
                bk = bankrot.next()
                if not bank_busy.get(bk.b.name):
                    if hold:
                        bank_busy[bk.b.name] = True
                    return bk
            raise RuntimeError("no free PSUM bank")

        def prel(bk):
            bank_busy[bk.b.name] = False

        cst = mk(st0, "cst", [128, NCST], F32)[0]
        identb = mk(st0, "identb", [128, 128], BF16)[0]
        epsc = mk(st0, "epsc", [128, 1], F32)[0]

        def ld(eng, dst, dst_ap, src_ap, reads=()):
            return P.dma(eng, key_of(dst), lambda h: h.dma_start(out=dst_ap, in_=src_ap), reads=reads, writes=[dst])

        ld("sp", cst, cst.t[:], cst_d)
        ld("pool", identb, identb.t[:], ident_d)
        P.op("pool", lambda h: h.memset(epsc.t[:], EPS), writes=[epsc])

        def cc(col, n=1):
            return cst.t[:, col:col + n]

        stat = Rot(mk(st0, "stat", [128, 4], F32, 12))

        def norm_transpose(xsrc, xn_t, uT_t, ucol, gcol, ntok_cols):
            s = stat.next()
            P.op("pool", lambda h: h.memset(s.t[:], 0.0), writes=[s])
            P.op("act", lambda h: h.activation(out=xn_t.t[:], in_=xsrc.t[:], func=AF.Square, accum_out=s.t[:, 0:1]),
                 reads=[xsrc, s], writes=[xn_t, s])
            if ntok_cols == "lnexp":
                P.op("act", lambda h: h.activation(out=s.t[:, 1:2], in_=s.t[:, 0:1], func=AF.Ln, bias=epsc.t[:, 0:1], scale=1.0 / D),
                     reads=[s, epsc], writes=[s])
                P.op("act", lambda h: h.activation(out=s.t[:, 2:3], in_=s.t[:, 1:2], func=AF.Exp, scale=-0.5), reads=[s], writes=[s])
            else:
                P.op("act", lambda h: h.activation(out=s.t[:, 1:2], in_=s.t[:, 0:1], func=AF.Sqrt, bias=epsc.t[:, 0:1], scale=1.0 / D),
                     reads=[s, epsc], writes=[s])
                P.op("dve", lambda h: h.reciprocal(out=s.t[:, 2:3], in_=s.t[:, 1:2]), reads=[s], writes=[s])
            P.op("act", lambda h: h.activation(out=xn_t.t[:], in_=xsrc.t[:], func=AF.Copy, scale=s.t[:, 2:3]),
                 reads=[xsrc, s], writes=[xn_t])
            pb = pbank()
            pbv = pb.t.bitcast(BF16)

            def tr(h):
                ins = None
                for kc in range(8):
                    ins = h.transpose(pbv[:, kc * 128:(kc + 1) * 128], xn_t.t[:, kc * 128:(kc + 1) * 128], identb.t[:])
                return ins
            P.op("pe", tr, reads=[xn_t, identb], writes=[pb])
            P.op("dve", lambda h: h.tensor_tensor(
                out=uT_t.t[:, :, ucol:ucol + 128],
                in0=pbv[:, 0:1024].rearrange("p (k t) -> p k t", k=8),
                in1=cst.t[:, gcol:gcol + 8].unsqueeze(2).to_broadcast([128, 8, 128]), op=ALU.mult),
                reads=[pb, cst], writes=[uT_t])
            return s

        with ExitStack() as st1:
            W = BC * 128
            w_in_sb = mk(st1, "w_in_sb", [128, 8, DIN], BF16)[0]
            w_out_sb = mk(st1, "w_out_sb", [128, 8, D], BF16)[0]
            gwb = mk(st1, "gwb", [128, 8, 128], BF16)[0]
            dcw = mk(st1, "dcw", [128, 16, 128], BF16)[0]
            onesb = mk(st1, "onesb", [128, 128], BF16)[0]
            dmask = mk(st1, "dmask", [128, 4, 128], F32)[0]
            lruc = mk(st1, "lruc", [128, 8], F32)[0]
            hstate = mk(st1, "hstate", [128, 4], F32)[0]
            Sst = mk(st1, "Sst", [128, 4, 128], F32)[0]
            S_bfs = mk(st1, "S_bf", [128, 4, 128], BF16, 4)
            xs_r = Rot(mk(st1, "xs", [128, D], F32, 2))
            hm_r = Rot(mk(st1, "hm", [128, D], F32, 2))
            tab_r = Rot(mk(st1, "tab", [128, 256], F32, 6))
            xn_r = Rot(mk(st1, "xn", [128, D], BF16, 2))
            uT_r = Rot(mk(st1, "uT", [128, 8, W], BF16, 2))
            xl_r = Rot(mk(st1, "xl", [128, 4, 3 + W], BF16, 2))
            mixT_r = Rot(mk(st1, "mixT", [128, 8, W], BF16, 2))
            LS = []
            for i in range(2):
                d = {n: mk(st1, "%s%d" % (n, i), [128, W], F32)[0] for n in ("xc", "rr", "ii", "aa", "bb")}
                d["a2"], d["hl"], d["gg"] = d["rr"], d["ii"], d["xc"]
                d["xcb"] = mk(st1, "xcb%d" % i, [128, W], BF16)[0]
                d["zsq"] = mk(st1, "zsq%d" % i, [128, W], BF16)[0]
                LS.append(d)
            rstd_t = mk(st1, "rstd", [128, W], F32)[0]
            zz = mk(st1, "zz", [128, W], F32, 4)
            RS = []
            for i in range(3):
                d = {}
                for n in ("tA", "tB", "osb"):
                    d[n] = mk(st1, "%s%d" % (n, i), [128, 4, 128], F32)[0]
                d["sg"] = mk(st1, "sg%d" % i, [128, 512], F32)[0]
                for n in ("vbf", "vz", "qbf", "kbf", "qT", "kT", "sc"):
                    d[n] = mk(st1, "%s%d" % (n, i), [128, 4, 128], BF16)[0]
                d["yn"], d["ybf"] = d["tB"], d["qbf"]
                d["bst"] = mk(st1, "bst%d" % i, [128, 4, 8], F32)[0]
                RS.append(d)

            class V:
                def __init__(self, t_obj, flat):
                    self.t = t_obj.t[:].rearrange("p a b -> p (a b)") if flat else t_obj.t
                    self.b = t_obj.b

            LSP = [
                {"xc": V(zz[0], False), "rr": V(zz[1], False), "ii": V(zz[2], False), "aa": V(zz[3], False),
                 "bb": V(RS[0]["osb"], True), "xcb": V(RS[0]["qT"], True), "zsq": None},
                {"xc": V(RS[0]["sg"], False), "rr": V(RS[1]["sg"], False), "ii": V(RS[2]["sg"], False), "aa": V(RS[1]["osb"], True),
                 "bb": V(RS[2]["osb"], True), "xcb": V(RS[1]["qT"], True), "zsq": None},
            ]
            for d in LSP:
                d["a2"], d["hl"], d["gg"] = d["rr"], d["ii"], d["xc"]

            w_in_v = w_in.rearrange("(kc p) n -> p kc n", p=128)
            wib = [w_in_sb.b] * 6
            for kc in range(8):
                ld("pool", w_in_sb, w_in_sb.t[:, kc, :], w_in_v[:, kc, :])
            ld("pool", gwb, gwb.t[:].rearrange("p a b -> p (a b)"), gw_d)
            ld("sp", dmask, dmask.t[:].rearrange("p a b -> p (a b)"), dmask_d)
            w_out_v = w_out.rearrange("(kc p) n -> p kc n", p=128)
            for kc in range(8):
                ld("pool", w_out_sb, w_out_sb.t[:, kc, :], w_out_v[:, kc, :])
            wupd_b = Buf("wupd")
            wdnd_b = Buf("wdnd")
            bgq = []
            w_up_g = w_up.rearrange("(kc p) (g c) -> g p kc c", p=128, c=768)
            for g in (0, 4, 1, 5, 2, 6, 3, 7):
                bgq.append(lambda g=g: P.dma("pool", key_of(wupd_b), lambda h: h.dma_start(out=wup_bf[g], in_=w_up_g[g]), writes=[wupd_b]))
            w_dn_g = w_dn.rearrange("(m p) n -> p m n", p=128)
            for i in range(0, 24, 6):
                bgq.append(lambda i=i: P.dma("pool", key_of(wdnd_b), lambda h: h.dma_start(out=wdn_bf[:, i:i + 6, :], in_=w_dn_g[:, i:i + 6, :]), writes=[wdnd_b]))
            P.op("pool", lambda h: h.memset(onesb.t[:], 1.0), writes=[onesb])
            P.op("pool", lambda h: h.memset(hstate.t[:], 0.0), writes=[hstate])
            P.op("pool", lambda h: h.memset(Sst.t[:].rearrange("p a b -> p (a b)"), 0.0), writes=[Sst])
            for sb_ in S_bfs:
                P.op("pool", lambda h, sb_=sb_: h.memset(sb_.t[:].rearrange("p a b -> p (a b)"), 0.0), writes=[sb_])
            for xl in xl_r.items:
                P.op("pool", lambda h, xl=xl: h.memset(xl.t[:].rearrange("p a b -> p (a b)"), 0.0), writes=[xl])
            for j in range(16):
                P.op("dve", lambda h, j=j: h.tensor_scalar(out=dcw.t[:, j, :], in0=identb.t[:], scalar1=cc(C_LCW + j), scalar2=None, op0=ALU.mult),
                     reads=[identb, cst], writes=[dcw])
            P.op("act", lambda h: h.activation(out=lruc.t[:, 4:8], in_=cc(C_LAM, 4), func=AF.Exp, scale=-1.0), reads=[cst], writes=[lruc])
            P.op("act", lambda h: h.activation(out=lruc.t[:, 4:8], in_=lruc.t[:, 4:8], func=AF.Ln, bias=1.0), reads=[lruc], writes=[lruc])
            P.op("dve", lambda h: h.tensor_scalar(out=lruc.t[:, 0:4], in0=lruc.t[:, 4:8], scalar1=-8.0, scalar2=None, op0=ALU.mult), reads=[lruc], writes=[lruc])

            nbias = mk(st1, "nbias", [128, 8], F32)[0]
            P.op("dve", lambda h: h.tensor_scalar(out=nbias.t[:], in0=cst.t[:, C_LAB:C_LAB + 8], scalar1=-1.0, scalar2=None, op0=ALU.mult), reads=[cst], writes=[nbias])
            gC = [float(np.exp(128.0 * np.log1p(-np.exp2(-5.0 - hh)))) for hh in range(4)]

            def stageA(xd, tabd, c0, nch):
                uT = uT_r.next()
                tabs = []
                for c in range(nch):
                    xs = xs_r.next()
                    r0 = (c0 + c) * 128
                    ld("sp", xs, xs.t[:], xd[r0:r0 + 128, :])
                    tb = tab_r.next()
                    ld("sp", tb, tb.t[:], tabd[r0:r0 + 128, :])
                    tabs.append(tb)
                    xn = xn_r.next()
                    norm_transpose(xs, xn, uT, c * 128, C_G1, "lnexp")
                return uT, tabs

            def fm_proj(uT, col, N, pb):
                def f(h):
                    ins = None
                    for kc in range(8):
                        ins = h.matmul(pb.t[:, 0:N], lhsT=w_in_sb.t[:, kc, col:col + 128], rhs=uT.t[:, kc, 0:N],
                                       start=(kc == 0), stop=(kc == 7))
                    return ins
                P.op("pe", f, reads=[wib[col // 512], uT], writes=[pb])

            prevN = [W]

            def lru_prelude(N):
                xl = xl_r.next()
                xl_prev = xl_r.items[(xl_r.i) % 2]
                Np = prevN[0]
                P.op("pool", lambda h: h.tensor_copy(out=xl.t[:, :, 0:3], in_=xl_prev.t[:, :, Np:Np + 3]),
                     reads=[xl_prev], writes=[xl])
                prevN[0] = N
                return xl

            def lru_tile(uT, N, full, t, xl, B):
                xc, rr, ii, aa, a2, bb, hl, xcb, zsq = (B[n] for n in ("xc", "rr", "ii", "aa", "a2", "bb", "hl", "xcb", "zsq"))
                pb = pbank(True)
                fm_proj(uT, t * 128, N, pb)
                yield
                P.op("act", lambda h: h.activation(out=xl.t[:, t, 3:3 + N], in_=pb.t[:, 0:N], func=AF.Copy), reads=[pb], writes=[xl])
                prel(pb)
                yield
                pc = pbank(True)

                def fconv(h):
                    ins = None
                    for k in range(4):
                        ins = h.matmul(pc.t[:, 0:N], lhsT=dcw.t[:, k * 4 + t, :], rhs=xl.t[:, t, k:k + N], start=(k == 0), stop=(k == 3))
                    return ins
                P.op("pe", fconv, reads=[dcw, xl], writes=[pc])
                yield
                P.op("act", lambda h: h.activation(out=xc.t[:, 0:N], in_=pc.t[:, 0:N], func=AF.Identity, bias=cc(C_LCB + t)),
                     reads=[pc, cst], writes=[xc])
                prel(pc)
                P.op("pool", lambda h: h.tensor_copy(out=xcb.t[:, 0:N], in_=xc.t[:, 0:N]), reads=[xc], writes=[xcb])
                yield
                pa = pbank(True)
                P.op("pe", lambda h: h.matmul(pa.t[:, 0:N], lhsT=gwb.t[:, t, :], rhs=xcb.t[:, 0:N], start=True, stop=True), reads=[gwb, xcb], writes=[pa])
                yield
                P.op("act", lambda h: h.activation(out=rr.t[:, 0:N], in_=pa.t[:, 0:N], func=AF.Exp, bias=nbias.t[:, t:t + 1], scale=-1.0), reads=[pa, nbias], writes=[rr])
                prel(pa)
                px = pbank(True)
                P.op("pe", lambda h: h.matmul(px.t[:, 0:N], lhsT=gwb.t[:, 4 + t, :], rhs=xcb.t[:, 0:N], start=True, stop=True), reads=[gwb, xcb], writes=[px])
                yield
                P.op("act", lambda h: h.activation(out=rr.t[:, 0:N], in_=rr.t[:, 0:N], func=AF.Ln, bias=1.0), reads=[rr], writes=[rr])
                P.op("act", lambda h: h.activation(out=ii.t[:, 0:N], in_=px.t[:, 0:N], func=AF.Exp, bias=nbias.t[:, 4 + t:5 + t], scale=-1.0), reads=[px, nbias], writes=[ii])
                prel(px)
                yield
                P.op("act", lambda h: h.activation(out=rr.t[:, 0:N], in_=rr.t[:, 0:N], func=AF.Exp, scale=-1.0), reads=[rr], writes=[rr])
                P.op("act", lambda h: h.activation(out=ii.t[:, 0:N], in_=ii.t[:, 0:N], func=AF.Ln, bias=1.0), reads=[ii], writes=[ii])
                yield
                P.op("act", lambda h: h.activation(out=aa.t[:, 0:N], in_=rr.t[:, 0:N], func=AF.Exp, scale=lruc.t[:, t:t + 1]), reads=[rr, lruc], writes=[aa])
                P.op("act", lambda h: h.activation(out=ii.t[:, 0:N], in_=ii.t[:, 0:N], func=AF.Exp, scale=-1.0), reads=[ii], writes=[ii])
                yield
                P.op("pool", lambda h: h.tensor_tensor(out=a2.t[:, 0:N], in0=aa.t[:, 0:N], in1=aa.t[:, 0:N], op=ALU.mult), reads=[aa], writes=[a2])
                P.op("pool", lambda h: h.tensor_tensor(out=bb.t[:, 0:N], in0=ii.t[:, 0:N], in1=xc.t[:, 0:N], op=ALU.mult), reads=[ii, xc], writes=[bb])
                yield
                P.op("act", lambda h: h.activation(out=a2.t[:, 0:N], in_=a2.t[:, 0:N], func=AF.Ln, bias=1.0, scale=-1.0), reads=[a2], writes=[a2])
                yield
                P.op("act", lambda h: h.activation(out=a2.t[:, 0:N], in_=a2.t[:, 0:N], func=AF.Exp, scale=0.5), reads=[a2], writes=[a2])
                yield
                P.op("pool", lambda h: h.tensor_tensor(out=bb.t[:, 0:N], in0=bb.t[:, 0:N], in1=a2.t[:, 0:N], op=ALU.mult), reads=[a2, bb], writes=[bb])
                yield
                P.op("dve", lambda h: h.tensor_tensor_scan(out=hl.t[:, 0:N], data0=aa.t[:, 0:N], data1=bb.t[:, 0:N],
                                                           initial=hstate.t[:, t:t + 1], op0=ALU.mult, op1=ALU.add),
                     reads=[aa, bb, hstate], writes=[hl])
                yield
                P.op("dve", lambda h: h.tensor_copy(out=hstate.t[:, t:t + 1], in_=hl.t[:, N - 1:N]), reads=[hl], writes=[hstate])
                if full:
                    P.op("dve", lambda h: h.tensor_tensor(out=zz[t].t[:, 0:N], in0=hl.t[:, 0:N], in1=zz[t].t[:, 0:N], op=ALU.mult), reads=[hl, zz[t]], writes=[zz[t]])
                    yield
                    P.op("pool", lambda h: h.tensor_tensor(out=zsq.t[:, 0:N], in0=zz[t].t[:, 0:N], in1=zz[t].t[:, 0:N], op=ALU.mult), reads=[zz[t]], writes=[zsq])
                    yield
                    P.op("pe", lambda h: h.matmul(pss_bank.t[:, 0:N], lhsT=onesb.t[:], rhs=zsq.t[:, 0:N], start=(t == 0), stop=(t == 3)),
                         reads=[onesb, zsq], writes=[pss_bank])
                yield

            def gelu_batch(uT, N):
                for t in range(4):
                    pg = pbank(True)
                    fm_proj(uT, 512 + t * 128, N, pg)
                    yield
                    P.op("act", lambda h, t=t, pg=pg: h.activation(out=zz[t].t[:, 0:N], in_=pg.t[:, 0:N], func=AF.Gelu), reads=[pg], writes=[zz[t]])
                    prel(pg)

            def lru_final(N, mixT):
                P.op("act", lambda h: h.activation(out=rstd_t.t[:, 0:N], in_=pss_bank.t[:, 0:N], func=AF.Ln, bias=epsc.t[:, 0:1], scale=1.0 / 512),
                     reads=[pss_bank, epsc], writes=[rstd_t])
                P.op("act", lambda h: h.activation(out=rstd_t.t[:, 0:N], in_=rstd_t.t[:, 0:N], func=AF.Exp, scale=-0.5), reads=[rstd_t], writes=[rstd_t])
                for t in range(4):
                    P.op("dve", lambda h, t=t: h.scalar_tensor_tensor(out=mixT.t[:, t, 0:N], in0=zz[t].t[:, 0:N], scalar=cc(C_LNG + t),
                                                                      in1=rstd_t.t[:, 0:N], op0=ALU.mult, op1=ALU.mult),
                         reads=[zz[t], rstd_t, cst], writes=[mixT])

            def rotary(psrc, tb, dst_bf, tA, tB):
                p3 = psrc.t[:].rearrange("p (a b) -> p a b", a=4)
                Cb = tb.t[:, 0:128].unsqueeze(1).to_broadcast([128, 4, 128])
                S0 = tb.t[:, 128:192].unsqueeze(1).to_broadcast([128, 4, 64])
                S1 = tb.t[:, 192:256].unsqueeze(1).to_broadcast([128, 4, 64])
                P.op("dve", lambda h: h.tensor_tensor(out=tA.t[:], in0=p3, in1=Cb, op=ALU.mult), reads=[psrc, tb], writes=[tA])
                P.op("dve", lambda h: h.tensor_tensor(out=tB.t[:, :, 0:64], in0=p3[:, :, 64:128], in1=S0, op=ALU.mult), reads=[psrc, tb], writes=[tB])
                P.op("dve", lambda h: h.tensor_tensor(out=tB.t[:, :, 64:128], in0=p3[:, :, 0:64], in1=S1, op=ALU.mult), reads=[psrc, tb], writes=[tB])
                P.op("pool", lambda h: h.tensor_tensor(out=dst_bf.t[:], in0=tA.t[:], in1=tB.t[:], op=ALU.add), reads=[tA, tB], writes=[dst_bf])

            def tm_proj(uT, c, col, pb):
                def f(h):
                    ins = None
                    for kc in range(8):
                        ins = h.matmul(pb.t[:], lhsT=uT.t[:, kc, c * 128:(c + 1) * 128], rhs=w_in_sb.t[:, kc, col:col + 512],
                                       start=(kc == 0), stop=(kc == 7))
                    return ins
                P.op("pe", f, reads=[wib[col // 512], uT], writes=[pb])

            def heads4(pb, fn_h, reads):
                def f(h):
                    ins = None
                    for hh in range(4):
                        ins = fn_h(h, hh)
                    return ins
                P.op("pe", f, reads=reads, writes=[pb])

            gci = [0]

            def ret_chunk(uT, c, tb, full, mixT, B):
                g = gci[0]
                gci[0] += 1
                Sb_old = S_bfs[g % 4]
                Sb_new = S_bfs[(g + 1) % 4]
                tA, tB, osb, yn, sg, vbf, vz, qbf, kbf, qT, kT, sc, ybf, bst = (B[n] for n in
                    ("tA", "tB", "osb", "yn", "sg", "vbf", "vz", "qbf", "kbf", "qT", "kT", "sc", "ybf", "bst"))
                pk = pbank(True)
                tm_proj(uT, c, 1536, pk)
                yield
                rotary(pk, tb, kbf, tA, tB)
                prel(pk)
                pv = pbank(True)
                tm_proj(uT, c, 2048, pv)
                yield
                pv3 = pv.t[:].rearrange("p (a b) -> p a b", a=4)
                P.op("dve", lambda h: h.tensor_tensor(out=vz.t[:], in0=pv3, in1=cst.t[:, C_ZETA:C_ZETA + 4].unsqueeze(2).to_broadcast([128, 4, 128]), op=ALU.mult),
                     reads=[pv, cst], writes=[vz])
                if full:
                    P.op("act", lambda h: h.activation(out=vbf.t[:].rearrange("p a b -> p (a b)"), in_=pv.t[:], func=AF.Copy), reads=[pv], writes=[vbf])
                prel(pv)
                yield
                pkv = pbank(True)
                heads4(pkv, lambda h, hh: h.matmul(pkv.t[:, hh * 128:(hh + 1) * 128], lhsT=kbf.t[:, hh, :], rhs=vz.t[:, hh, :], start=True, stop=True), [kbf, vz])
                yield
                for hh in range(4):
                    P.op("dve", lambda h, hh=hh: h.scalar_tensor_tensor(out=Sst.t[:, hh, :], in0=Sst.t[:, hh, :], scalar=gC[hh],
                                                                        in1=pkv.t[:, hh * 128:(hh + 1) * 128], op0=ALU.mult, op1=ALU.add),
                         reads=[pkv, Sst], writes=[Sst])
                prel(pkv)
                if full:
                    pq = pbank(True)
                    tm_proj(uT, c, 1024, pq)
                P.op("pool", lambda h: h.tensor_copy(out=Sb_new.t[:].rearrange("p a b -> p (a b)"), in_=Sst.t[:].rearrange("p a b -> p (a b)")),
                     reads=[Sst], writes=[Sb_new])
                yield
                if not full:
                    return
                rotary(pq, tb, qbf, tA, tB)
                prel(pq)
                pg = pbank(True)
                tm_proj(uT, c, 2560, pg)
                yield
                P.op("act", lambda h: h.activation(out=sg.t[:], in_=pg.t[:], func=AF.Exp, scale=-1.0), reads=[pg], writes=[sg])
                P.op("act", lambda h: h.activation(out=sg.t[:], in_=sg.t[:], func=AF.Ln, bias=1.0), reads=[sg], writes=[sg])
                P.op("act", lambda h: h.activation(out=sg.t[:], in_=sg.t[:], func=AF.Exp, scale=-1.0), reads=[sg], writes=[sg])
                P.op("dve", lambda h: h.tensor_tensor(out=sg.t[:], in0=pg.t[:], in1=sg.t[:], op=ALU.mult), reads=[pg, sg], writes=[sg])
                prel(pg)
                pT2 = pbank(True)
                pT2v = pT2.t.bitcast(BF16)
                heads4(pT2, lambda h, hh: h.transpose(pT2v[:, hh * 128:(hh + 1) * 128], kbf.t[:, hh, :], identb.t[:]), [kbf, identb])
                yield
                P.op("act", lambda h: h.activation(out=kT.t[:].rearrange("p a b -> p (a b)"), in_=pT2v[:, 0:512], func=AF.Copy), reads=[pT2], writes=[kT])
                prel(pT2)
                pT = pbank(True)
                pTv = pT.t.bitcast(BF16)
                heads4(pT, lambda h, hh: h.transpose(pTv[:, hh * 128:(hh + 1) * 128], qbf.t[:, hh, :], identb.t[:]), [qbf, identb])
                yield
                P.op("act", lambda h: h.activation(out=qT.t[:].rearrange("p a b -> p (a b)"), in_=pTv[:, 0:512], func=AF.Copy), reads=[pT], writes=[qT])
                prel(pT)
                yield
                psc = pbank(True)
                heads4(psc, lambda h, hh: h.matmul(psc.t[:, hh * 128:(hh + 1) * 128], lhsT=kT.t[:, hh, :], rhs=qT.t[:, hh, :], start=True, stop=True), [kT, qT])
                yield
                P.op("dve", lambda h: h.tensor_tensor(out=sc.t[:].rearrange("p a b -> p (a b)"), in0=psc.t[:], in1=dmask.t[:].rearrange("p a b -> p (a b)"), op=ALU.mult),
                     reads=[psc, dmask], writes=[sc])
                prel(psc)
                pcx = pbank(True)
                heads4(pcx, lambda h, hh: h.matmul(pcx.t[:, hh * 128:(hh + 1) * 128], lhsT=qT.t[:, hh, :], rhs=Sb_old.t[:, hh, :], start=True, stop=True), [qT, Sb_old])
                yield
                for hh in range(4):
                    P.op("act", lambda h, hh=hh: h.activation(out=tA.t[:, hh, :], in_=pcx.t[:, hh * 128:(hh + 1) * 128], func=AF.Copy, scale=cc(C_XI + hh)),
                         reads=[pcx, cst], writes=[tA])
                prel(pcx)
                po = pbank(True)
                heads4(po, lambda h, hh: h.matmul(po.t[:, hh * 128:(hh + 1) * 128], lhsT=sc.t[:, hh, :], rhs=vbf.t[:, hh, :], start=True, stop=True), [sc, vbf])
                yield
                P.op("dve", lambda h: h.tensor_tensor(out=osb.t[:].rearrange("p a b -> p (a b)"), in0=po.t[:], in1=tA.t[:].rearrange("p a b -> p (a b)"), op=ALU.add),
                     reads=[po, tA], writes=[osb])
                prel(po)
                yield
                for hh in range(4):
                    P.op("dve", lambda h, hh=hh: h.bn_stats(out=bst.t[:, hh, 0:6], in_=osb.t[:, hh, :]), reads=[osb], writes=[bst])
                for hh in range(4):
                    P.op("dve", lambda h, hh=hh: h.bn_aggr(out=bst.t[:, hh, 6:8], in_=bst.t[:, hh, 0:6]), reads=[bst], writes=[bst])
                yield
                s = stat.next()
                P.op("act", lambda h: h.activation(out=s.t[:, 0:4], in_=bst.t[:, :, 7], func=AF.Ln, bias=epsc.t[:, 0:1]), reads=[bst, epsc], writes=[s])
                yield
                P.op("act", lambda h: h.activation(out=s.t[:, 0:4], in_=s.t[:, 0:4], func=AF.Exp, scale=-0.5), reads=[s], writes=[s])
                yield
                for hh in range(4):
                    P.op("dve", lambda h, hh=hh: h.tensor_scalar(out=yn.t[:, hh, :], in0=osb.t[:, hh, :], scalar1=bst.t[:, hh, 6:7], scalar2=s.t[:, hh:hh + 1],
                                                                 op0=ALU.subtract, op1=ALU.mult),
                         reads=[osb, bst, s], writes=[yn])
                yield
                P.op("pool", lambda h: h.tensor_tensor(out=ybf.t[:].rearrange("p a b -> p (a b)"), in0=yn.t[:].rearrange("p a b -> p (a b)"), in1=sg.t[:], op=ALU.mult),
                     reads=[yn, sg], writes=[ybf])
                yield
                pY = pbank(True)
                pYv = pY.t.bitcast(BF16)
                heads4(pY, lambda h, hh: h.transpose(pYv[:, hh * 128:(hh + 1) * 128], ybf.t[:, hh, :], identb.t[:]), [ybf, identb])
                yield
                P.op("dve", lambda h: h.tensor_tensor(out=mixT.t[:, 4:8, c * 128:(c + 1) * 128], in0=pYv[:, 0:512].rearrange("p (a b) -> p a b", a=4),
                                                      in1=cst.t[:, C_RNG:C_RNG + 4].unsqueeze(2).to_broadcast([128, 4, 128]), op=ALU.mult),
                     reads=[pY, cst], writes=[mixT])
                prel(pY)
                yield

            def outproj_chunk(mixT, c, gch):
                r0 = gch * 128
                hm = hm_r.next()
                ld("sp", hm, hm.t[:], xm[r0:r0 + 128, :])
                for half in range(2):
                    pb = pbank()

                    def f(h, half=half, pb=pb):
                        ins = None
                        for kc in range(8):
                            ins = h.matmul(pb.t[:], lhsT=mixT.t[:, kc, c * 128:(c + 1) * 128], rhs=w_out_sb.t[:, kc, half * 512:(half + 1) * 512],
                                           start=(kc == 0), stop=(kc == 7))
                        return ins
                    P.op("pe", f, reads=[mixT, w_out_sb], writes=[pb])
                    P.op("dve", lambda h, half=half, pb=pb: h.tensor_tensor(out=hm.t[:, half * 512:(half + 1) * 512], in0=pb.t[:], in1=hm.t[:, half * 512:(half + 1) * 512], op=ALU.add),
                         reads=[pb, hm], writes=[hm])
                hb = hbufs[gch]
                P.dma("sp", key_of(hm), lambda h: h.dma_start(out=hmid[r0:r0 + 128, :], in_=hm.t[:]), reads=[hm], writes=[hb])

            def run_lanes(lanes, delays=None):
                cur = [None] * len(lanes)
                idx = [0] * len(lanes)
                dl = list(delays) if delays else [0] * len(lanes)
                dl = dl + [0] * (len(lanes) - len(dl))
                active = True
                while active:
                    active = False
                    for li, lane in enumerate(lanes):
                        if dl[li] > 0:
                            dl[li] -= 1
                            active = True
                            continue
                        if cur[li] is None:
                            if idx[li] < len(lane):
                                cur[li] = lane[idx[li]]
                                idx[li] += 1
                            else:
                                continue
                        active = True
                        try:
                            next(cur[li])
                        except StopIteration:
                            cur[li] = None

            hbufs = [Buf("hmid%d" % i) for i in range(NM)]

            def tail_gen(prev, uT, N):
                if prev is not None:
                    lru_final(prev[0], prev[1])
                yield
                if uT is not None:
                    for _ in gelu_batch(uT, N):
                        yield
                    yield
                if prev is not None:
                    for c in range(prev[3]):
                        outproj_chunk(prev[1], c, prev[2] + c)
                        yield
                        yield

            def mixer(blocks, xd, tabd, full, flag_after_first):
                if not blocks:
                    return
                nxt = stageA(xd, tabd, *blocks[0])
                tail = None
                for bi, (c0, nch) in enumerate(blocks):
                    N = nch * 128
                    uT, tabs = nxt
                    if bi + 1 < len(blocks):
                        nxt = stageA(xd, tabd, *blocks[bi + 1])
                    mixT = mixT_r.next() if full else None
                    xl = lru_prelude(N)
                    if full:
                        lanes = [
                            [lru_tile(uT, N, full, 0, xl, LS[0]), lru_tile(uT, N, full, 2, xl, LS[0])],
                            [lru_tile(uT, N, full, 1, xl, LS[1]), lru_tile(uT, N, full, 3, xl, LS[1])],
                        ]
                        dls = [0, LDL]
                    else:
                        lanes = [[lru_tile(uT, N, full, 0, xl, LS[0])], [lru_tile(uT, N, full, 1, xl, LS[1])],
                                 [lru_tile(uT, N, full, 2, xl, LSP[0])], [lru_tile(uT, N, full, 3, xl, LSP[1])]]
                        dls = [0, 1, 2, 3]
                    lanes = lanes + [[ret_chunk(uT, c, tabs[c], full, mixT, RS[c])] for c in range(nch)]
                    dls = dls + [(RDL if full else 2) * c for c in range(nch)]
                    if full:
                        lanes.append([tail_gen(tail, uT, N)])
                    run_lanes(lanes, dls)
                    if bgq and bi >= 1:
                        bgq.pop(0)()
                    tail = (N, mixT, c0, nch) if full else None
                    if flag_after_first and bi == 0:
                        P.op("dve", lambda h: h.tensor_scalar(out=hstate.t[:], in0=hstate.t[:], scalar1=cc(C_FLAG), scalar2=None, op0=ALU.mult),
                             reads=[hstate, cst], writes=[hstate])
                if tail is not None:
                    run_lanes([[tail_gen(tail, None, 0)]])

            mixer(split_blocks(NP, BC) if STG >= 1 else [], xp, tabp, False, False)
            mixer(split_blocks(NM, BC, first=1) if STG >= 2 else [], xm, tabm, True, True)
            while bgq:
                bgq.pop(0)()
            P.barrier()
            P.emit_block()

        with ExitStack() as st2:
            NF = FBC * 128
            w_up_all = mk(st2, "w_up_all", [128, 8, 8, 768], BF16)[0]
            w_dn_all = mk(st2, "w_dn_all", [128, 24, D], BF16)[0]
            wup_b = [Buf("wupg%d" % g) for g in range(8)]
            wdn_b = [Buf("wdng%d" % g) for g in range(4)]
            gfin = mk(st2, "gfin", [128, D], F32)[0]
            tails = mk(st2, "tails", [128, 48, 2], F32)[0]
            tails_b = [Buf("tails%d" % j) for j in range(48)]
            hs_r = Rot(mk(st2, "hs", [128, D], F32, FBC))
            ha_r = Rot(mk(st2, "ha", [128, D], F32, 2))
            xn2_r = Rot(mk(st2, "xn2", [128, D], BF16, 1))
            u2T_r = Rot(mk(st2, "u2T", [128, 8, NF], BF16, 1))
            gT = [Rot(mk(st2, "gT%d" % m, [128, NF], BF16, 1)) for m in range(24)]
            ya_r = Rot(mk(st2, "ya", [128, NF], F32, 3))
            yv_r = Rot(mk(st2, "yv", [128, NF], F32, 2))
            U_r = Rot(mk(st2, "U", [128, NF + 2], F32, 2))

            for g in (0, 4, 1, 5, 2, 6, 3, 7):
                P.dma("sp", key_of(wup_b[g]), lambda h, g=g: h.dma_start(out=w_up_all.t[:, g, :, :].rearrange("p a b -> p (a b)"), in_=wup_bf[g].rearrange("p a b -> p (a b)")),
                      reads=[wupd_b], writes=[wup_b[g]])
            for i in range(4):
                P.dma("sp", key_of(wdn_b[i]), lambda h, i=i: h.dma_start(out=w_dn_all.t[:, i * 6:(i + 1) * 6, :].rearrange("p a b -> p (a b)"), in_=wdn_bf[:, i * 6:(i + 1) * 6, :].rearrange("p a b -> p (a b)")),
                      reads=[wdnd_b], writes=[wdn_b[i]])
            ld("sp", gfin, gfin.t[:], gfin_d)
            for j in range(48):
                P.op("pool", lambda h, j=j: h.memset(tails.t[:, j, :], 0.0), writes=[tails_b[j]])

            Uh_b = {}

            def conv_tile(pu, j, N, y, init_on_act):
                w0, w1, w2 = cc(C_FCW + j), cc(C_FCW + 48 + j), cc(C_FCW + 96 + j)
                U = U_r.next()
                Uh = Uh_b.setdefault(U.b.name, Buf(U.b.name + "_h"))
                P.op("pool", lambda h: h.tensor_copy(out=U.t[:, 0:2], in_=tails.t[:, j, :]), reads=[tails_b[j]], writes=[Uh])
                P.op("act", lambda h: h.activation(out=U.t[:, 2:2 + N], in_=pu.t[:, 0:N], func=AF.Copy), reads=[pu], writes=[U])
                if init_on_act:
                    P.op("act", lambda h: h.activation(out=y.t[:, 0:N], in_=pu.t[:, 0:N], func=AF.Identity, bias=cc(C_FCB + j), scale=w2),
                         reads=[pu, cst], writes=[y])
                else:
                    P.op("dve", lambda h: h.tensor_scalar(out=y.t[:, 0:N], in0=U.t[:, 2:2 + N], scalar1=w2, scalar2=cc(C_FCB + j), op0=ALU.mult, op1=ALU.add),
                         reads=[U, cst], writes=[y])
                P.op("dve", lambda h: h.scalar_tensor_tensor(out=y.t[:, 0:N], in0=U.t[:, 1:1 + N], scalar=w1, in1=y.t[:, 0:N], op0=ALU.mult, op1=ALU.add),
                     reads=[U, Uh, y, cst], writes=[y])
                P.op("dve", lambda h: h.scalar_tensor_tensor(out=y.t[:, 0:N], in0=U.t[:, 0:N], scalar=w0, in1=y.t[:, 0:N], op0=ALU.mult, op1=ALU.add),
                     reads=[U, Uh, y, cst], writes=[y])
                P.op("pool", lambda h: h.tensor_copy(out=tails.t[:, j, :], in_=U.t[:, N:N + 2]), reads=[U], writes=[tails_b[j]])

            last_out = None
            fblocks = split_blocks(NM, FBC) if STG >= 3 else []

            def stageA_F(c0, nch):
                u2T = u2T_r.next()
                for c in range(nch):
                    ha = ha_r.next()
                    r0 = (c0 + c) * 128
                    P.dma("sp", key_of(ha), lambda h, ha=ha, r0=r0: h.dma_start(out=ha.t[:], in_=hmid[r0:r0 + 128, :]), reads=[hbufs[c0 + c]], writes=[ha])
                    xn2 = xn2_r.next()
                    norm_transpose(ha, xn2, u2T, c * 128, C_G2, None)
                return u2T

            u2T_next = stageA_F(*fblocks[0]) if fblocks else None
            for bi, (c0, nch) in enumerate(fblocks):
                N = nch * 128
                u2T = u2T_next
                pend = None
                for m in range(24):
                    pa = pbank()
                    pv = pbank()
                    for (pb, j) in ((pa, m), (pv, 24 + m)):
                        def f(h, pb=pb, j=j, N=N, u2T=u2T):
                            ins = None
                            for kc in range(8):
                                ins = h.matmul(pb.t[:, 0:N], lhsT=w_up_all.t[:, j // 6, kc, (j % 6) * 128:(j % 6 + 1) * 128], rhs=u2T.t[:, kc, 0:N], start=(kc == 0), stop=(kc == 7))
                            return ins
                        P.op("pe", f, reads=[wup_b[j // 6], u2T], writes=[pb])
                    ya = ya_r.next()
                    yv = yv_r.next()
                    conv_tile(pa, m, N, ya, True)
                    if pend is not None:
                        pend()
                    conv_tile(pv, 24 + m, N, yv, True)

                    def pend(ya=ya, yv=yv, m=m, N=N):
                        P.op("act", lambda h: h.activation(out=ya.t[:, 0:N], in_=ya.t[:, 0:N], func=AF.Gelu), reads=[ya], writes=[ya])
                        g = gT[m].next()
                        P.op("pool", lambda h: h.tensor_tensor(out=g.t[:, 0:N], in0=ya.t[:, 0:N], in1=yv.t[:, 0:N], op=ALU.mult),
                             reads=[ya, yv], writes=[g])
                pend()
                hss = []
                for c in range(nch):
                    hs = hs_r.next()
                    r0 = (c0 + c) * 128
                    P.dma("sp", key_of(hs), lambda h, hs=hs, r0=r0: h.dma_start(out=hs.t[:], in_=hmid[r0:r0 + 128, :]), reads=[hbufs[c0 + c]], writes=[hs])
                    hss.append(hs)
                if bi + 1 < len(fblocks):
                    u2T_next = stageA_F(*fblocks[bi + 1])
                for c in range(nch):
                    hs = hss[c]
                    for half in range(2):
                        pb = pbank()

                        def f(h, half=half, pb=pb, c=c):
                            ins = None
                            for m in range(24):
                                ins = h.matmul(pb.t[:], lhsT=gT[m].items[0].t[:, c * 128:(c + 1) * 128], rhs=w_dn_all.t[:, m, half * 512:(half + 1) * 512],
                                               start=(m == 0), stop=(m == 23))
                            return ins
                        P.op("pe", f, reads=[gT[m].items[0] for m in range(24)] + wdn_b, writes=[pb])
                        P.op("dve", lambda h, half=half, pb=pb, hs=hs: h.tensor_tensor(out=hs.t[:, half * 512:(half + 1) * 512], in0=pb.t[:],
                                                                                       in1=hs.t[:, half * 512:(half + 1) * 512], op=ALU.add),
                             reads=[pb, hs], writes=[hs])
                    s = stat.next()
                    junk = xn2_r.next()
                    P.op("pool", lambda h, s=s: h.memset(s.t[:], 0.0), writes=[s])
                    P.op("act", lambda h, s=s, hs=hs, junk=junk: h.activation(out=junk.t[:], in_=hs.t[:], func=AF.Square, accum_out=s.t[:, 0:1]),
                         reads=[hs, s], writes=[junk, s])
                    P.op("act", lambda h, s=s: h.activation(out=s.t[:, 1:2], in_=s.t[:, 0:1], func=AF.Sqrt, bias=epsc.t[:, 0:1], scale=1.0 / D),
                         reads=[s, epsc], writes=[s])
                    P.op("dve", lambda h, s=s: h.reciprocal(out=s.t[:, 2:3], in_=s.t[:, 1:2]), reads=[s], writes=[s])
                    P.op("act", lambda h, s=s, hs=hs: h.activation(out=hs.t[:], in_=hs.t[:], func=AF.Copy, scale=s.t[:, 2:3]),
                         reads=[hs, s], writes=[hs])
                    P.op("pool", lambda h, hs=hs: h.tensor_tensor(out=hs.t[:], in0=hs.t[:], in1=gfin.t[:], op=ALU.mult),
                         reads=[hs, gfin], writes=[hs])
                    r0 = (c0 + c) * 128
                    last_out = P.dma("sp", key_of(hs), lambda h, hs=hs, r0=r0: h.dma_start(out=out[r0:r0 + 128, :], in_=hs.t[:]), reads=[hs])
            P.barrier()
            P.emit_block()
    return nc


_CACHE = {}


def _tables(pos):
    inv_freq = (np.float32(10000.0) ** (-np.arange(0, 128, 2, dtype=np.float32) / np.float32(128))).astype(np.float32)
    ang = (pos.astype(np.float32)[:, None] * inv_freq[None, :]).astype(np.float32)
    c = np.cos(ang).astype(np.float32)
    s = np.sin(ang).astype(np.float32)
    return np.ascontiguousarray(np.concatenate([c, c, -s, s], axis=1).astype(np.float32))


def _consts():
    log_g = np.log1p(-np.exp2(-5.0 - np.arange(4, dtype=np.float64)))
    idx = np.arange(128, dtype=np.float64)
    sc = 128.0 ** -0.5
    diff = idx[None, :] - idx[:, None]
    dm = np.where(diff[:, None, :] >= 0, np.exp(np.maximum(diff, 0.0)[:, None, :] * log_g[None, :, None]), 0.0) * sc
    xi = np.exp((idx + 1.0)[:, None] * log_g[None, :])
    zeta = np.exp((127.0 - idx)[:, None] * log_g[None, :]) * sc
    return dm.reshape(128, 512).astype(np.float32), xi.astype(np.float32), zeta.astype(np.float32)


def kernel(x, norm1_gain, w_in, lru_conv_w, lru_conv_b, lru_gate_a_w, lru_gate_a_b,
           lru_gate_x_w, lru_gate_x_b, lru_lambda, lru_norm_gain, ret_norm_gain, w_out,
           norm2_gain, ffn_up_w, ffn_conv_w, ffn_conv_b, ffn_down_w, final_norm_gain, _dbg=None):
    f = lambda a: np.ascontiguousarray(np.asarray(a, dtype=np.float32))
    x = f(x)
    B, S, _ = x.shape
    half = S // 2
    NM = half // 128 + 1
    NP = half // 128 - 1
    key = (NM, NP)
    if key not in _CACHE:
        _CACHE[key] = build(NM, NP)
    nc = _CACHE[key]

    dmask, xi, zeta = _consts()

    def pp(v, n):
        return f(v).reshape(n, 128).T
    cst = np.zeros((128, NCST), np.float32)
    cst[:, C_G1:C_G1 + 8] = pp(norm1_gain[0], 8)
    cst[:, C_G2:C_G2 + 8] = pp(norm2_gain[0], 8)
    for k in range(4):
        cst[:, C_LCW + k * 4:C_LCW + k * 4 + 4] = pp(lru_conv_w[0, k], 4)
    cst[:, C_LCB:C_LCB + 4] = pp(lru_conv_b[0], 4)
    cst[:, C_LAB:C_LAB + 4] = pp(lru_gate_a_b[0], 4)
    cst[:, C_LXB:C_LXB + 4] = pp(lru_gate_x_b[0], 4)
    cst[:, C_LAM:C_LAM + 4] = pp(lru_lambda[0], 4)
    cst[:, C_LNG:C_LNG + 4] = pp(lru_norm_gain[0], 4)
    cst[:, C_RNG:C_RNG + 4] = pp(ret_norm_gain[0], 4)
    for k in range(3):
        cst[:, C_FCW + k * 48:C_FCW + k * 48 + 48] = pp(ffn_conv_w[0, k], 48)
    cst[:, C_FCB:C_FCB + 48] = pp(ffn_conv_b[0], 48)
    cst[:, C_XI:C_XI + 4] = xi
    cst[:, C_ZETA:C_ZETA + 4] = zeta
    gw = np.zeros((128, 8, 128), np.float32)
    for j, wsrc in enumerate((f(lru_gate_a_w[0]), f(lru_gate_x_w[0]))):
        for t in range(4):
            gw[0:64, j * 4 + t, 0:64] = wsrc[2 * t]
            gw[64:128, j * 4 + t, 64:128] = wsrc[2 * t + 1]
    gfin = np.ascontiguousarray(np.broadcast_to(f(final_norm_gain)[None, :], (128, D)))
    ident = np.eye(128, dtype=np.float32)
    shared = {
        "gfin": gfin, "dmask": dmask, "gw": gw.reshape(128, 1024), "ident": ident,
        "w_in": f(w_in[0]), "w_out": f(w_out[0]), "w_up": f(ffn_up_w[0]), "w_dn": f(ffn_down_w[0]),
    }
    in_maps = []
    for b in range(B):
        for hh in range(2):
            c2 = cst.copy()
            c2[:, C_FLAG] = float(hh)
            if hh == 0:
                xmc = np.concatenate([np.zeros((128, D), np.float32), x[b, :half]], axis=0)
                xpc = np.zeros((max(NP, 1) * 128, D), np.float32)
                posm = np.concatenate([np.arange(128), np.arange(half)])
                posp = np.zeros(max(NP, 1) * 128)
            else:
                xmc = x[b, half - 128:]
                xpc = x[b, :half - 128] if NP > 0 else np.zeros((128, D), np.float32)
                posm = np.arange(half - 128, S)
                posp = np.arange(half - 128) if NP > 0 else np.zeros(128)
            m = dict(shared)
            m.update({"xm": np.ascontiguousarray(xmc), "xp": np.ascontiguousarray(xpc), "tabm": _tables(posm), "tabp": _tables(posp), "cst": c2})
            in_maps.append(m)
    res = run_bass_kernel_spmd(nc, in_maps, core_ids=list(range(B * 2)))
    if os.environ.get("KHM"):
        global _HM
        _HM = np.empty((B, S, D), np.float32)
        for b in range(B):
            for hh in range(2):
                _HM[b, hh * half:(hh + 1) * half] = res.results[b * 2 + hh]["hmid"][128:]
    outp = np.empty((B, S, D), np.float32)
    for b in range(B):
        for hh in range(2):
            o = res.results[b * 2 + hh]["out"]
            outp[b, hh * half:(hh + 1) * half] = o[128:]
    return outp
```
